# Optimizing a Trainium2 kernel written in Bass

```python
import math
import jax, jax.numpy as jnp
from jax import lax
import numpy as np

D_MODEL = 1024
BATCH = 8
SEQ = 2048
DEPTH = 4
DEC_BATCH = 128
DEC_SEQ = 1
PAST_LEN = 16384
PAGE_SIZE = 128

CONF_W = D_MODEL
CONF_KERNEL = 31
SSD_HEAD_DIM = 64
SSD_HEADS = D_MODEL // SSD_HEAD_DIM
SSD_W = SSD_HEADS * SSD_HEAD_DIM
SSD_GROUPS = 2
SSD_STATE = 128
SSD_CONV = 4
SSD_CHUNK = 128
XBC_W = SSD_W + 2 * SSD_GROUPS * SSD_STATE
D_MIX = CONF_W + SSD_W
D_IN = 2 * CONF_W + SSD_W + XBC_W + SSD_HEADS
SPLIT_IN = (CONF_W, 2 * CONF_W, 2 * CONF_W + SSD_W, 2 * CONF_W + SSD_W + XBC_W)
SPLIT_XBC = (SSD_W, SSD_W + SSD_GROUPS * SSD_STATE)
MEM_LEN = 256
XA_HEADS = 4
XA_HEAD_DIM = D_MODEL // XA_HEADS
XA_W = XA_HEADS * XA_HEAD_DIM
D_FF = 11 * D_MODEL // 4
FFN_CONV = 3
ALPHA = (2.0 * DEPTH) ** 0.25
BETA = (8.0 * DEPTH) ** -0.25
LN_EPS = 1e-5

kernel_name = "hymba_conformer_ssd_memxattn_convffn_step"


def layer_norm(x, g, b):
    xf = x.astype(jnp.float32)
    mu = jnp.mean(xf, -1, keepdims=True)
    var = jnp.mean(jnp.square(xf - mu), -1, keepdims=True)
    y = (xf - mu) * lax.rsqrt(var + LN_EPS) * g.astype(jnp.float32) + b.astype(jnp.float32)
    return y.astype(x.dtype)


def rms_norm_f32(xf, g):
    ms = jnp.mean(jnp.square(xf), -1, keepdims=True)
    return xf * lax.rsqrt(ms + LN_EPS) * g.astype(jnp.float32)


def causal_dwconv(x, prev, w, b):
    k = w.shape[0]
    xp = jnp.concatenate([prev.astype(x.dtype), x], axis=1)
    y = lax.conv_general_dilated(xp, w[:, None, :].astype(x.dtype), window_strides=(1,),
                                 padding='VALID', dimension_numbers=('NWC', 'WIO', 'NWC'),
                                 feature_group_count=x.shape[-1])
    return y + b.astype(x.dtype), xp[:, xp.shape[1] - (k - 1):]


def ssd_chunked(x, dt, a, bm, cm, h0):
    bsz, t, nh, p = x.shape
    g = bm.shape[2]
    f32 = jnp.float32
    x = x.astype(f32)
    bh = jnp.repeat(bm.astype(f32), nh // g, axis=2)
    ch = jnp.repeat(cm.astype(f32), nh // g, axis=2)
    q = min(SSD_CHUNK, t)
    nc = -(-t // q)
    pad = nc * q - t

    def chunk(z):
        z = jnp.pad(z, [(0, 0), (0, pad)] + [(0, 0)] * (z.ndim - 2))
        return z.reshape((bsz, nc, q) + z.shape[2:])

    dtc = chunk(dt)
    xc, bc, cc = chunk(x), chunk(bh), chunk(ch)
    a_cum = jnp.cumsum(dtc * a, axis=2)
    dtx = xc * dtc[..., None]
    seg = a_cum[:, :, :, None, :] - a_cum[:, :, None, :, :]
    causal = jnp.tril(jnp.ones((q, q), bool))[None, None, :, :, None]
    decay = jnp.exp(jnp.where(causal, seg, -jnp.inf))
    scores = jnp.einsum('bclhn,bcshn->bclsh', cc, bc) * decay
    y_diag = jnp.einsum('bclsh,bcshp->bclhp', scores, dtx)
    decay_to_end = jnp.exp(a_cum[:, :, -1:, :] - a_cum)
    chunk_states = jnp.einsum('bclhn,bclh,bclhp->bchpn', bc, decay_to_end, dtx)
    chunk_decay = jnp.exp(a_cum[:, :, -1, :])

    def step(h, inp):
        s, d = inp
        return h * d[:, :, None, None] + s, h

    h_last, h_prev = lax.scan(step, h0.astype(f32),
                              (jnp.moveaxis(chunk_states, 1, 0), jnp.moveaxis(chunk_decay, 1, 0)))
    h_prev = jnp.moveaxis(h_prev, 0, 1)
    y_off = jnp.einsum('bclhn,bchpn,bclh->bclhp', cc, h_prev, jnp.exp(a_cum))
    y = (y_diag + y_off).reshape(bsz, nc * q, nh, p)[:, :t]
    return y, h_last


def decoder_layer(x, mk, mv, conf_st, ssdc_st, ssd_st, ffn_st, p):
    bsz, t, _ = x.shape
    h = x @ p["w_in"]
    glu_v, glu_g, z, xbc, dt_raw = jnp.split(h, SPLIT_IN, axis=-1)
    a = glu_v * jax.nn.sigmoid(glu_g)
    a, new_conf = causal_dwconv(a, conf_st, p["conf_conv_w"], p["conf_conv_b"])
    a = jax.nn.silu(layer_norm(a, p["conf_ln_g"], p["conf_ln_b"]))
    xbc, new_ssdc = causal_dwconv(xbc, ssdc_st, p["ssd_conv_w"], p["ssd_conv_b"])
    xbc = jax.nn.silu(xbc)
    xs, bm, cm = jnp.split(xbc, SPLIT_XBC, axis=-1)
    xs = xs.reshape(bsz, t, SSD_HEADS, SSD_HEAD_DIM)
    bm = bm.reshape(bsz, t, SSD_GROUPS, SSD_STATE)
    cm = cm.reshape(bsz, t, SSD_GROUPS, SSD_STATE)
    dt = jax.nn.softplus(dt_raw.astype(jnp.float32) + p["ssd_dt_bias"].astype(jnp.float32))
    a_neg = -jnp.exp(p["ssd_a_log"].astype(jnp.float32))
    y, new_ssd = ssd_chunked(xs, dt, a_neg, bm, cm, ssd_st)
    y = y + p["ssd_d"].astype(jnp.float32)[:, None] * xs.astype(jnp.float32)
    y = y.reshape(bsz, t, SSD_W) * jax.nn.silu(z.astype(jnp.float32))
    y = rms_norm_f32(y, p["ssd_norm_g"]).astype(x.dtype)
    mix = jnp.concatenate([a, y], axis=-1) @ p["w_out"]
    x = layer_norm(ALPHA * x + mix, p["ln_mix_g"], p["ln_mix_b"])
    q = (x @ p["xa_wq"]).reshape(bsz, t, XA_HEADS, XA_HEAD_DIM)
    s = jnp.einsum('bthd,bmhd->bhtm', q, mk.astype(x.dtype)).astype(jnp.float32) * (XA_HEAD_DIM ** -0.5)
    pr = jax.nn.softmax(s, axis=-1).astype(x.dtype)
    o = jnp.einsum('bhtm,bmhd->bthd', pr, mv.astype(x.dtype)).reshape(bsz, t, XA_W)
    x = layer_norm(ALPHA * x + o @ p["xa_wo"], p["ln_xa_g"], p["ln_xa_b"])
    u = x @ p["ffn_w_up"]
    u, new_ffn = causal_dwconv(u, ffn_st, p["ffn_conv_w"], p["ffn_conv_b"])
    uv, ug = jnp.split(u, 2, axis=-1)
    f = (jax.nn.silu(ug) * uv) @ p["ffn_w_down"]
    x = layer_norm(ALPHA * x + f, p["ln_ffn_g"], p["ln_ffn_b"])
    return x, (new_conf, new_ssdc, new_ssd.astype(ssd_st.dtype), new_ffn)


def run_trunk(x, mem_k, mem_v, conf_st, ssdc_st, ssd_st, ffn_st, prm):
    new = ([], [], [], [])
    for l in range(DEPTH):
        lp = {name: w[l] for name, w in prm.items()}
        x, st = decoder_layer(x, mem_k[l], mem_v[l], conf_st[l], ssdc_st[l], ssd_st[l], ffn_st[l], lp)
        for lst, s in zip(new, st):
            lst.append(s)
    return x, [jnp.stack(s) for s in new]


def setup_inputs(seed: int = 0) -> dict:
    key = jax.random.key(seed)
    ks = iter(jax.random.split(key, 48))
    f32 = jnp.float32

    def nrm(shape, scale):
        return jax.random.normal(next(ks), shape, f32) * scale

    def gain(shape):
        return 1.0 + nrm(shape, 0.02)

    L = DEPTH
    dt0 = jnp.exp(jax.random.uniform(next(ks), (L, SSD_HEADS), f32, math.log(1e-3), math.log(1e-1)))
    a_log = jnp.log(jax.random.uniform(next(ks), (L, SSD_HEADS), f32, 1.0, 16.0))
    return {
        "x_prompt": nrm((BATCH, SEQ, D_MODEL), 1.0),
        "x_sample": nrm((DEC_BATCH, DEC_SEQ, D_MODEL), 1.0),
        "cache_mem_k": nrm((L, DEC_BATCH, MEM_LEN, XA_HEADS, XA_HEAD_DIM), 1.0),
        "cache_mem_v": nrm((L, DEC_BATCH, MEM_LEN, XA_HEADS, XA_HEAD_DIM), BETA),
        "state_conf_conv": nrm((L, DEC_BATCH, CONF_KERNEL - 1, CONF_W), 0.5),
        "state_ssd_conv": nrm((L, DEC_BATCH, SSD_CONV - 1, XBC_W), 1.0),
        "state_ssd": nrm((L, DEC_BATCH, SSD_HEADS, SSD_HEAD_DIM, SSD_STATE), 0.3),
        "state_ffn_conv": nrm((L, DEC_BATCH, FFN_CONV - 1, 2 * D_FF), BETA),
        "mem_prompt": nrm((BATCH, MEM_LEN, D_MODEL), 1.0),
        "w_in": nrm((L, D_MODEL, D_IN), D_MODEL ** -0.5),
        "conf_conv_w": nrm((L, CONF_KERNEL, CONF_W), CONF_KERNEL ** -0.5),
        "conf_conv_b": nrm((L, CONF_W), 0.02),
        "conf_ln_g": gain((L, CONF_W)),
        "conf_ln_b": nrm((L, CONF_W), 0.02),
        "ssd_conv_w": nrm((L, SSD_CONV, XBC_W), SSD_CONV ** -0.5),
        "ssd_conv_b": nrm((L, XBC_W), 0.02),
        "ssd_dt_bias": dt0 + jnp.log(-jnp.expm1(-dt0)),
        "ssd_a_log": a_log,
        "ssd_d": gain((L, SSD_HEADS)),
        "ssd_norm_g": gain((L, SSD_W)),
        "w_out": nrm((L, D_MIX, D_MODEL), BETA * D_MIX ** -0.5),
        "ln_mix_g": gain((L, D_MODEL)),
        "ln_mix_b": nrm((L, D_MODEL), 0.02),
        "xa_wq": nrm((L, D_MODEL, XA_W), D_MODEL ** -0.5),
        "xa_wk": nrm((L, D_MODEL, XA_W), D_MODEL ** -0.5),
        "xa_wv": nrm((L, D_MODEL, XA_W), BETA * D_MODEL ** -0.5),
        "xa_wo": nrm((L, XA_W, D_MODEL), BETA * XA_W ** -0.5),
        "ln_xa_g": gain((L, D_MODEL)),
        "ln_xa_b": nrm((L, D_MODEL), 0.02),
        "ffn_w_up": nrm((L, D_MODEL, 2 * D_FF), BETA * D_MODEL ** -0.5),
        "ffn_conv_w": nrm((L, FFN_CONV, 2 * D_FF), FFN_CONV ** -0.5),
        "ffn_conv_b": nrm((L, 2 * D_FF), 0.02),
        "ffn_w_down": nrm((L, D_FF, D_MODEL), BETA * D_FF ** -0.5),
        "ln_ffn_g": gain((L, D_MODEL)),
        "ln_ffn_b": nrm((L, D_MODEL), 0.02),
    }


def reference(x_prompt, x_sample, cache_mem_k, cache_mem_v, state_conf_conv, state_ssd_conv, state_ssd,
              state_ffn_conv, mem_prompt, w_in, conf_conv_w, conf_conv_b, conf_ln_g, conf_ln_b,
              ssd_conv_w, ssd_conv_b, ssd_dt_bias, ssd_a_log, ssd_d, ssd_norm_g, w_out, ln_mix_g, ln_mix_b,
              xa_wq, xa_wk, xa_wv, xa_wo, ln_xa_g, ln_xa_b, ffn_w_up, ffn_conv_w, ffn_conv_b, ffn_w_down,
              ln_ffn_g, ln_ffn_b):
    prm = dict(w_in=w_in, conf_conv_w=conf_conv_w, conf_conv_b=conf_conv_b, conf_ln_g=conf_ln_g,
               conf_ln_b=conf_ln_b, ssd_conv_w=ssd_conv_w, ssd_conv_b=ssd_conv_b, ssd_dt_bias=ssd_dt_bias,
               ssd_a_log=ssd_a_log, ssd_d=ssd_d, ssd_norm_g=ssd_norm_g, w_out=w_out, ln_mix_g=ln_mix_g,
               ln_mix_b=ln_mix_b, xa_wq=xa_wq, xa_wo=xa_wo, ln_xa_g=ln_xa_g, ln_xa_b=ln_xa_b,
               ffn_w_up=ffn_w_up, ffn_conv_w=ffn_conv_w, ffn_conv_b=ffn_conv_b, ffn_w_down=ffn_w_down,
               ln_ffn_g=ln_ffn_g, ln_ffn_b=ln_ffn_b)
    bp = x_prompt.shape[0]
    dtp = x_prompt.dtype
    mk_p = jnp.einsum('bmd,lde->lbme', mem_prompt, xa_wk).reshape(DEPTH, bp, MEM_LEN, XA_HEADS, XA_HEAD_DIM)
    mv_p = jnp.einsum('bmd,lde->lbme', mem_prompt, xa_wv).reshape(DEPTH, bp, MEM_LEN, XA_HEADS, XA_HEAD_DIM)
    y_prompt, st_p = run_trunk(
        x_prompt, mk_p, mv_p,
        jnp.zeros((DEPTH, bp, CONF_KERNEL - 1, CONF_W), dtp),
        jnp.zeros((DEPTH, bp, SSD_CONV - 1, XBC_W), dtp),
        jnp.zeros((DEPTH, bp, SSD_HEADS, SSD_HEAD_DIM, SSD_STATE), dtp),
        jnp.zeros((DEPTH, bp, FFN_CONV - 1, 2 * D_FF), dtp),
        prm)
    y_sample, st_s = run_trunk(x_sample, cache_mem_k, cache_mem_v, state_conf_conv, state_ssd_conv,
                               state_ssd, state_ffn_conv, prm)
    return (y_prompt, y_sample, st_p[0], st_p[1], st_p[2], st_p[3], mk_p, mv_p,
            st_s[0], st_s[1], st_s[2], st_s[3])
```

```python
from contextlib import ExitStack
import numpy as np
import concourse.bass as bass
import concourse.mybir as mybir
from concourse.bass_utils import run_bass_kernel_spmd

F32 = mybir.dt.float32
BF16 = mybir.dt.bfloat16
AF = mybir.ActivationFunctionType
ALU = mybir.AluOpType
AX = mybir.AxisListType

NCORES = 8
L = 4
T = 2048
NS = 16
ALPHA = (2.0 * L) ** 0.25
EPS = 1e-5
D_IN = 4624
NFF = 5632

O_CONFW, O_CONFB, O_CLNG, O_CLNB = 0, 248, 256, 264
O_SSDW, O_SSDB, O_RMSG = 272, 320, 332
O_MIXG, O_MIXB, O_XAG, O_XAB, O_FFG, O_FFB = 340, 348, 356, 364, 372, 380
O_FFW, O_FFBIAS = 388, 520
O_DTB, O_ALOG, O_DROW, O_DCOL = 564, 565, 566, 582
NPK = 590
C_ID, C_ULE, C_MST, C_ONE, C_ID16, C_EHP = 0, 128, 256, 384, 512, 768
NCST = 768 + 1024

ENGS = ["pe", "act", "dve", "pool", "sp"]
ENGMAP = {"pe": "tensor", "act": "scalar", "dve": "vector", "pool": "gpsimd", "sp": "sync"}
NLANES = 32


class Tk:
    __slots__ = ("w", "r", "excl")

    def __init__(self, excl=False):
        self.w = None
        self.r = []
        self.excl = excl


class Sched:
    def __init__(self):
        self.ops = {e: [] for e in ENGS}
        self.lane_rr = 0
        self.lane_rr2 = [0, 0]
        self.lane_last = [None] * NLANES
        self.lane_seq = [0] * NLANES

    def _compress(self, deps):
        best = {}
        out = set()
        for d in deps:
            op = self.ops[d[0]][d[1]]
            if op["dma"]:
                key = ("L", op["lane"])
                if key not in best or op["ticket"] > best[key][0]:
                    best[key] = (op["ticket"], d)
            else:
                key = d[0]
                if key not in best or d[1] > best[key][0]:
                    best[key] = (d[1], d)
        for v in best.values():
            out.add(v[1])
        return out

    mute = False

    def add(self, eng, fn, r=(), w=(), dma=False, dur=100.0, lat=0.0):
        if self.mute:
            return None
        idx = len(self.ops[eng])
        me = (eng, idx)
        deps = set()
        if any(t.excl for t in r):
            w = list(w) + [t for t in r if t.excl]
            r = [t for t in r if not t.excl]
        for t in r:
            if t.w is not None:
                deps.add(t.w)
        for t in w:
            if t.w is not None:
                deps.add(t.w)
            deps.update(t.r)
        deps.discard(me)
        import sys as _sys
        op = dict(fn=fn, deps=None, dma=dma, signal=bool(dma), lane=None, ticket=None, dur=dur, lat=lat,
                  line=_sys._getframe(2).f_lineno)
        if dma:
            half = NLANES // 2
            k = 1 if eng == "pool" else 0
            lane = k * half + self.lane_rr2[k]
            self.lane_rr2[k] = (self.lane_rr2[k] + 1) % half
            op["lane"] = lane
            if self.lane_last[lane] is not None:
                deps.add(self.lane_last[lane])
            self.lane_last[lane] = me
            self.lane_seq[lane] += 1
            op["ticket"] = 16 * self.lane_seq[lane]
        op["raw"] = deps
        self.ops[eng].append(op)
        for t in r:
            t.r.append(me)
        for t in w:
            t.w = me
            t.r = []
        return me

    def schedule(self):
        import heapq
        SYNC = 120.0
        ndeps = {}
        users = {}
        for e in ENGS:
            for i, op in enumerate(self.ops[e]):
                ndeps[(e, i)] = len(op["raw"])
                for d in op["raw"]:
                    users.setdefault(d, []).append((e, i))
        ready = {e: [] for e in ENGS}
        readyt = {}
        for e in ENGS:
            for i, op in enumerate(self.ops[e]):
                if not op["raw"]:
                    heapq.heappush(ready[e], i)
                    readyt[(e, i)] = 0.0
        free = {e: 0.0 for e in ENGS}
        busy = {e: False for e in ENGS}
        order = {e: [] for e in ENGS}
        events = []
        now = 0.0
        remaining = sum(len(self.ops[e]) for e in ENGS)

        def try_issue(e, now):
            if busy[e] or not ready[e]:
                return
            i = heapq.heappop(ready[e])
            op = self.ops[e][i]
            order[e].append(i)
            busy[e] = True
            t_free = now + op["dur"]
            heapq.heappush(events, (t_free, 1, e, -1))
            heapq.heappush(events, (t_free + op["lat"] + SYNC, 0, e, i))

        for e in ENGS:
            try_issue(e, 0.0)
        while events:
            t, kind, e, i = heapq.heappop(events)
            now = t
            if kind == 1:
                busy[e] = False
                try_issue(e, now)
            else:
                remaining -= 1
                for u in users.get((e, i), ()):
                    ndeps[u] -= 1
                    if ndeps[u] == 0:
                        heapq.heappush(ready[u[0]], u[1])
                        try_issue(u[0], now)
        assert remaining == 0, ("scheduler: unscheduled ops (cycle?)", remaining)
        self.est_ns = now
        return order

    def emit(self, nc, stack, reorder=True):
        if reorder:
            order = self.schedule()
        else:
            order = {e: list(range(len(self.ops[e]))) for e in ENGS}
        pos = {}
        for e in ENGS:
            for k, i in enumerate(order[e]):
                pos[(e, i)] = k
        for e in ENGS:
            for i in order[e]:
                op = self.ops[e][i]
                best = {}
                for d in op["raw"]:
                    dop = self.ops[d[0]][d[1]]
                    if dop["dma"]:
                        key = ("L", dop["lane"])
                        val = dop["ticket"]
                    else:
                        if e == "pe" and d[0] == "pe" and not op["dma"]:
                            assert pos[d] < pos[(e, i)]
                            continue
                        key = d[0]
                        val = pos[d]
                        if d[0] == e:
                            assert pos[d] < pos[(e, i)], "same-engine dependency order violated"
                    if key not in best or val > best[key][0]:
                        best[key] = (val, d)
                op["deps"] = [v[1] for v in best.values()]
                for d in op["deps"]:
                    self.ops[d[0]][d[1]]["signal"] = True
        esem = {e: stack.enter_context(nc.semaphore("s_" + e)) for e in ENGS}
        lsem = [stack.enter_context(nc.semaphore("l_%d" % i)) for i in range(NLANES)]
        for e in ENGS:
            cnt = 0
            for i in order[e]:
                op = self.ops[e][i]
                if op["dma"]:
                    continue
                if op["signal"]:
                    cnt += 1
                    op["ticket"] = cnt

        def sem_of(dep):
            dop = self.ops[dep[0]][dep[1]]
            if dop["dma"]:
                return lsem[dop["lane"]], dop["ticket"]
            return esem[dep[0]], dop["ticket"]

        with nc.Block() as block:
            for e in ENGS:
                ops = [self.ops[e][i] for i in order[e]]

                def body(eng, e=e, ops=ops):
                    waited = {}
                    for op in ops:
                        for dep in sorted(op["deps"]):
                            sem, val = sem_of(dep)
                            if waited.get(sem.num, 0) >= val:
                                continue
                            eng.wait_ge(sem, val)
                            waited[sem.num] = val
                        inst = op["fn"](eng)
                        if op["signal"]:
                            if op["dma"]:
                                inst.then_inc(lsem[op["lane"]], 16)
                            else:
                                inst.then_inc(esem[e], 1)
                    if e == "sp":
                        for ln in range(NLANES):
                            last = self.lane_last[ln]
                            if last is not None:
                                sem, val = sem_of(last)
                                if waited.get(sem.num, 0) < val:
                                    eng.wait_ge(sem, val)
                                    waited[sem.num] = val

                getattr(block, ENGMAP[e])(body)


class Buf:
    def __init__(self, ap, tk):
        self.ap = ap
        self.tk = tk

    def v(self, pat, **kw):
        return self.ap.rearrange(pat, **kw)


class Scratch:
    def __init__(self, tensor, nbytes):
        self.t = tensor
        self.nbytes = nbytes
        self.top = 0
        self.hist = []

    def alloc(self, nelem, dtype, ntk=1):
        esz = 4 if dtype == F32 else 2
        size = (nelem * esz + 31) // 32 * 32
        off = self.top
        assert off + size <= self.nbytes, ("scratch overflow", off, size, self.nbytes)
        self.top = off + size
        tks = [Tk() for _ in range(ntk)]
        inherit = []
        keep = []
        for (o, s, ts) in self.hist:
            if o < off + size and off < o + s:
                for t in ts:
                    if t.w is not None:
                        inherit.append(t.w)
                    inherit.extend(t.r)
                if not (off <= o and o + s <= off + size):
                    keep.append((o, s, ts))
            else:
                keep.append((o, s, ts))
        self.hist = keep
        inherit = list(set(inherit))
        for t in tks:
            t.r = list(inherit)
        self.hist.append((off, size, tks))
        ap = self.t[:, off // 2:(off + size) // 2]
        if dtype == F32:
            ap = ap.bitcast(F32)[:, 0:nelem]
        else:
            ap = ap[:, 0:nelem]
        return Buf(ap, tks)

    def mark(self):
        return self.top

    def release(self, m):
        self.top = m


class _Stop(Exception):
    pass


def build_program(stop=None, only=None):
    nc = bass.Bass("TRN2", target_bir_lowering=False)

    def chk(l, ph, sub=0):
        if stop is not None and (l, ph, sub) > tuple(stop) + (0,) * (3 - len(stop)):
            S.mute = False
            raise _Stop()
        S.mute = only is not None and ph not in only
        import os
        if os.environ.get("KVSKIP") == "2" and ph == 4 and sub < 1:
            S.mute = True

    S = Sched()

    def din(name, shape):
        return nc.dram_tensor(name, list(shape), F32, kind="ExternalInput").ap()

    def dout(name, shape):
        return nc.dram_tensor(name, list(shape), F32, kind="ExternalOutput").ap()

    xT = din("xT", [128, 8, T]); xsT = din("xsT", [128, 8, NS]); memT = din("memT", [128, 8, 256])
    kcT = din("kcT", [L, NS, 128, 8, 256]); vc = din("vc", [L, NS, 256, 1024])
    confT = din("confT", [L, 128, 8, 30, NS]); ssdcT = din("ssdcT", [L, 128, 12, 3, NS])
    ffncT = din("ffncT", [L, 128, 44, 2, NS]); ssdst = din("ssdst", [L, NS, 1024, 128])
    w_in = din("w_in", [L, 128, 8, D_IN]); w_oA = din("w_oA", [L, 128, 8, 1024]); w_oY = din("w_oY", [L, 128, 8, 1024])
    wq = din("wq", [L, 128, 8, 1024]); wk = din("wk", [L, 128, 8, 1024]); wv = din("wv", [L, 128, 8, 1024])
    wo = din("wo", [L, 128, 8, 1024]); w_up = din("w_up", [L, 128, 8, NFF]); w_dn = din("w_dn", [L, 128, 22, 1024])
    pack = din("pack", [L, 128, NPK]); cst = din("cst", [128, NCST])

    yT = dout("yT", [128, 8, T]); ysT = dout("ysT", [128, 8, NS])
    confP = dout("confP", [L, 128, 8, 30]); ssdcP = dout("ssdcP", [L, 128, 12, 3])
    ssdPT = dout("ssdPT", [L, 128, 1024]); ffncP = dout("ffncP", [L, 128, 44, 2])
    kP = dout("kP", [L, 128, 8, 256]); vP = dout("vP", [L, 256, 1024])
    confS = dout("confS", [L, 128, 8, 30, NS]); ssdcS = dout("ssdcS", [L, 128, 12, 3, NS])
    ssdS = dout("ssdS", [L, NS, 1024, 128]); ffncS = dout("ffncS", [L, 128, 44, 2, NS])

    with ExitStack() as st:
        def sb(name, shape, dt):
            return st.enter_context(nc.sbuf_tensor(name, shape, dt))

        XFp_t = sb("XFp", [128, 8 * T], F32)
        XFs_t = sb("XFs", [128, 8 * NS], F32)
        XFp = XFp_t[:].rearrange("p (c t) -> p c t", c=8)
        XFs = XFs_t[:].rearrange("p (c t) -> p c t", c=8)
        tkXp = [[Tk() for _ in range(16)] for _ in range(8)]

        def XT(c, t0, W):
            return [tkXp[c][j] for j in range(t0 // 128, (t0 + W - 1) // 128 + 1)]
        tkXs = [Tk() for _ in range(8)]
        cstf = sb("cstf", [128, NCST], F32)
        cstb = sb("cstb", [128, 768], BF16)
        pk = sb("pk", [128, NPK], F32)
        acol = sb("acol", [16, 1], F32)
        tkC, tkPk, tkAcol = Tk(), Tk(), Tk()
        SCRB = 130 * 1024
        scr_t = sb("scr", [128, SCRB // 2], BF16)
        scr = Scratch(scr_t, SCRB)
        xbd = nc.dram_tensor("xbd", [128, 16, 8, 128], BF16).ap()
        tkXbd = [Tk() for _ in range(16)]
        XBS_t = sb("XBS", [128, 8 * NS], BF16)
        XBS3 = XBS_t[:].rearrange("p (k s) -> p k s", k=8)
        tkXBS = [Tk()]
        psb = [st.enter_context(nc.psum_tensor("psb%d" % i, [128, 512], F32)) for i in range(8)]
        tkPS = [Tk(excl=True) for _ in range(8)]
        psrr = [0]

        def psum():
            i = psrr[0]
            psrr[0] = (i + 1) % 8
            return psb[i], tkPS[i]

        identf = cstf[:, C_ID:C_ID + 128]
        ulef = cstf[:, C_ULE:C_ULE + 128]
        onesf = cstf[:, C_ONE:C_ONE + 128]
        ehpf = cstf[0:16, C_EHP:C_EHP + 1024].rearrange("p (c q) -> p c q", c=8)
        identb = cstb[:, 0:128]
        mstb = cstb[:, 256:384]
        onesb = cstb[:, 384:512]
        id16b = cstb[:, 512:768].rearrange("p (a b) -> p a b", a=16)

        def fsz(ap):
            n = 1
            for d in ap.shape[1:]:
                n *= d
            return n

        def mm(out, lhsT, rhs, start, stop, r, w):
            n = fsz(rhs)
            d = 25.0 + max(n, 64) / 2.0
            if rhs.dtype == F32:
                d *= 4
            S.add("pe", lambda e: e.matmul(out, lhsT=lhsT, rhs=rhs, start=start, stop=stop), r=r, w=w, dur=d)

        def tr(out, in_, ident, r, w):
            S.add("pe", lambda e: e.transpose(out=out, in_=in_, identity=ident), r=r, w=w, dur=90.0)

        def act(out, in_, func, r, w, bias=None, scale=None, accum_out=None):
            kw = {}
            if bias is not None:
                kw["bias"] = bias
            if scale is not None:
                kw["scale"] = scale
            if accum_out is not None:
                kw["accum_out"] = accum_out
            S.add("act", lambda e: e.activation(out=out, in_=in_, func=func, **kw), r=r, w=w,
                  dur=230.0 + fsz(out) / 1.2)

        def tt(eng, out, in0, in1, op, r, w):
            S.add(eng, lambda e: e.tensor_tensor(out=out, in0=in0, in1=in1, op=op), r=r, w=w,
                  dur=(80.0 + 1.6 * fsz(out)) if eng == "dve" else (200.0 + 2.2 * fsz(out)))

        def ts(eng, out, in0, s1, s2, op0, op1, r, w):
            S.add(eng, lambda e: e.tensor_scalar(out=out, in0=in0, scalar1=s1, scalar2=s2, op0=op0, op1=op1), r=r, w=w,
                  dur=(80.0 + 1.05 * fsz(out)) if eng == "dve" else (200.0 + 2.0 * fsz(out)))

        def stt(out, in0, scalar, in1, op0, op1, r, w, accum_out=None):
            kw = {}
            if accum_out is not None:
                kw["accum_out"] = accum_out
            S.add("dve", lambda e: e.scalar_tensor_tensor(out=out, in0=in0, scalar=scalar, in1=in1,
                                                          op0=op0, op1=op1, **kw), r=r, w=w, dur=80.0 + 1.1 * fsz(out))

        def cp(eng, out, in_, r, w):
            if eng == "act":
                act(out, in_, AF.Copy, r, w)
            else:
                S.add(eng, lambda e: e.tensor_copy(out=out, in_=in_), r=r, w=w,
                      dur=(80.0 + 1.05 * fsz(out)) if eng == "dve" else (200.0 + 1.7 * fsz(out)))

        def red(out, in_, op, r, w):
            S.add("dve", lambda e: e.tensor_reduce(out=out, in_=in_, axis=AX.X, op=op), r=r, w=w, dur=80.0 + 1.05 * fsz(in_))

        def recip(out, in_, r, w):
            S.add("dve", lambda e: e.reciprocal(out=out, in_=in_), r=r, w=w, dur=80.0 + 8.4 * fsz(out))

        def scan(out, d0, d1, r, w):
            S.add("dve", lambda e: e.tensor_tensor_scan(out=out, data0=d0, data1=d1, initial=0.0,
                                                          op0=ALU.mult, op1=ALU.add), r=r, w=w, dur=80.0 + 2.1 * fsz(out))

        def mset(eng, ap, val, w):
            S.add(eng, lambda e: e.memset(ap, val), w=w, dur=100.0 + fsz(ap))

        swq = []
        SWLIM = 600

        def dma(q, out, in_, r=(), w=()):
            r = list(r)
            if q == "pool":
                def nd(ap):
                    n = ap.shape[0]
                    for d in ap.shape[1:-1]:
                        n *= d
                    return max(1, n // 16)
                n = max(nd(out), nd(in_))
                while swq and sum(x[1] for x in swq) + n > SWLIM:
                    old = swq.pop(0)
                    t = Tk()
                    t.w = old[0]
                    r.append(t)
                nbytes = fsz(out) * out.shape[0] * (4 if out.dtype == F32 else 2)
                me = S.add(q, lambda e: e.dma_start(out=out, in_=in_), r=r, w=w, dma=True, dur=1200.0,
                           lat=2000.0 + nbytes / 150.0)
                if me is not None:
                    swq.append((me, n))
                return
            nbytes = fsz(out) * out.shape[0] * (4 if out.dtype == F32 else 2)
            S.add(q, lambda e: e.dma_start(out=out, in_=in_), r=r, w=w, dma=True, dur=100.0,
                  lat=2000.0 + nbytes / 150.0)

        dma("sp", cstf[:], cst, w=[tkC])
        for c in range(8):
            for b in range(4):
                dma("sp", XFp[:, c, b * 512:(b + 1) * 512], xT[:, c, b * 512:(b + 1) * 512], w=XT(c, b * 512, 512))
        dma("sp", XFs, xsT, w=tkXs)
        cp("pool", cstb[:, 0:384], cstf[:, 0:384], [tkC], [tkC])
        cp("pool", cstb[:, 512:768], cstf[:, C_ID16:C_ID16 + 256], [tkC], [tkC])
        mset("pool", cstb[:, 384:512], 1.0 / 1024.0, [tkC])

        def xf(blk, c):
            if blk[0] == "S":
                return XFs[:, c, :]
            return XFp[:, c, blk[1]:blk[1] + blk[2]]

        def xtk(blk, c):
            if blk[0] == "S":
                return [tkXs[c]]
            return XT(c, blk[1], blk[2])

        def bw(blk):
            return NS if blk[0] == "S" else blk[2]

        def pkc(off, n=1):
            return pk[:, off:off + n]

        def load_w(dst3, src3, tks, step):
            M = dst3.shape[2]
            i = 0
            for m0 in range(0, M, step):
                m1 = min(M, m0 + step)
                dma("pool", dst3[:, :, m0:m1], src3[:, :, m0:m1], w=[tks[i]])
                i += 1

        class LNT:
            def __init__(self, W):
                self.W = W
                self.sq = scr.alloc(8 * W, BF16)
                self.sbb = scr.alloc(8 * W, BF16)
                self.small = [scr.alloc(W, F32) for _ in range(4)]
                self.t1 = [scr.alloc(W, F32) for _ in range(2)]
                self.t2 = [scr.alloc(W, F32) for _ in range(2)]
                self.i = 0

        def layernorm(lt, W, src, src_tk, src_is_bf, dst, dst_tk, gcol, bcol, func, rms=False):
            sq3 = lt.sq.ap.rearrange("p (c w) -> p c w", c=8)
            sb3 = lt.sbb.ap.rearrange("p (c w) -> p c w", c=8)
            for c in range(8):
                act(sq3[:, c, 0:W], src(c), AF.Square, src_tk(c), lt.sq.tk)
                if not src_is_bf and not rms:
                    cp("dve", sb3[:, c, 0:W], src(c), src_tk(c), lt.sbb.tk)
            psq, tq = psum()
            for c in range(8):
                mm(psq[:, 0:W], onesb, sq3[:, c, 0:W], c == 0, c == 7, [tkC] + lt.sq.tk, [tq])
            mean, m2, rstd, nmr = [b for b in lt.small]
            if not rms:
                psm, tm = psum()
                for c in range(8):
                    rhs = src(c) if src_is_bf else sb3[:, c, 0:W]
                    rtk = src_tk(c) if src_is_bf else lt.sbb.tk
                    mm(psm[:, 0:W], onesb, rhs, c == 0, c == 7, [tkC] + rtk, [tm])
                act(mean.ap[:, 0:W], psm[:, 0:W], AF.Copy, [tm], mean.tk)
                tt("dve", m2.ap[:, 0:W], mean.ap[:, 0:W], mean.ap[:, 0:W], ALU.mult, mean.tk, m2.tk)
                tt("dve", m2.ap[:, 0:W], psq[:, 0:W], m2.ap[:, 0:W], ALU.subtract, [tq] + m2.tk, m2.tk)
                act(rstd.ap[:, 0:W], m2.ap[:, 0:W], AF.Ln, m2.tk, rstd.tk, bias=EPS)
            else:
                act(rstd.ap[:, 0:W], psq[:, 0:W], AF.Ln, [tq], rstd.tk, bias=EPS)
            act(rstd.ap[:, 0:W], rstd.ap[:, 0:W], AF.Exp, rstd.tk, rstd.tk, scale=-0.5)
            if not rms:
                stt(nmr.ap[:, 0:W], mean.ap[:, 0:W], -1.0, rstd.ap[:, 0:W], ALU.mult, ALU.mult,
                    mean.tk + rstd.tk, nmr.tk)
            for c in range(8):
                i = lt.i
                lt.i = (i + 1) % 2
                t1, t2 = lt.t1[i], lt.t2[i]
                tt("dve", t1.ap[:, 0:W], src(c), rstd.ap[:, 0:W], ALU.mult, src_tk(c) + rstd.tk, t1.tk)
                if not rms:
                    tt("dve" if W <= 256 else "pool", t2.ap[:, 0:W], t1.ap[:, 0:W], nmr.ap[:, 0:W], ALU.add, t1.tk + nmr.tk, t2.tk)
                    tin, ttk = t2, t2.tk
                else:
                    tin, ttk = t1, t1.tk
                outs = dst(c)
                for oi, o in enumerate(outs):
                    if bcol is not None:
                        act(o, tin.ap[:, 0:W], func, ttk + [tkPk], dst_tk(c), bias=bcol(c), scale=gcol(c))
                    else:
                        act(o, tin.ap[:, 0:W], func, ttk + [tkPk], dst_tk(c), scale=gcol(c))

        PB512 = [("P", b * 512, 512) for b in range(4)]
        SB = ("S", 0, NS)

        for l in range(L):
          try:
            dma("sp", pk[:], pack[l], w=[tkPk])
            act(acol[:], pk[0:16, O_ALOG:O_ALOG + 1], AF.Exp, [tkPk], [tkAcol])
            ts("dve", acol[:], acol[:], -1.0, None, ALU.mult, ALU.bypass, [tkAcol], [tkAcol])

            chk(l, 0)
            scr.release(0)
            CO = scr.alloc(8 * T, BF16, ntk=32)
            COs = scr.alloc(8 * NS, BF16, ntk=8)
            CO3 = CO.v("p (c t) -> p c t", c=8)
            COs3 = COs.v("p (c t) -> p c t", c=8)
            mA = scr.mark()
            XB = scr.alloc(8 * T, BF16, ntk=32)
            XB3 = XB.v("p (c t) -> p c t", c=8)
            XBs3 = XBS3
            abuf = [scr.alloc(30 + T, BF16, ntk=5) for _ in range(2)]
            hs = [scr.alloc(31 * NS, BF16, ntk=2) for _ in range(2)]
            diag = [scr.alloc(31 * 128, BF16) for _ in range(2)]
            Wc = [scr.alloc(2 * 8 * 128, BF16, ntk=2) for _ in range(2)]
            sg = [scr.alloc(512, F32) for _ in range(2)]
            anew = [scr.alloc(NS, F32) for _ in range(2)]
            tailb = [scr.alloc(30, F32) for _ in range(2)]
            for c in range(8):
                for b in range(4):
                    cp("pool", XB3[:, c, b * 512:(b + 1) * 512], XFp[:, c, b * 512:(b + 1) * 512],
                       XT(c, b * 512, 512), [XB.tk[c * 4 + b]])
            cp("pool", XBs3, XFs, tkXs, tkXBS)
            for j in range(16):
                dma("sp", xbd[:, j], XB3[:, :, j * 128:(j + 1) * 128], r=[XB.tk[c * 4 + j // 4] for c in range(8)], w=[tkXbd[j]])
            for i in range(2):
                mset("pool", abuf[i].ap[:, 0:30], 0.0, [abuf[i].tk[0]])
            sgi = 0
            for c in range(8):
                i = c % 2
                Wc4 = Wc[i].v("p (h k m) -> p h k m", h=2, k=8)
                dma("pool", Wc4[:, 0], w_in[l][:, :, c * 128:(c + 1) * 128], w=[Wc[i].tk[0]])
                dma("pool", Wc4[:, 1], w_in[l][:, :, 1024 + c * 128:1024 + (c + 1) * 128], w=[Wc[i].tk[1]])
                dg3 = diag[i].v("p (k m) -> p k m", k=31)
                tt("pool", dg3, identb.unsqueeze(1).to_broadcast([128, 31, 128]),
                   pkc(O_CONFW + c * 31, 31).unsqueeze(2).to_broadcast([128, 31, 128]), ALU.mult,
                   [tkC, tkPk], diag[i].tk)
                hs3 = hs[i].v("p (k s) -> p k s", k=31)
                dma("pool", hs3[:, 0:30, :], confT[l][:, c], w=[hs[i].tk[0]])
                dma("sp", confS[l][:, c, 0:29, :], confT[l][:, c, 1:30, :])
                bcol = pkc(O_CONFB + c)
                for b in range(4):
                    psv, tv = psum()
                    pg, tg = psum()
                    for kc in range(8):
                        mm(psv[:, :], Wc4[:, 0, kc, :], XB3[:, kc, b * 512:(b + 1) * 512], kc == 0, kc == 7,
                           [Wc[i].tk[0], XB.tk[kc * 4 + b]], [tv])
                    for kc in range(8):
                        mm(pg[:, :], Wc4[:, 1, kc, :], XB3[:, kc, b * 512:(b + 1) * 512], kc == 0, kc == 7,
                           [Wc[i].tk[1], XB.tk[kc * 4 + b]], [tg])
                    s_ = sg[sgi]
                    sgi = (sgi + 1) % 2
                    act(s_.ap, pg[:, :], AF.Sigmoid, [tg], s_.tk)
                    tt("dve", abuf[i].ap[:, 30 + b * 512:30 + (b + 1) * 512], psv[:, :], s_.ap, ALU.mult,
                       [tv] + s_.tk, [abuf[i].tk[1 + b]])
                psv, tv = psum()
                pg, tg = psum()
                for kc in range(8):
                    mm(psv[:, 0:NS], Wc4[:, 0, kc, :], XBs3[:, kc, :], kc == 0, kc == 7, [Wc[i].tk[0]] + tkXBS, [tv])
                for kc in range(8):
                    mm(pg[:, 0:NS], Wc4[:, 1, kc, :], XBs3[:, kc, :], kc == 0, kc == 7, [Wc[i].tk[1]] + tkXBS, [tg])
                s_ = sg[sgi]
                sgi = (sgi + 1) % 2
                act(s_.ap[:, 0:NS], pg[:, 0:NS], AF.Sigmoid, [tg], s_.tk)
                tt("dve", anew[i].ap, psv[:, 0:NS], s_.ap[:, 0:NS], ALU.mult, [tv] + s_.tk, anew[i].tk)
                cp("pool", hs3[:, 30, :], anew[i].ap, anew[i].tk, [hs[i].tk[1]])
                dma("sp", confS[l][:, c, 29, :], anew[i].ap, r=anew[i].tk)
                for b in range(4):
                    pc, tc = psum()
                    rt = [abuf[i].tk[1 + b], abuf[i].tk[b]] + diag[i].tk
                    for k in range(31):
                        mm(pc[:, :], dg3[:, k, :], abuf[i].ap[:, b * 512 + k:b * 512 + k + 512], k == 0, k == 30, rt, [tc])
                    act(CO3[:, c, b * 512:(b + 1) * 512], pc[:, :], AF.Identity, [tc, tkPk], [CO.tk[c * 4 + b]], bias=bcol)
                pc, tc = psum()
                for k in range(31):
                    mm(pc[:, 0:NS], dg3[:, k, :], hs3[:, k, :], k == 0, k == 30, hs[i].tk + diag[i].tk, [tc])
                act(COs3[:, c, :], pc[:, 0:NS], AF.Identity, [tc, tkPk], [COs.tk[c]], bias=bcol)
                cp("act", tailb[i].ap, abuf[i].ap[:, T:T + 30], [abuf[i].tk[4]], tailb[i].tk)
                dma("sp", confP[l][:, c, :], tailb[i].ap, r=tailb[i].tk)

            chk(l, 1)
            scr.release(mA)
            WoA = scr.alloc(8 * 1024, BF16, ntk=4)
            WoA3 = WoA.v("p (k m) -> p k m", k=8)
            load_w(WoA3, w_oA[l], WoA.tk, 256)
            lnt = LNT(512)
            for blk in PB512 + [SB]:
                W = bw(blk)
                if blk[0] == "S":
                    srcf = lambda c: COs3[:, c, :]
                    stk = lambda c: [COs.tk[c]]
                else:
                    b = blk[1] // 512
                    srcf = lambda c, b=b: CO3[:, c, b * 512:(b + 1) * 512]
                    stk = lambda c, b=b: [CO.tk[c * 4 + b]]
                layernorm(lnt, W, srcf, stk, True, lambda c: [srcf(c)], stk,
                          lambda c: pkc(O_CLNG + c), lambda c: pkc(O_CLNB + c), AF.Silu)
                for m in range(8):
                    ps, tp = psum()
                    for kc in range(8):
                        mm(ps[:, 0:W], WoA3[:, kc, m * 128:(m + 1) * 128], srcf(kc), kc == 0, kc == 7,
                           [WoA.tk[m // 2]] + stk(kc), [tp])
                    stt(xf(blk, m), xf(blk, m), ALPHA, ps[:, 0:W], ALU.mult, ALU.add, xtk(blk, m) + [tp], xtk(blk, m))

            chk(l, 2)
            scr.release(0)
            Wx = scr.alloc(8 * 1536, BF16, ntk=6)
            Wz = scr.alloc(8 * 1024, BF16, ntk=4)
            Wdt = scr.alloc(8 * 16, BF16)
            WoY = scr.alloc(8 * 1024, BF16, ntk=4)
            Wx3 = Wx.v("p (k m) -> p k m", k=8)
            Wz3 = Wz.v("p (k m) -> p k m", k=8)
            Wdt3 = Wdt.v("p (k m) -> p k m", k=8)
            WoY3 = WoY.v("p (k m) -> p k m", k=8)
            load_w(Wx3, w_in[l][:, :, 3072:4608], Wx.tk, 256)
            load_w(Wdt3, w_in[l][:, :, 4608:4624], Wdt.tk, 16)
            load_w(Wz3, w_in[l][:, :, 2048:3072], Wz.tk, 256)
            load_w(WoY3, w_oY[l], WoY.tk, 256)
            dg4 = scr.alloc(12 * 4 * 128, BF16, ntk=12)
            dg44 = dg4.v("p (u k m) -> p u k m", u=12, k=4)
            for u in range(12):
                tt("pool", dg44[:, u], identb.unsqueeze(1).to_broadcast([128, 4, 128]),
                   pkc(O_SSDW + u * 4, 4).unsqueeze(2).to_broadcast([128, 4, 128]), ALU.mult, [tkC, tkPk], [dg4.tk[u]])
            lnt = LNT(128)
            mW = scr.mark()
            hT = scr.alloc(1024, F32, ntk=2)
            hTb = scr.alloc(1024, BF16, ntk=2)
            mset("pool", hT.ap, 0.0, hT.tk)
            mset("pool", hTb.ap, 0.0, hTb.tk)
            xbr = [scr.alloc(8 * 128, BF16) for _ in range(3)]
            raw = scr.alloc(12 * 131, BF16)
            raw3 = raw.v("p (u t) -> p u t", u=12)
            xact = scr.alloc(12 * 128, BF16)
            xact3 = xact.v("p (u t) -> p u t", u=12)
            rawt = scr.alloc(36, F32)
            fm = [scr.alloc(128, F32) for _ in range(4)]
            tok4 = scr.alloc(64, F32)
            tok43 = tok4.v("p (a h) -> p a h", a=4)
            e1 = scr.alloc(16, F32); dte = scr.alloc(16, F32); cdec = scr.alloc(16, F32)
            xsT_ = scr.alloc(1024, BF16); BtT = scr.alloc(256, BF16)
            Rb = scr.alloc(16 * 128, BF16); LT = scr.alloc(16 * 128, BF16); scT = Rb
            Gm = scr.alloc(256, BF16)
            dtx = scr.alloc(1024, BF16); dtxe = scr.alloc(1024, BF16); Dxs = scr.alloc(1024, BF16)
            yo = scr.alloc(1024, F32); zs = scr.alloc(1024, BF16)
            yn = scr.alloc(1024, BF16); Yfm = scr.alloc(1024, BF16); junk = yn
            Yfm3 = Yfm.v("p (c t) -> p c t", c=8)
            ss = scr.alloc(1, F32); rs1 = scr.alloc(1, F32)

            def bc_h(ap16, lo, n):
                return ap16[:, lo:lo + n].unsqueeze(2).to_broadcast([128, n, 64])

            for j in range(16):
                t0 = j * 128
                blk = ("P", t0, 128)
                b5 = t0 // 512
                xb = xbr[j % 3]
                xb3 = xb.v("p (k t) -> p k t", k=8)
                dma("sp", xb3, xbd[:, j], r=[tkXbd[j]], w=xb.tk)
                if j == 0:
                    mset("pool", raw3[:, :, 0:3], 0.0, raw.tk)
                else:
                    cp("pool", raw3[:, :, 0:3], raw3[:, :, 128:131], raw.tk, raw.tk)
                for g4 in range(3):
                    ps, tp = psum()
                    for uu in range(4):
                        u = g4 * 4 + uu
                        for kc in range(8):
                            mm(ps[:, uu * 128:(uu + 1) * 128], Wx3[:, kc, u * 128:(u + 1) * 128], xb3[:, kc, :],
                               kc == 0, kc == 7, [Wx.tk[u // 2]] + xb.tk, [tp])
                    cp("dve", raw3[:, g4 * 4:g4 * 4 + 4, 3:131], ps[:, :].rearrange("p (u t) -> p u t", u=4), [tp], raw.tk)
                if j == 15:
                    cp("act", rawt.ap.rearrange("p (u k) -> p u k", u=12), raw3[:, :, 128:131], raw.tk, rawt.tk)
                    dma("sp", ssdcP[l], rawt.ap.rearrange("p (u k) -> p u k", u=12), r=rawt.tk)
                for g4 in range(3):
                    ps, tp = psum()
                    for uu in range(4):
                        u = g4 * 4 + uu
                        for k in range(4):
                            mm(ps[:, uu * 128:(uu + 1) * 128], dg44[:, u, k, :], raw3[:, u, k:k + 128], k == 0, k == 3,
                               raw.tk + [dg4.tk[u]], [tp])
                    for uu in range(4):
                        u = g4 * 4 + uu
                        act(xact3[:, u, :], ps[:, uu * 128:(uu + 1) * 128], AF.Silu, [tp, tkPk], xact.tk, bias=pkc(O_SSDB + u))
                ps, tp = psum()
                for kc in range(8):
                    mm(ps[0:16, 0:128], Wdt3[:, kc, :], xb3[:, kc, :], kc == 0, kc == 7, Wdt.tk + xb.tk, [tp])
                fe, fdt, fdtA, fac = fm
                act(fe.ap[0:16, :], ps[0:16, 0:128], AF.Exp, [tp, tkPk], fe.tk, bias=pk[0:16, O_DTB:O_DTB + 1])
                act(fdt.ap[0:16, :], fe.ap[0:16, :], AF.Ln, fe.tk, fdt.tk, bias=1.0)
                ts("dve", fdtA.ap[0:16, :], fdt.ap[0:16, :], acol[:], None, ALU.mult, ALU.bypass, fdt.tk + [tkAcol], fdtA.tk)
                scan(fac.ap[0:16, :], onesf[0:16, :], fdtA.ap[0:16, :], fdtA.tk + [tkC], fac.tk)
                ps, tp = psum()
                tr(ps[:, 0:16], fdt.ap[0:16, :], identf[0:16, 0:16], fdt.tk + [tkC], [tp])
                tr(ps[:, 16:32], fdtA.ap[0:16, :], identf[0:16, 0:16], fdtA.tk + [tkC], [tp])
                tr(ps[:, 32:48], fac.ap[0:16, :], identf[0:16, 0:16], fac.tk + [tkC], [tp])
                tr(ps[:, 48:64], fac.ap[0:16, 127:128].to_broadcast([16, 128]), identf[0:16, 0:16], fac.tk + [tkC], [tp])
                cp("dve", tok4.ap, ps[:, 0:64], [tp], tok4.tk)
                dtT, dtAT, acT, totT = tok43[:, 0, :], tok43[:, 1, :], tok43[:, 2, :], tok43[:, 3, :]
                act(e1.ap, acT, AF.Exp, tok4.tk, e1.tk)
                tt("dve", dte.ap, totT, acT, ALU.subtract, tok4.tk, dte.tk)
                act(dte.ap, dte.ap, AF.Exp, dte.tk, dte.tk)
                tt("dve", dte.ap, dte.ap, dtT, ALU.mult, dte.tk + tok4.tk, dte.tk)
                act(cdec.ap, totT, AF.Exp, tok4.tk, cdec.tk)
                ps, tp = psum()
                pbf = ps[:, :].bitcast(BF16)
                for c in range(8):
                    tr(pbf[:, c * 128:(c + 1) * 128], xact3[:, c, :], identb, xact.tk + [tkC], [tp])
                cp("act", xsT_.ap, pbf[:, 0:1024], [tp], xsT_.tk)
                ps, tp = psum()
                pbf = ps[:, :].bitcast(BF16)
                for g in range(2):
                    tr(pbf[:, g * 128:(g + 1) * 128], xact3[:, 8 + g, :], identb, xact.tk + [tkC], [tp])
                cp("dve", BtT.ap, pbf[:, 0:256], [tp], BtT.tk)
                Rb3 = Rb.v("p (h l) -> p h l", h=16)
                LT3 = LT.v("p (h l) -> p h l", h=16)
                sc3 = scT.v("p (h l) -> p h l", h=16)
                tt("pool", Rb3, ulef.unsqueeze(1).to_broadcast([128, 16, 128]),
                   dtAT.unsqueeze(2).to_broadcast([128, 16, 128]), ALU.mult, [tkC] + tok4.tk, Rb.tk)
                for q in range(4):
                    ps, tp = psum()
                    mm(ps[:, :], mstb, Rb.ap[:, q * 512:(q + 1) * 512], True, True, [tkC] + Rb.tk, [tp])
                    act(LT.ap[:, q * 512:(q + 1) * 512], ps[:, :], AF.Exp, [tp], LT.tk)
                ps, tp = psum()
                for g in range(2):
                    mm(ps[:, g * 128:(g + 1) * 128], xact3[:, 8 + g, :], xact3[:, 10 + g, :], True, True, xact.tk, [tp])
                tt("dve", Gm.v("p (g l) -> p g l", g=2), ps[:, 0:256].rearrange("p (g l) -> p g l", g=2),
                   ulef.unsqueeze(1).to_broadcast([128, 2, 128]), ALU.mult, [tp, tkC], Gm.tk)
                Gm3 = Gm.v("p (g l) -> p g l", g=2)
                for g in range(2):
                    tt("pool" if g else "dve", sc3[:, 8 * g:8 * g + 8, :], LT3[:, 8 * g:8 * g + 8, :],
                       Gm3[:, g, :].unsqueeze(1).to_broadcast([128, 8, 128]), ALU.mult, LT.tk + Gm.tk, scT.tk)
                xs3 = xsT_.v("p (h d) -> p h d", h=16)
                tt("dve", dtx.v("p (h d) -> p h d", h=16), xs3, bc_h(dtT, 0, 16), ALU.mult, xsT_.tk + tok4.tk, dtx.tk)
                tt("pool", dtxe.v("p (h d) -> p h d", h=16), xs3, bc_h(dte.ap, 0, 16), ALU.mult, xsT_.tk + dte.tk, dtxe.tk)
                tt("pool", Dxs.v("p (h d) -> p h d", h=16), xs3, bc_h(pk[:, O_DROW:O_DROW + 16], 0, 16), ALU.mult,
                   xsT_.tk + [tkPk], Dxs.tk)
                dtx3 = dtx.v("p (h d) -> p h d", h=16)
                psY = []
                for g in range(2):
                    ps, tp = psum()
                    mm(ps[:, :], identb, Dxs.ap[:, g * 512:(g + 1) * 512], True, False, [tkC] + Dxs.tk, [tp])
                    for hh in range(8):
                        h = g * 8 + hh
                        mm(ps[:, hh * 64:(hh + 1) * 64], sc3[:, h, :], dtx3[:, h, :], False, hh == 7, scT.tk + dtx.tk, [tp])
                    psY.append((ps, tp))
                yo3 = yo.v("p (h d) -> p h d", h=16)
                for g in range(2):
                    ps, tp = psum()
                    mm(ps[:, :], xact3[:, 10 + g, :], hTb.ap[:, g * 512:(g + 1) * 512], True, True, xact.tk + [hTb.tk[g]], [tp])
                    tt("dve", yo3[:, 8 * g:8 * g + 8, :], ps[:, :].rearrange("p (h d) -> p h d", h=8),
                       bc_h(e1.ap, 8 * g, 8), ALU.mult, [tp] + e1.tk, yo.tk)
                    tt("dve", yo.ap[:, g * 512:(g + 1) * 512], psY[g][0][:, :], yo.ap[:, g * 512:(g + 1) * 512], ALU.add,
                       [psY[g][1]] + yo.tk, yo.tk)
                for hf in range(2):
                    ps, tp = psum()
                    for kc in range(8):
                        mm(ps[:, :], xb3[:, kc, :], Wz3[:, kc, hf * 512:(hf + 1) * 512], kc == 0, kc == 7,
                           xb.tk + [Wz.tk[2 * hf], Wz.tk[2 * hf + 1]], [tp])
                    act(zs.ap[:, hf * 512:(hf + 1) * 512], ps[:, :], AF.Silu, [tp], zs.tk)
                tt("dve", yo.ap, yo.ap, zs.ap, ALU.mult, yo.tk + zs.tk, yo.tk)
                act(junk.ap, yo.ap, AF.Square, yo.tk, junk.tk + ss.tk, accum_out=ss.ap)
                act(rs1.ap, ss.ap, AF.Ln, ss.tk, rs1.tk, bias=EPS, scale=1.0 / 1024.0)
                act(rs1.ap, rs1.ap, AF.Exp, rs1.tk, rs1.tk, scale=-0.5)
                act(yn.ap, yo.ap, AF.Copy, yo.tk + rs1.tk, yn.tk, scale=rs1.ap)
                ps, tp = psum()
                pbf = ps[:, :].bitcast(BF16)
                for c in range(8):
                    tr(pbf[:, c * 128:(c + 1) * 128], yn.ap[:, c * 128:(c + 1) * 128], identb, yn.tk + [tkC], [tp])
                tt("dve", Yfm3, pbf[:, 0:1024].rearrange("p (c t) -> p c t", c=8),
                   pk[:, O_RMSG:O_RMSG + 8].unsqueeze(2).to_broadcast([128, 8, 128]), ALU.mult, [tp, tkPk], Yfm.tk)
                hT3 = hT.v("p (h d) -> p h d", h=16)
                for g in range(2):
                    ps, tp = psum()
                    mm(ps[:, :], BtT.ap[:, g * 128:(g + 1) * 128], dtxe.ap[:, g * 512:(g + 1) * 512], True, True,
                       BtT.tk + dtxe.tk, [tp])
                    tt("pool", hT3[:, 8 * g:8 * g + 8, :], hT3[:, 8 * g:8 * g + 8, :], bc_h(cdec.ap, 8 * g, 8), ALU.mult,
                       [hT.tk[g]] + cdec.tk, [hT.tk[g]])
                    tt("dve", hT.ap[:, g * 512:(g + 1) * 512], hT.ap[:, g * 512:(g + 1) * 512], ps[:, :], ALU.add,
                       [hT.tk[g], tp], [hT.tk[g]])
                    cp("act", hTb.ap[:, g * 512:(g + 1) * 512], hT.ap[:, g * 512:(g + 1) * 512], [hT.tk[g]], [hTb.tk[g]])
                for m4 in range(2):
                    ps, tp = psum()
                    for mm_ in range(4):
                        m = m4 * 4 + mm_
                        for kc in range(8):
                            mm(ps[:, mm_ * 128:(mm_ + 1) * 128], WoY3[:, kc, m * 128:(m + 1) * 128], Yfm3[:, kc, :],
                               kc == 0, kc == 7, [WoY.tk[m // 2]] + Yfm.tk, [tp])
                    xtks = [tkXp[m4 * 4 + q][j] for q in range(4)]
                    tt("dve", XFp[:, m4 * 4:m4 * 4 + 4, t0:t0 + 128], XFp[:, m4 * 4:m4 * 4 + 4, t0:t0 + 128],
                       ps[:, :].rearrange("p (m t) -> p m t", m=4), ALU.add, xtks + [tp], xtks)
                layernorm(lnt, 128, lambda c: xf(blk, c), lambda c: xtk(blk, c), False,
                          lambda c: [xf(blk, c)], lambda c: xtk(blk, c),
                          lambda c: pkc(O_MIXG + c), lambda c: pkc(O_MIXB + c), AF.Identity)
            dma("sp", ssdPT[l], hT.ap, r=hT.tk)

            chk(l, 3)
            scr.release(mW)
            xbs = Buf(XBS_t[:], tkXBS)
            xbs3 = XBS3
            raws = scr.alloc(12 * NS, F32)
            raws3 = raws.v("p (u s) -> p u s", u=12)
            hs4 = scr.alloc(12 * 4 * NS, BF16, ntk=2)
            hs44 = hs4.v("p (u k s) -> p u k s", u=12, k=4)
            xas = scr.alloc(12 * NS, F32)
            xas3 = xas.v("p (u s) -> p u s", u=12)
            ps, tp = psum()
            for u in range(12):
                for kc in range(8):
                    mm(ps[:, u * NS:(u + 1) * NS], Wx3[:, kc, u * 128:(u + 1) * 128], xbs3[:, kc, :], kc == 0, kc == 7,
                       [Wx.tk[u // 2]] + xbs.tk, [tp])
            cp("act", raws.ap, ps[:, 0:12 * NS], [tp], raws.tk)
            dma("sp", ssdcS[l][:, :, 2, :], raws3, r=raws.tk)
            dma("sp", ssdcS[l][:, :, 0:2, :], ssdcT[l][:, :, 1:3, :])
            dma("pool", hs44[:, :, 0:3, :], ssdcT[l], w=[hs4.tk[0]])
            cp("pool", hs44[:, :, 3, :], raws3, raws.tk, [hs4.tk[1]])
            ps, tp = psum()
            for u in range(12):
                for k in range(4):
                    mm(ps[:, u * NS:(u + 1) * NS], dg44[:, u, k, :], hs44[:, u, k, :], k == 0, k == 3, hs4.tk + [dg4.tk[u]], [tp])
            for u in range(12):
                act(xas3[:, u, :], ps[:, u * NS:(u + 1) * NS], AF.Silu, [tp, tkPk], xas.tk, bias=pkc(O_SSDB + u))
            xasb = scr.alloc(4 * NS, BF16)
            xasb3 = xasb.v("p (u s) -> p u s", u=4)
            cp("pool", xasb3, xas3[:, 8:12, :], xas.tk, xasb.tk)
            dts = scr.alloc(NS, F32); dAs = scr.alloc(NS, F32)
            ps, tp = psum()
            for kc in range(8):
                mm(ps[0:16, 0:NS], Wdt3[:, kc, :], xbs3[:, kc, :], kc == 0, kc == 7, Wdt.tk + xbs.tk, [tp])
            act(dts.ap[0:16, :], ps[0:16, 0:NS], AF.Exp, [tp, tkPk], dts.tk, bias=pk[0:16, O_DTB:O_DTB + 1])
            act(dts.ap[0:16, :], dts.ap[0:16, :], AF.Ln, dts.tk, dts.tk, bias=1.0)
            act(dAs.ap[0:16, :], dts.ap[0:16, :], AF.Exp, dts.tk + [tkAcol], dAs.tk, scale=acol[:])
            dE = scr.alloc(2 * 8 * NS, F32)
            dE4 = dE.v("p (a c s) -> p a c s", a=2, c=8)
            ps, tp = psum()
            for c in range(8):
                mm(ps[:, c * NS:(c + 1) * NS], ehpf[:, c, :], dts.ap[0:16, :], True, True, [tkC] + dts.tk, [tp])
            for c in range(8):
                mm(ps[:, 128 + c * NS:128 + (c + 1) * NS], ehpf[:, c, :], dAs.ap[0:16, :], True, True, [tkC] + dAs.tk, [tp])
            cp("dve", dE.ap, ps[:, 0:256], [tp], dE.tk)
            dtxs = scr.alloc(8 * NS, F32)
            dtxs3 = dtxs.v("p (c s) -> p c s", c=8)
            tt("dve", dtxs3, dE4[:, 0], xas3[:, 0:8, :], ALU.mult, dE.tk + xas.tk, dtxs.tk)
            ysm = scr.alloc(8 * NS, F32)
            ysm3 = ysm.v("p (c s) -> p c s", c=8)
            Hb = [scr.alloc(8 * 128, F32) for _ in range(2)]
            bcb = [scr.alloc(4 * 128, BF16) for _ in range(2)]
            t1s = [scr.alloc(128, F32) for _ in range(2)]
            jk2 = [scr.alloc(128, F32) for _ in range(2)]
            for s in range(NS):
                H = Hb[s % 2]
                H3 = H.v("p (c n) -> p c n", c=8)
                dma("sp", H3, ssdst[l, s].rearrange("(c q) n -> q c n", q=128), w=H.tk)
                bb = bcb[s % 2]
                bb3 = bb.v("p (u q) -> p u q", u=4)
                cp("pool", bb3, xasb3[:, :, s:s + 1].to_broadcast([128, 4, 128]), xasb.tk, bb.tk)
                psbc, tbc = psum()
                for u in range(4):
                    mm(psbc[:, u * 128:(u + 1) * 128], bb3[:, u, :], identb, True, True, bb.tk + [tkC], [tbc])
                for c in range(8):
                    g = c // 4
                    t1 = t1s[c % 2]
                    act(t1.ap, psbc[:, g * 128:(g + 1) * 128], AF.Copy, [tbc] + dtxs.tk, t1.tk, scale=dtxs3[:, c, s:s + 1])
                    stt(H3[:, c, :], H3[:, c, :], dE4[:, 1, c, s:s + 1], t1.ap, ALU.mult, ALU.add, H.tk + dE.tk + t1.tk, H.tk)
                    jk = jk2[c % 2]
                    stt(jk.ap, H3[:, c, :], 1.0, psbc[:, (2 + g) * 128:(3 + g) * 128], ALU.mult, ALU.mult,
                        H.tk + [tbc], jk.tk + ysm.tk, accum_out=ysm3[:, c, s:s + 1])
                dma("sp", ssdS[l, s].rearrange("(c q) n -> q c n", q=128), H3, r=H.tk)
            tmpd = scr.alloc(8 * NS, F32)
            tmpd3 = tmpd.v("p (c s) -> p c s", c=8)
            tt("dve", tmpd3, xas3[:, 0:8, :], pk[:, O_DCOL:O_DCOL + 8].unsqueeze(2).to_broadcast([128, 8, NS]), ALU.mult,
               xas.tk + [tkPk], tmpd.tk)
            tt("dve", ysm.ap, ysm.ap, tmpd.ap, ALU.add, ysm.tk + tmpd.tk, ysm.tk)
            zss = scr.alloc(8 * NS, F32)
            ps, tp = psum()
            for c in range(8):
                for kc in range(8):
                    mm(ps[:, c * NS:(c + 1) * NS], Wz3[:, kc, c * 128:(c + 1) * 128], xbs3[:, kc, :], kc == 0, kc == 7,
                       [Wz.tk[c // 2]] + xbs.tk, [tp])
            act(zss.ap, ps[:, 0:8 * NS], AF.Silu, [tp], zss.tk)
            tt("dve", ysm.ap, ysm.ap, zss.ap, ALU.mult, ysm.tk + zss.tk, ysm.tk)
            Ys = scr.alloc(8 * NS, BF16)
            Ys3 = Ys.v("p (c s) -> p c s", c=8)
            layernorm(lnt, NS, lambda c: ysm3[:, c, :], lambda c: ysm.tk, False, lambda c: [Ys3[:, c, :]], lambda c: Ys.tk,
                      lambda c: pkc(O_RMSG + c), None, AF.Copy, rms=True)
            ps, tp = psum()
            for m in range(8):
                for kc in range(8):
                    mm(ps[:, m * NS:(m + 1) * NS], WoY3[:, kc, m * 128:(m + 1) * 128], Ys3[:, kc, :], kc == 0, kc == 7,
                       [WoY.tk[m // 2]] + Ys.tk, [tp])
            tt("dve", XFs, XFs, ps[:, 0:8 * NS].rearrange("p (m s) -> p m s", m=8), ALU.add, tkXs + [tp], tkXs)
            layernorm(lnt, NS, lambda c: xf(SB, c), lambda c: xtk(SB, c), False, lambda c: [xf(SB, c)], lambda c: xtk(SB, c),
                      lambda c: pkc(O_MIXG + c), lambda c: pkc(O_MIXB + c), AF.Identity)

            chk(l, 4)
            scr.release(0)
            Wq = scr.alloc(8 * 1024, BF16, ntk=4); Wo = scr.alloc(8 * 1024, BF16, ntk=4)
            Wq3 = Wq.v("p (k m) -> p k m", k=8); Wo3 = Wo.v("p (k m) -> p k m", k=8)
            KT = scr.alloc(8 * 256, BF16); Vb = scr.alloc(2 * 1024, BF16)
            KT3 = KT.v("p (c m) -> p c m", c=8); Vb3 = Vb.v("p (a e) -> p a e", a=2)
            mC = scr.mark()
            Wk = scr.alloc(8 * 1024, BF16, ntk=4); Wv = scr.alloc(8 * 1024, BF16, ntk=4)
            Wk3 = Wk.v("p (k m) -> p k m", k=8); Wv3 = Wv.v("p (k m) -> p k m", k=8)
            memb = scr.alloc(8 * 256, BF16)
            memb3 = memb.v("p (c m) -> p c m", c=8)
            kf = [scr.alloc(256, F32) for _ in range(2)]
            vf = [scr.alloc(512, F32) for _ in range(2)]
            import os
            if os.environ.get("KVSKIP"):
                S.mute = True
            load_w(Wk3, wk[l], Wk.tk, 256)
            load_w(Wv3, wv[l], Wv.tk, 256)
            load_w(Wq3, wq[l], Wq.tk, 256)
            load_w(Wo3, wo[l], Wo.tk, 256)
            for hh in range(4):
                dma("pool", memb3[:, 2 * hh:2 * hh + 2, :], memT[:, 2 * hh:2 * hh + 2, :], w=memb.tk)
            chk(l, 4, -2)
            for c in range(8):
                ps, tp = psum()
                for kc in range(8):
                    mm(ps[:, 0:256], Wk3[:, kc, c * 128:(c + 1) * 128], memb3[:, kc, :], kc == 0, kc == 7,
                       [Wk.tk[c // 2]] + memb.tk, [tp])
                cp("act", KT3[:, c, :], ps[:, 0:256], [tp], KT.tk)
                cp("dve", kf[c % 2].ap, ps[:, 0:256], [tp], kf[c % 2].tk)
                dma("sp", kP[l][:, c, :], kf[c % 2].ap, r=kf[c % 2].tk)
            chk(l, 4, -1)
            vi = 0
            for mc in range(2):
                for hf in range(2):
                    ps, tp = psum()
                    for kc in range(8):
                        mm(ps[:, :], memb3[:, kc, mc * 128:(mc + 1) * 128], Wv3[:, kc, hf * 512:(hf + 1) * 512], kc == 0, kc == 7,
                           memb.tk + [Wv.tk[2 * hf], Wv.tk[2 * hf + 1]], [tp])
                    import os
                    V_ = os.environ.get("VDBG", "abc")
                    if "a" in V_:
                        cp("act", Vb3[:, mc, hf * 512:(hf + 1) * 512], ps[:, :], [tp], Vb.tk)
                    if "b" in V_:
                        cp("dve", vf[vi].ap, ps[:, :], [tp], vf[vi].tk)
                    if "c" in V_:
                        dma("sp", vP[l][mc * 128:(mc + 1) * 128, hf * 512:(hf + 1) * 512], vf[vi].ap, r=vf[vi].tk)
                    vi = (vi + 1) % 2
            chk(l, 4, 1)
            scr.release(mC)
            NCB = 2
            CSETS = []
            for _k in range(NCB):
                _d = {}
                _d["xb"] = scr.alloc(8 * 128, BF16)
                _d["qT"] = scr.alloc(8 * 128, BF16)
                for _n in ("mx", "nb", "rsum", "rinv"):
                    _d[_n] = scr.alloc(4, F32)
                _d["Pm"] = scr.alloc(4 * 256, BF16)
                _d["PT"] = scr.alloc(8 * 128, BF16)
                _d["Ob"] = scr.alloc(1024, BF16)
                _d["OT"] = scr.alloc(8 * 128, BF16)
                _d["lnt"] = LNT(128)
                CSETS.append(_d)
            NKB = 3
            kb = [scr.alloc(8 * 256, BF16) for _ in range(NKB)]
            vb = [scr.alloc(2 * 1024, BF16) for _ in range(NKB)]
            Qm = scr.alloc(8 * NS * NS, BF16)
            PTm = scr.alloc(8 * NS * NS, BF16)
            SCALE = 256.0 ** -0.5
            _d = CSETS[0]
            xb, qT, mx, nb, rsum, rinv, Pm, PT, Ob, OT, lnt = (_d[k] for k in
                                                                  ("xb", "qT", "mx", "nb", "rsum", "rinv", "Pm", "PT", "Ob", "OT", "lnt"))
            xb3 = xb.v("p (k t) -> p k t", k=8); qT3 = qT.v("p (c t) -> p c t", c=8)
            Pm3 = Pm.v("p (h m) -> p h m", h=4); PT4 = PT.v("p (h a t) -> p h a t", h=4, a=2)
            OT3 = OT.v("p (c t) -> p c t", c=8)

            def softmax_pv(Q, sbanks, blk):
                for h in range(4):
                    red(mx.ap[0:Q, h:h + 1], sbanks[h][0], ALU.max, [sbanks[h][1]], mx.tk)
                ts("dve", nb.ap[0:Q, :], mx.ap[0:Q, :], -SCALE, None, ALU.mult, ALU.bypass, mx.tk, nb.tk)
                for h in range(4):
                    act(Pm3[0:Q, h, :], sbanks[h][0], AF.Exp, [sbanks[h][1]] + nb.tk, Pm.tk + rsum.tk,
                        bias=nb.ap[0:Q, h:h + 1], scale=SCALE, accum_out=rsum.ap[0:Q, h:h + 1])
                recip(rinv.ap[0:Q, :], rsum.ap[0:Q, :], rsum.tk, rinv.tk)
                ps, tp = psum()
                pbf = ps[:, :].bitcast(BF16)
                for h in range(4):
                    for a in range(2):
                        tr(pbf[:, (h * 2 + a) * 128:(h * 2 + a) * 128 + Q], Pm3[0:Q, h, a * 128:(a + 1) * 128], identb[0:Q, 0:Q],
                           Pm.tk + [tkC], [tp])
                cp("act", PT4[:, :, :, 0:Q], pbf[:, 0:1024].rearrange("p (h a t) -> p h a t", h=4, a=2)[:, :, :, 0:Q], [tp], PT.tk)

            xbs = Buf(XBS_t[:], tkXBS)
            xbs3 = XBS3
            cp("pool", xbs3, XFs, tkXs, xbs.tk)
            qs = scr.alloc(8 * NS, BF16)
            qs3 = qs.v("p (c s) -> p c s", c=8)
            ps, tp = psum()
            for m in range(8):
                for kc in range(8):
                    mm(ps[:, m * NS:(m + 1) * NS], Wq3[:, kc, m * 128:(m + 1) * 128], xbs3[:, kc, :], kc == 0, kc == 7,
                       [Wq.tk[m // 2]] + xbs.tk, [tp])
            cp("act", qs.ap, ps[:, 0:8 * NS], [tp], qs.tk)
            Qm4 = Qm.v("p (c s q) -> p c s q", c=8, s=NS)
            tt("pool", Qm4, qs3.unsqueeze(3).to_broadcast([128, 8, NS, NS]),
               id16b.unsqueeze(1).to_broadcast([128, 8, NS, NS]), ALU.mult, qs.tk + [tkC], Qm.tk)
            sbk = [psum() for _ in range(4)]
            for s in range(NS):
                kbs = kb[s % NKB]
                kbs3 = kbs.v("p (c m) -> p c m", c=8)
                for hh in range(2):
                    dma("pool", kbs3[:, 4 * hh:4 * hh + 4, :], kcT[l, s][:, 4 * hh:4 * hh + 4, :], w=kbs.tk)
                for h in range(4):
                    for dc in range(2):
                        mm(sbk[h][0][0:NS, 0:256], Qm4[:, 2 * h + dc, s, :], kbs3[:, 2 * h + dc, :], s == 0 and dc == 0,
                           s == NS - 1 and dc == 1, Qm.tk + kbs.tk, [sbk[h][1]])
            softmax_pv(NS, [(sbk[h][0][0:NS, 0:256], sbk[h][1]) for h in range(4)], SB)
            PTm5 = PTm.v("p (h a s q) -> p h a s q", h=4, a=2, s=NS)
            for h in range(4):
                tt("pool", PTm5[:, h], PT4[:, h, :, 0:NS].unsqueeze(3).to_broadcast([128, 2, NS, NS]),
                   id16b.unsqueeze(1).to_broadcast([128, 2, NS, NS]), ALU.mult, PT.tk + [tkC], PTm.tk)
            obk = [psum() for _ in range(4)]
            for s in range(NS):
                vbs = vb[s % NKB]
                vbs3 = vbs.v("p (a e) -> p a e", a=2)
                dma("pool", vbs3, vc[l, s].rearrange("(a p) e -> p a e", p=128), w=vbs.tk)
                for h in range(4):
                    for a in range(2):
                        mm(obk[h][0][0:NS, 0:256], PTm5[:, h, a, s, :], vbs3[:, a, h * 256:(h + 1) * 256], s == 0 and a == 0,
                           s == NS - 1 and a == 1, PTm.tk + vbs.tk, [obk[h][1]])
            for h in range(4):
                ts("dve", Ob.ap[0:NS, h * 256:(h + 1) * 256], obk[h][0][0:NS, 0:256], rinv.ap[0:NS, h:h + 1], None,
                   ALU.mult, ALU.bypass, [obk[h][1]] + rinv.tk, Ob.tk)
            ps, tp = psum()
            pbf = ps[:, :].bitcast(BF16)
            for c in range(8):
                tr(pbf[:, c * 128:c * 128 + NS], Ob.ap[0:NS, c * 128:(c + 1) * 128], identb[0:NS, 0:NS], Ob.tk + [tkC], [tp])
            cp("act", OT3[:, :, 0:NS], pbf[:, 0:1024].rearrange("p (c t) -> p c t", c=8)[:, :, 0:NS], [tp], OT.tk)
            ps, tp = psum()
            for m in range(8):
                for kc in range(8):
                    mm(ps[:, m * NS:(m + 1) * NS], Wo3[:, kc, m * 128:(m + 1) * 128], OT3[:, kc, 0:NS], kc == 0, kc == 7,
                       [Wo.tk[m // 2]] + OT.tk, [tp])
            stt(XFs, XFs, ALPHA, ps[:, 0:8 * NS].rearrange("p (m s) -> p m s", m=8), ALU.mult, ALU.add, tkXs + [tp], tkXs)
            layernorm(lnt, NS, lambda c: xf(SB, c), lambda c: xtk(SB, c), False, lambda c: [xf(SB, c)], lambda c: xtk(SB, c),
                      lambda c: pkc(O_XAG + c), lambda c: pkc(O_XAB + c), AF.Identity)

            chk(l, 5)
            for j in range(16):
                t0 = j * 128
                blk = ("P", t0, 128)
                b5 = t0 // 512
                _d = CSETS[j % NCB]
                xb, qT, mx, nb, rsum, rinv, Pm, PT, Ob, OT, lnt = (_d[k] for k in
                                                                      ("xb", "qT", "mx", "nb", "rsum", "rinv", "Pm", "PT", "Ob", "OT", "lnt"))
                xb3 = xb.v("p (k t) -> p k t", k=8); qT3 = qT.v("p (c t) -> p c t", c=8)
                Pm3 = Pm.v("p (h m) -> p h m", h=4); PT4 = PT.v("p (h a t) -> p h a t", h=4, a=2)
                OT3 = OT.v("p (c t) -> p c t", c=8)
                import os
                Q_ = os.environ.get("QDBG", "abc")
                if "a" in Q_:
                    if os.environ.get("XBSRC") == "cst":
                        cp("pool", xb.ap, cstf[:, 0:1024], [tkC], xb.tk)
                    elif os.environ.get("XBSRC") == "2d":
                        for c in range(8):
                            cp("pool", xb3[:, c, :], XFp[:, c, t0:t0 + 128], [tkXp[c][j]], xb.tk)
                    else:
                        cp(os.environ.get("XBENG", "pool"), xb3, XFp[:, :, t0:t0 + 128], [tkXp[c][j] for c in range(8)], xb.tk)
                for m4 in range(2):
                    ps, tp = psum()
                    for mm_ in range(4):
                        m = m4 * 4 + mm_
                        for kc in range(8):
                            if "b" in Q_:
                                mm(ps[:, mm_ * 128:(mm_ + 1) * 128], Wq3[:, kc, m * 128:(m + 1) * 128], xb3[:, kc, :], kc == 0, kc == 7,
                                   [Wq.tk[m // 2]] + xb.tk, [tp])
                    if "c" in Q_:
                        cp("act", qT3[:, m4 * 4:m4 * 4 + 4, :], ps[:, :].rearrange("p (m t) -> p m t", m=4), [tp], qT.tk)
                chk(l, 4, 2)
                sb_ = []
                for hp in range(2):
                    ps, tp = psum()
                    for hh in range(2):
                        h = hp * 2 + hh
                        for dc in range(2):
                            mm(ps[:, hh * 256:(hh + 1) * 256], qT3[:, 2 * h + dc, :], KT3[:, 2 * h + dc, :], dc == 0, dc == 1,
                               qT.tk + KT.tk, [tp])
                    sb_.append((ps[:, 0:256], tp))
                    sb_.append((ps[:, 256:512], tp))
                chk(l, 4, 3)
                softmax_pv(128, sb_, blk)
                chk(l, 4, 4)
                for hp in range(2):
                    ps, tp = psum()
                    for hh in range(2):
                        h = hp * 2 + hh
                        for a in range(2):
                            mm(ps[:, hh * 256:(hh + 1) * 256], PT4[:, h, a, :], Vb3[:, a, h * 256:(h + 1) * 256], a == 0, a == 1,
                               PT.tk + Vb.tk, [tp])
                    tt("dve", Ob.ap[:, hp * 512:(hp + 1) * 512].rearrange("p (h d) -> p h d", h=2),
                       ps[:, :].rearrange("p (h d) -> p h d", h=2),
                       rinv.ap[:, hp * 2:hp * 2 + 2].unsqueeze(2).to_broadcast([128, 2, 256]), ALU.mult, [tp] + rinv.tk, Ob.tk)
                chk(l, 4, 5)
                ps, tp = psum()
                pbf = ps[:, :].bitcast(BF16)
                for c in range(8):
                    tr(pbf[:, c * 128:(c + 1) * 128], Ob.ap[:, c * 128:(c + 1) * 128], identb, Ob.tk + [tkC], [tp])
                cp("act", OT.ap, pbf[:, 0:1024], [tp], OT.tk)
                chk(l, 4, 6)
                for m4 in range(2):
                    ps, tp = psum()
                    for mm_ in range(4):
                        m = m4 * 4 + mm_
                        for kc in range(8):
                            mm(ps[:, mm_ * 128:(mm_ + 1) * 128], Wo3[:, kc, m * 128:(m + 1) * 128], OT3[:, kc, :], kc == 0, kc == 7,
                               [Wo.tk[m // 2]] + OT.tk, [tp])
                    xtks = [tkXp[m4 * 4 + q][j] for q in range(4)]
                    stt(XFp[:, m4 * 4:m4 * 4 + 4, t0:t0 + 128], XFp[:, m4 * 4:m4 * 4 + 4, t0:t0 + 128], ALPHA,
                        ps[:, :].rearrange("p (m t) -> p m t", m=4), ALU.mult, ALU.add, xtks + [tp], xtks)
                layernorm(lnt, 128, lambda c: xf(blk, c), lambda c: xtk(blk, c), False,
                          lambda c: [xf(blk, c)], lambda c: xtk(blk, c),
                          lambda c: pkc(O_XAG + c), lambda c: pkc(O_XAB + c), AF.Identity)

            chk(l, 6)
            WD = 256
            for gi in range(2):
                scr.release(0)
                Wup = scr.alloc(8 * 2 * 1408, BF16, ntk=22)
                Wup4 = Wup.v("p (k h m) -> p k h m", k=8, h=2)
                Wdn = scr.alloc(11 * 1024, BF16, ntk=8)
                Wdn3 = Wdn.v("p (k m) -> p k m", k=11)
                for hh in range(2):
                    for jj in range(11):
                        c0 = hh * 2816 + gi * 1408 + jj * 128
                        dma("pool", Wup4[:, :, hh, jj * 128:(jj + 1) * 128], w_up[l][:, :, c0:c0 + 128], w=[Wup.tk[hh * 11 + jj]])
                for m in range(8):
                    dma("pool", Wdn3[:, :, m * 128:(m + 1) * 128], w_dn[l][:, gi * 11:(gi + 1) * 11, m * 128:(m + 1) * 128],
                        w=[Wdn.tk[m]])
                dg3 = scr.alloc(22 * 3 * 128, BF16, ntk=22)
                dg34 = dg3.v("p (u k m) -> p u k m", u=22, k=3)

                def chid(u):
                    return (u // 11) * 22 + gi * 11 + (u % 11)

                for u in range(22):
                    tt("pool", dg34[:, u], identb.unsqueeze(1).to_broadcast([128, 3, 128]),
                       pkc(O_FFW + chid(u) * 3, 3).unsqueeze(2).to_broadcast([128, 3, 128]), ALU.mult, [tkC, tkPk], [dg3.tk[u]])
                xb = scr.alloc(8 * WD, BF16)
                xb3 = xb.v("p (k t) -> p k t", k=8)
                ur = scr.alloc(22 * (WD + 2), BF16, ntk=22)
                ur3 = ur.v("p (u t) -> p u t", u=22)
                gt = scr.alloc(11 * WD, BF16, ntk=11)
                gt3 = gt.v("p (u t) -> p u t", u=11)
                sgf = [scr.alloc(WD, F32) for _ in range(2)]
                urt = scr.alloc(44, F32)
                lnt = LNT(WD)
                xbs = Buf(XBS_t[:], tkXBS)
                xbs3 = XBS3
                if gi == 0:
                    cp("pool", xbs3, XFs, tkXs, xbs.tk)
                urs = scr.alloc(22 * NS, F32)
                urs3 = urs.v("p (u s) -> p u s", u=22)
                h3 = scr.alloc(22 * 3 * NS, BF16, ntk=2)
                h34 = h3.v("p (u k s) -> p u k s", u=22, k=3)
                ps, tp = psum()
                for u in range(22):
                    for kc in range(8):
                        mm(ps[:, u * NS:(u + 1) * NS], Wup4[:, kc, u // 11, (u % 11) * 128:(u % 11 + 1) * 128], xbs3[:, kc, :],
                           kc == 0, kc == 7, [Wup.tk[u]] + xbs.tk, [tp])
                cp("act", urs.ap, ps[:, 0:22 * NS], [tp], urs.tk)
                for hh in range(2):
                    c0 = hh * 22 + gi * 11
                    dma("sp", ffncS[l][:, c0:c0 + 11, 1, :], urs3[:, hh * 11:(hh + 1) * 11, :], r=urs.tk)
                    dma("sp", ffncS[l][:, c0:c0 + 11, 0, :], ffncT[l][:, c0:c0 + 11, 1, :])
                    dma("pool", h34[:, hh * 11:(hh + 1) * 11, 0:2, :], ffncT[l][:, c0:c0 + 11], w=[h3.tk[0]])
                cp("pool", h34[:, :, 2, :], urs3, urs.tk, [h3.tk[1]])
                pv, tv = psum()
                for u in range(22):
                    for k in range(3):
                        mm(pv[:, u * NS:(u + 1) * NS], dg34[:, u, k, :], h34[:, u, k, :], k == 0, k == 2, h3.tk + [dg3.tk[u]], [tv])
                sgs = scr.alloc(11 * NS, F32)
                gts = scr.alloc(11 * NS, BF16)
                gts3 = gts.v("p (u s) -> p u s", u=11)
                for jj in range(11):
                    act(sgs.ap[:, jj * NS:(jj + 1) * NS], pv[:, (11 + jj) * NS:(12 + jj) * NS], AF.Silu, [tv, tkPk], sgs.tk,
                        bias=pkc(O_FFBIAS + chid(11 + jj)))
                for jj in range(11):
                    stt(gts3[:, jj, :], pv[:, jj * NS:(jj + 1) * NS], pkc(O_FFBIAS + chid(jj)), sgs.ap[:, jj * NS:(jj + 1) * NS],
                        ALU.add, ALU.mult, [tv, tkPk] + sgs.tk, gts.tk)
                ps, tp = psum()
                for m in range(8):
                    for jj in range(11):
                        mm(ps[:, m * NS:(m + 1) * NS], Wdn3[:, jj, m * 128:(m + 1) * 128], gts3[:, jj, :], jj == 0, jj == 10,
                           [Wdn.tk[m]] + gts.tk, [tp])
                psv3 = ps[:, 0:8 * NS].rearrange("p (m s) -> p m s", m=8)
                if gi == 0:
                    stt(XFs, XFs, ALPHA, psv3, ALU.mult, ALU.add, tkXs + [tp], tkXs)
                else:
                    tt("dve", XFs, XFs, psv3, ALU.add, tkXs + [tp], tkXs)
                    layernorm(lnt, NS, lambda c: xf(SB, c), lambda c: xtk(SB, c), False, lambda c: [xf(SB, c)],
                              lambda c: xtk(SB, c), lambda c: pkc(O_FFG + c), lambda c: pkc(O_FFB + c), AF.Identity)

                for bi in range(T // WD):
                    t0 = bi * WD
                    blk = ("P", t0, WD)
                    b5 = t0 // 512
                    if gi == 0:
                        cp("pool", xb3, XFp[:, :, t0:t0 + WD], [t_ for c in range(8) for t_ in XT(c, t0, WD)], xb.tk)
                        for a in range(2):
                            dma("sp", xbd[:, 2 * bi + a], xb3[:, :, a * 128:(a + 1) * 128], r=xb.tk, w=[tkXbd[2 * bi + a]])
                    else:
                        for a in range(2):
                            dma("sp", xb3[:, :, a * 128:(a + 1) * 128], xbd[:, 2 * bi + a], r=[tkXbd[2 * bi + a]], w=xb.tk)
                    if bi == 0:
                        mset("pool", ur3[:, :, 0:2], 0.0, ur.tk)
                    else:
                        cp("pool", ur3[:, :, 0:2], ur3[:, :, WD:WD + 2], ur.tk, ur.tk)
                    for u in range(22):
                        ps, tp = psum()
                        for kc in range(8):
                            mm(ps[:, 0:WD], Wup4[:, kc, u // 11, (u % 11) * 128:(u % 11 + 1) * 128], xb3[:, kc, :], kc == 0, kc == 7,
                               [Wup.tk[u]] + xb.tk, [tp])
                        cp("act" if u % 2 else "dve", ur3[:, u, 2:WD + 2], ps[:, 0:WD], [tp], [ur.tk[u]])
                    if bi == T // WD - 1:
                        cp("act", urt.ap.rearrange("p (u k) -> p u k", u=22), ur3[:, :, WD:WD + 2], ur.tk, urt.tk)
                        urt3 = urt.ap.rearrange("p (u k) -> p u k", u=22)
                        dma("sp", ffncP[l][:, gi * 11:(gi + 1) * 11, :], urt3[:, 0:11, :], r=urt.tk)
                        dma("sp", ffncP[l][:, 22 + gi * 11:22 + (gi + 1) * 11, :], urt3[:, 11:22, :], r=urt.tk)
                    for jj in range(11):
                        pv, tv = psum()
                        pg, tg = psum()
                        for k in range(3):
                            mm(pv[:, 0:WD], dg34[:, jj, k, :], ur3[:, jj, k:k + WD], k == 0, k == 2, [ur.tk[jj], dg3.tk[jj]], [tv])
                        for k in range(3):
                            mm(pg[:, 0:WD], dg34[:, 11 + jj, k, :], ur3[:, 11 + jj, k:k + WD], k == 0, k == 2,
                               [ur.tk[11 + jj], dg3.tk[11 + jj]], [tg])
                        s_ = sgf[jj % 2]
                        act(s_.ap, pg[:, 0:WD], AF.Silu, [tg, tkPk], s_.tk, bias=pkc(O_FFBIAS + chid(11 + jj)))
                        stt(gt3[:, jj, :], pv[:, 0:WD], pkc(O_FFBIAS + chid(jj)), s_.ap, ALU.add, ALU.mult,
                            [tv, tkPk] + s_.tk, [gt.tk[jj]])
                    for m in range(8):
                        ps, tp = psum()
                        for jj in range(11):
                            mm(ps[:, 0:WD], Wdn3[:, jj, m * 128:(m + 1) * 128], gt3[:, jj, :], jj == 0, jj == 10,
                               [Wdn.tk[m], gt.tk[jj]], [tp])
                        if gi == 0:
                            stt(xf(blk, m), xf(blk, m), ALPHA, ps[:, 0:WD], ALU.mult, ALU.add, xtk(blk, m) + [tp], xtk(blk, m))
                        else:
                            tt("dve", xf(blk, m), xf(blk, m), ps[:, 0:WD], ALU.add, xtk(blk, m) + [tp], xtk(blk, m))
                    if gi == 1:
                        layernorm(lnt, WD, lambda c: xf(blk, c), lambda c: xtk(blk, c), False,
                                  lambda c: [xf(blk, c)], lambda c: xtk(blk, c),
                                  lambda c: pkc(O_FFG + c), lambda c: pkc(O_FFB + c), AF.Identity)
          except _Stop:
            break
        for c in range(8):
            for b in range(4):
                dma("sp", yT[:, c, b * 512:(b + 1) * 512], XFp[:, c, b * 512:(b + 1) * 512], r=XT(c, b * 512, 512))
        dma("sp", ysT, XFs, r=tkXs)
        S.emit(nc, st)
    return nc


def _wl(w):
    Lw, K, M = w.shape
    return np.ascontiguousarray(w.reshape(Lw, K // 128, 128, M).transpose(0, 2, 1, 3))


def _colT(v, nch):
    return v.reshape(v.shape[0], nch, 128).transpose(0, 2, 1)


def _build_pack(inp):
    pk = np.zeros((L, 128, NPK), np.float32)
    cw = inp["conf_conv_w"]
    pk[:, :, O_CONFW:O_CONFW + 248] = cw.reshape(L, 31, 8, 128).transpose(0, 3, 2, 1).reshape(L, 128, 248)
    pk[:, :, O_CONFB:O_CONFB + 8] = _colT(inp["conf_conv_b"], 8)
    pk[:, :, O_CLNG:O_CLNG + 8] = _colT(inp["conf_ln_g"], 8)
    pk[:, :, O_CLNB:O_CLNB + 8] = _colT(inp["conf_ln_b"], 8)
    sw = inp["ssd_conv_w"]
    pk[:, :, O_SSDW:O_SSDW + 48] = sw.reshape(L, 4, 12, 128).transpose(0, 3, 2, 1).reshape(L, 128, 48)
    pk[:, :, O_SSDB:O_SSDB + 12] = _colT(inp["ssd_conv_b"], 12)
    pk[:, :, O_RMSG:O_RMSG + 8] = _colT(inp["ssd_norm_g"], 8)
    for off, nm in ((O_MIXG, "ln_mix_g"), (O_MIXB, "ln_mix_b"), (O_XAG, "ln_xa_g"), (O_XAB, "ln_xa_b"),
                    (O_FFG, "ln_ffn_g"), (O_FFB, "ln_ffn_b")):
        pk[:, :, off:off + 8] = _colT(inp[nm], 8)
    fw = inp["ffn_conv_w"]
    pk[:, :, O_FFW:O_FFW + 132] = fw.reshape(L, 3, 44, 128).transpose(0, 3, 2, 1).reshape(L, 128, 132)
    pk[:, :, O_FFBIAS:O_FFBIAS + 44] = _colT(inp["ffn_conv_b"], 44)
    pk[:, 0:16, O_DTB] = inp["ssd_dt_bias"]
    pk[:, 0:16, O_ALOG] = inp["ssd_a_log"]
    pk[:, :, O_DROW:O_DROW + 16] = inp["ssd_d"][:, None, :]
    q = np.arange(128)
    for c in range(8):
        pk[:, :, O_DCOL + c] = inp["ssd_d"][:, 2 * c + q // 64]
    return pk


def _build_cst():
    c = np.zeros((128, NCST), np.float32)
    p = np.arange(128)
    c[:, C_ID:C_ID + 128] = np.eye(128)
    c[:, C_ULE:C_ULE + 128] = (p[:, None] <= p[None, :])
    c[:, C_MST:C_MST + 128] = (p[:, None] > p[None, :])
    c[:, C_ONE:C_ONE + 128] = 1.0
    c[:, C_ID16:C_ID16 + 256] = np.eye(16).reshape(1, 256)
    e = np.zeros((16, 8, 128), np.float32)
    for cc in range(8):
        for q in range(128):
            e[2 * cc + q // 64, cc, q] = 1.0
    c[0:16, C_EHP:C_EHP + 1024] = e.reshape(16, 1024)
    return c


_NC_CACHE = {}
STOP = None


def kernel(**inp):
    inp = {k: np.asarray(v) for k, v in inp.items()}
    f = np.float32
    shared = {
        "w_in": _wl(inp["w_in"]), "w_oA": _wl(inp["w_out"][:, 0:1024, :]), "w_oY": _wl(inp["w_out"][:, 1024:2048, :]),
        "wq": _wl(inp["xa_wq"]), "wk": _wl(inp["xa_wk"]), "wv": _wl(inp["xa_wv"]), "wo": _wl(inp["xa_wo"]),
        "w_up": _wl(inp["ffn_w_up"]), "w_dn": _wl(inp["ffn_w_down"]),
        "pack": _build_pack(inp), "cst": _build_cst(),
    }
    in_maps = []
    for i in range(NCORES):
        sl = slice(NS * i, NS * (i + 1))
        m = dict(shared)
        m["xT"] = np.ascontiguousarray(inp["x_prompt"][i].reshape(T, 8, 128).transpose(2, 1, 0))
        m["xsT"] = np.ascontiguousarray(inp["x_sample"][sl, 0].reshape(NS, 8, 128).transpose(2, 1, 0))
        m["memT"] = np.ascontiguousarray(inp["mem_prompt"][i].reshape(256, 8, 128).transpose(2, 1, 0))
        m["kcT"] = np.ascontiguousarray(inp["cache_mem_k"][:, sl].reshape(L, NS, 256, 8, 128).transpose(0, 1, 4, 3, 2))
        m["vc"] = np.ascontiguousarray(inp["cache_mem_v"][:, sl].reshape(L, NS, 256, 1024))
        m["confT"] = np.ascontiguousarray(inp["state_conf_conv"][:, sl].reshape(L, NS, 30, 8, 128).transpose(0, 4, 3, 2, 1))
        m["ssdcT"] = np.ascontiguousarray(inp["state_ssd_conv"][:, sl].reshape(L, NS, 3, 12, 128).transpose(0, 4, 3, 2, 1))
        m["ffncT"] = np.ascontiguousarray(inp["state_ffn_conv"][:, sl].reshape(L, NS, 2, 44, 128).transpose(0, 4, 3, 2, 1))
        m["ssdst"] = np.ascontiguousarray(inp["state_ssd"][:, sl].reshape(L, NS, 1024, 128))
        in_maps.append(m)
    if "nc" not in _NC_CACHE:
        _NC_CACHE["nc"] = build_program()
    nc = _NC_CACHE["nc"]
    res = run_bass_kernel_spmd(nc, in_maps, core_ids=list(range(NCORES)))
    R = res.results
    B = NCORES
    y_prompt = np.stack([R[i]["yT"].transpose(2, 1, 0).reshape(T, 1024) for i in range(B)]).astype(f)
    y_sample = np.concatenate([R[i]["ysT"].transpose(2, 1, 0).reshape(NS, 1, 1024) for i in range(B)]).astype(f)
    confP = np.stack([R[i]["confP"].transpose(0, 3, 2, 1).reshape(L, 30, 1024) for i in range(B)], axis=1)
    ssdcP = np.stack([R[i]["ssdcP"].transpose(0, 3, 2, 1).reshape(L, 3, 1536) for i in range(B)], axis=1)
    ssdP = np.stack([R[i]["ssdPT"].transpose(0, 2, 1).reshape(L, 16, 64, 128) for i in range(B)], axis=1)
    ffncP = np.stack([R[i]["ffncP"].transpose(0, 3, 2, 1).reshape(L, 2, NFF) for i in range(B)], axis=1)
    kPo = np.stack([R[i]["kP"].transpose(0, 3, 2, 1).reshape(L, 256, 4, 256) for i in range(B)], axis=1)
    vPo = np.stack([R[i]["vP"].reshape(L, 256, 4, 256) for i in range(B)], axis=1)
    confS = np.concatenate([R[i]["confS"].transpose(0, 4, 3, 2, 1).reshape(L, NS, 30, 1024) for i in range(B)], axis=1)
    ssdcS = np.concatenate([R[i]["ssdcS"].transpose(0, 4, 3, 2, 1).reshape(L, NS, 3, 1536) for i in range(B)], axis=1)
    ssdS = np.concatenate([R[i]["ssdS"].reshape(L, NS, 16, 64, 128) for i in range(B)], axis=1)
    ffncS = np.concatenate([R[i]["ffncS"].transpose(0, 4, 3, 2, 1).reshape(L, NS, 2, NFF) for i in range(B)], axis=1)
    outs = (y_prompt, y_sample, confP, ssdcP, ssdP, ffncP, kPo, vPo, confS, ssdcS, ssdS, ffncS)
    return tuple(np.ascontiguousarray(o, dtype=f) for o in outs)
```

```python
from contextlib import ExitStack
import numpy as np
import concourse.bass as bass
import concourse.mybir as mybir
from concourse.bass_utils import run_bass_kernel_spmd

F32 = mybir.dt.float32
BF16 = mybir.dt.bfloat16
AF = mybir.ActivationFunctionType
ALU = mybir.AluOpType
AX = mybir.AxisListType

NCORES = 8
L = 4
T = 2048
NS = 16
ALPHA = (2.0 * L) ** 0.25
EPS = 1e-5
D_IN = 4624
NFF = 5632

O_CONFW, O_CONFB, O_CLNG, O_CLNB = 0, 248, 256, 264
O_SSDW, O_SSDB, O_RMSG = 272, 320, 332
O_MIXG, O_MIXB, O_XAG, O_XAB, O_FFG, O_FFB = 340, 348, 356, 364, 372, 380
O_FFW, O_FFBIAS = 388, 520
O_DTB, O_ALOG, O_DROW, O_DCOL = 564, 565, 566, 582
NPK = 590
C_ID, C_ULE, C_MST, C_ONE, C_ID16, C_EHP = 0, 128, 256, 384, 512, 768
NCST = 768 + 1024

ENGS = ["pe", "act", "dve", "pool", "sp"]
ENGMAP = {"pe": "tensor", "act": "scalar", "dve": "vector", "pool": "gpsimd", "sp": "sync"}
NLANES = 32


class Tk:
    __slots__ = ("w", "r", "excl")

    def __init__(self, excl=False):
        self.w = None
        self.r = []
        self.excl = excl


class Sched:
    def __init__(self):
        self.ops = {e: [] for e in ENGS}
        self.lane_rr = 0
        self.lane_rr2 = [0, 0]
        self.lane_last = [None] * NLANES
        self.lane_seq = [0] * NLANES

    def _compress(self, deps):
        best = {}
        out = set()
        for d in deps:
            op = self.ops[d[0]][d[1]]
            if op["dma"]:
                key = ("L", op["lane"])
                if key not in best or op["ticket"] > best[key][0]:
                    best[key] = (op["ticket"], d)
            else:
                key = d[0]
                if key not in best or d[1] > best[key][0]:
                    best[key] = (d[1], d)
        for v in best.values():
            out.add(v[1])
        return out

    mute = False

    def add(self, eng, fn, r=(), w=(), dma=False, dur=100.0, lat=0.0):
        if self.mute:
            return None
        idx = len(self.ops[eng])
        me = (eng, idx)
        deps = set()
        if any(t.excl for t in r):
            w = list(w) + [t for t in r if t.excl]
            r = [t for t in r if not t.excl]
        for t in r:
            if t.w is not None:
                deps.add(t.w)
        for t in w:
            if t.w is not None:
                deps.add(t.w)
            deps.update(t.r)
        deps.discard(me)
        import sys as _sys
        op = dict(fn=fn, deps=None, dma=dma, signal=bool(dma), lane=None, ticket=None, dur=dur, lat=lat,
                  line=_sys._getframe(2).f_lineno)
        if dma:
            half = NLANES // 2
            k = 1 if eng == "pool" else 0
            lane = k * half + self.lane_rr2[k]
            self.lane_rr2[k] = (self.lane_rr2[k] + 1) % half
            op["lane"] = lane
            if self.lane_last[lane] is not None:
                deps.add(self.lane_last[lane])
            self.lane_last[lane] = me
            self.lane_seq[lane] += 1
            op["ticket"] = 16 * self.lane_seq[lane]
        op["raw"] = deps
        self.ops[eng].append(op)
        for t in r:
            t.r.append(me)
        for t in w:
            t.w = me
            t.r = []
        return me

    def schedule(self):
        import heapq
        SYNC = 120.0
        ndeps = {}
        users = {}
        for e in ENGS:
            for i, op in enumerate(self.ops[e]):
                ndeps[(e, i)] = len(op["raw"])
                for d in op["raw"]:
                    users.setdefault(d, []).append((e, i))
        ready = {e: [] for e in ENGS}
        readyt = {}
        for e in ENGS:
            for i, op in enumerate(self.ops[e]):
                if not op["raw"]:
                    heapq.heappush(ready[e], i)
                    readyt[(e, i)] = 0.0
        free = {e: 0.0 for e in ENGS}
        busy = {e: False for e in ENGS}
        order = {e: [] for e in ENGS}
        events = []
        now = 0.0
        remaining = sum(len(self.ops[e]) for e in ENGS)

        def try_issue(e, now):
            if busy[e] or not ready[e]:
                return
            i = heapq.heappop(ready[e])
            op = self.ops[e][i]
            order[e].append(i)
            busy[e] = True
            t_free = now + op["dur"]
            heapq.heappush(events, (t_free, 1, e, -1))
            heapq.heappush(events, (t_free + op["lat"] + SYNC, 0, e, i))

        for e in ENGS:
            try_issue(e, 0.0)
        while events:
            t, kind, e, i = heapq.heappop(events)
            now = t
            if kind == 1:
                busy[e] = False
                try_issue(e, now)
            else:
                remaining -= 1
                for u in users.get((e, i), ()):
                    ndeps[u] -= 1
                    if ndeps[u] == 0:
                        heapq.heappush(ready[u[0]], u[1])
                        try_issue(u[0], now)
        assert remaining == 0, ("scheduler: unscheduled ops (cycle?)", remaining)
        self.est_ns = now
        return order

    def emit(self, nc, stack, reorder=True):
        if reorder:
            order = self.schedule()
        else:
            order = {e: list(range(len(self.ops[e]))) for e in ENGS}
        pos = {}
        for e in ENGS:
            for k, i in enumerate(order[e]):
                pos[(e, i)] = k
        for e in ENGS:
            for i in order[e]:
                op = self.ops[e][i]
                best = {}
                for d in op["raw"]:
                    dop = self.ops[d[0]][d[1]]
                    if dop["dma"]:
                        key = ("L", dop["lane"])
                        val = dop["ticket"]
                    else:
                        if e == "pe" and d[0] == "pe" and not op["dma"]:
                            assert pos[d] < pos[(e, i)]
                            continue
                        key = d[0]
                        val = pos[d]
                        if d[0] == e:
                            assert pos[d] < pos[(e, i)], "same-engine dependency order violated"
                    if key not in best or val > best[key][0]:
                        best[key] = (val, d)
                op["deps"] = [v[1] for v in best.values()]
                for d in op["deps"]:
                    self.ops[d[0]][d[1]]["signal"] = True
        esem = {e: stack.enter_context(nc.semaphore("s_" + e)) for e in ENGS}
        lsem = [stack.enter_context(nc.semaphore("l_%d" % i)) for i in range(NLANES)]
        for e in ENGS:
            cnt = 0
            for i in order[e]:
                op = self.ops[e][i]
                if op["dma"]:
                    continue
                if op["signal"]:
                    cnt += 1
                    op["ticket"] = cnt

        def sem_of(dep):
            dop = self.ops[dep[0]][dep[1]]
            if dop["dma"]:
                return lsem[dop["lane"]], dop["ticket"]
            return esem[dep[0]], dop["ticket"]

        with nc.Block() as block:
            for e in ENGS:
                ops = [self.ops[e][i] for i in order[e]]

                def body(eng, e=e, ops=ops):
                    waited = {}
                    for op in ops:
                        for dep in sorted(op["deps"]):
                            sem, val = sem_of(dep)
                            if waited.get(sem.num, 0) >= val:
                                continue
                            eng.wait_ge(sem, val)
                            waited[sem.num] = val
                        inst = op["fn"](eng)
                        if op["signal"]:
                            if op["dma"]:
                                inst.then_inc(lsem[op["lane"]], 16)
                            else:
                                inst.then_inc(esem[e], 1)
                    if e == "sp":
                        for ln in range(NLANES):
                            last = self.lane_last[ln]
                            if last is not None:
                                sem, val = sem_of(last)
                                if waited.get(sem.num, 0) < val:
                                    eng.wait_ge(sem, val)
                                    waited[sem.num] = val

                getattr(block, ENGMAP[e])(body)


class Buf:
    def __init__(self, ap, tk):
        self.ap = ap
        self.tk = tk

    def v(self, pat, **kw):
        return self.ap.rearrange(pat, **kw)


class Scratch:
    def __init__(self, tensor, nbytes):
        self.t = tensor
        self.nbytes = nbytes
        self.top = 0
        self.hist = []

    def alloc(self, nelem, dtype, ntk=1):
        esz = 4 if dtype == F32 else 2
        size = (nelem * esz + 31) // 32 * 32
        off = self.top
        assert off + size <= self.nbytes, ("scratch overflow", off, size, self.nbytes)
        self.top = off + size
        tks = [Tk() for _ in range(ntk)]
        inherit = []
        keep = []
        for (o, s, ts) in self.hist:
            if o < off + size and off < o + s:
                for t in ts:
                    if t.w is not None:
                        inherit.append(t.w)
                    inherit.extend(t.r)
                if not (off <= o and o + s <= off + size):
                    keep.append((o, s, ts))
            else:
                keep.append((o, s, ts))
        self.hist = keep
        inherit = list(set(inherit))
        for t in tks:
            t.r = list(inherit)
        self.hist.append((off, size, tks))
        ap = self.t[:, off // 2:(off + size) // 2]
        if dtype == F32:
            ap = ap.bitcast(F32)[:, 0:nelem]
        else:
            ap = ap[:, 0:nelem]
        return Buf(ap, tks)

    def mark(self):
        return self.top

    def release(self, m):
        self.top = m


class _Stop(Exception):
    pass


def build_program(stop=None, only=None):
    nc = bass.Bass("TRN2", target_bir_lowering=False)

    def chk(l, ph, sub=0):
        if stop is not None and (l, ph, sub) > tuple(stop) + (0,) * (3 - len(stop)):
            S.mute = False
            raise _Stop()
        S.mute = only is not None and ph not in only
        import os
        if os.environ.get("KVSKIP") == "2" and ph == 4 and sub < 1:
            S.mute = True

    S = Sched()

    def din(name, shape):
        return nc.dram_tensor(name, list(shape), F32, kind="ExternalInput").ap()

    def dout(name, shape):
        return nc.dram_tensor(name, list(shape), F32, kind="ExternalOutput").ap()

    xT = din("xT", [128, 8, T]); xsT = din("xsT", [128, 8, NS]); memT = din("memT", [128, 8, 256])
    kcT = din("kcT", [L, NS, 128, 8, 256]); vc = din("vc", [L, NS, 256, 1024])
    confT = din("confT", [L, 128, 8, 30, NS]); ssdcT = din("ssdcT", [L, 128, 12, 3, NS])
    ffncT = din("ffncT", [L, 128, 44, 2, NS]); ssdst = din("ssdst", [L, NS, 1024, 128])
    w_in = din("w_in", [L, 128, 8, D_IN]); w_oA = din("w_oA", [L, 128, 8, 1024]); w_oY = din("w_oY", [L, 128, 8, 1024])
    wq = din("wq", [L, 128, 8, 1024]); wk = din("wk", [L, 128, 8, 1024]); wv = din("wv", [L, 128, 8, 1024])
    wo = din("wo", [L, 128, 8, 1024]); w_up = din("w_up", [L, 128, 8, NFF]); w_dn = din("w_dn", [L, 128, 22, 1024])
    pack = din("pack", [L, 128, NPK]); cst = din("cst", [128, NCST])

    yT = dout("yT", [128, 8, T]); ysT = dout("ysT", [128, 8, NS])
    confP = dout("confP", [L, 128, 8, 30]); ssdcP = dout("ssdcP", [L, 128, 12, 3])
    ssdPT = dout("ssdPT", [L, 128, 1024]); ffncP = dout("ffncP", [L, 128, 44, 2])
    kP = dout("kP", [L, 128, 8, 256]); vP = dout("vP", [L, 256, 1024])
    confS = dout("confS", [L, 128, 8, 30, NS]); ssdcS = dout("ssdcS", [L, 128, 12, 3, NS])
    ssdS = dout("ssdS", [L, NS, 1024, 128]); ffncS = dout("ffncS", [L, 128, 44, 2, NS])

    with ExitStack() as st:
        def sb(name, shape, dt):
            return st.enter_context(nc.sbuf_tensor(name, shape, dt))

        XFp_t = sb("XFp", [128, 8 * T], F32)
        XFs_t = sb("XFs", [128, 8 * NS], F32)
        XFp = XFp_t[:].rearrange("p (c t) -> p c t", c=8)
        XFs = XFs_t[:].rearrange("p (c t) -> p c t", c=8)
        tkXp = [[Tk() for _ in range(16)] for _ in range(8)]

        def XT(c, t0, W):
            return [tkXp[c][j] for j in range(t0 // 128, (t0 + W - 1) // 128 + 1)]
        tkXs = [Tk() for _ in range(8)]
        cstf = sb("cstf", [128, NCST], F32)
        cstb = sb("cstb", [128, 768], BF16)
        pk = sb("pk", [128, NPK], F32)
        acol = sb("acol", [16, 1], F32)
        tkC, tkPk, tkAcol = Tk(), Tk(), Tk()
        SCRB = 130 * 1024
        scr_t = sb("scr", [128, SCRB // 2], BF16)
        scr = Scratch(scr_t, SCRB)
        xbd = nc.dram_tensor("xbd", [128, 16, 8, 128], BF16).ap()
        tkXbd = [Tk() for _ in range(16)]
        XBS_t = sb("XBS", [128, 8 * NS], BF16)
        XBS3 = XBS_t[:].rearrange("p (k s) -> p k s", k=8)
        tkXBS = [Tk()]
        psb = [st.enter_context(nc.psum_tensor("psb%d" % i, [128, 512], F32)) for i in range(8)]
        tkPS = [Tk(excl=True) for _ in range(8)]
        psrr = [0]

        def psum():
            i = psrr[0]
            psrr[0] = (i + 1) % 8
            return psb[i], tkPS[i]

        identf = cstf[:, C_ID:C_ID + 128]
        ulef = cstf[:, C_ULE:C_ULE + 128]
        onesf = cstf[:, C_ONE:C_ONE + 128]
        ehpf = cstf[0:16, C_EHP:C_EHP + 1024].rearrange("p (c q) -> p c q", c=8)
        identb = cstb[:, 0:128]
        mstb = cstb[:, 256:384]
        onesb = cstb[:, 384:512]
        id16b = cstb[:, 512:768].rearrange("p (a b) -> p a b", a=16)

        def fsz(ap):
            n = 1
            for d in ap.shape[1:]:
                n *= d
            return n

        def mm(out, lhsT, rhs, start, stop, r, w):
            n = fsz(rhs)
            d = 25.0 + max(n, 64) / 2.0
            if rhs.dtype == F32:
                d *= 4
            S.add("pe", lambda e: e.matmul(out, lhsT=lhsT, rhs=rhs, start=start, stop=stop), r=r, w=w, dur=d)

        def tr(out, in_, ident, r, w):
            S.add("pe", lambda e: e.transpose(out=out, in_=in_, identity=ident), r=r, w=w, dur=90.0)

        def act(out, in_, func, r, w, bias=None, scale=None, accum_out=None):
            kw = {}
            if bias is not None:
                kw["bias"] = bias
            if scale is not None:
                kw["scale"] = scale
            if accum_out is not None:
                kw["accum_out"] = accum_out
            S.add("act", lambda e: e.activation(out=out, in_=in_, func=func, **kw), r=r, w=w,
                  dur=230.0 + fsz(out) / 1.2)

        def tt(eng, out, in0, in1, op, r, w):
            S.add(eng, lambda e: e.tensor_tensor(out=out, in0=in0, in1=in1, op=op), r=r, w=w,
                  dur=(80.0 + 1.6 * fsz(out)) if eng == "dve" else (200.0 + 2.2 * fsz(out)))

        def ts(eng, out, in0, s1, s2, op0, op1, r, w):
            S.add(eng, lambda e: e.tensor_scalar(out=out, in0=in0, scalar1=s1, scalar2=s2, op0=op0, op1=op1), r=r, w=w,
                  dur=(80.0 + 1.05 * fsz(out)) if eng == "dve" else (200.0 + 2.0 * fsz(out)))

        def stt(out, in0, scalar, in1, op0, op1, r, w, accum_out=None):
            kw = {}
            if accum_out is not None:
                kw["accum_out"] = accum_out
            S.add("dve", lambda e: e.scalar_tensor_tensor(out=out, in0=in0, scalar=scalar, in1=in1,
                                                          op0=op0, op1=op1, **kw), r=r, w=w, dur=80.0 + 1.1 * fsz(out))

        def cp(eng, out, in_, r, w):
            if eng == "act":
                act(out, in_, AF.Copy, r, w)
            else:
                S.add(eng, lambda e: e.tensor_copy(out=out, in_=in_), r=r, w=w,
                      dur=(80.0 + 1.05 * fsz(out)) if eng == "dve" else (200.0 + 1.7 * fsz(out)))

        def red(out, in_, op, r, w):
            S.add("dve", lambda e: e.tensor_reduce(out=out, in_=in_, axis=AX.X, op=op), r=r, w=w, dur=80.0 + 1.05 * fsz(in_))

        def recip(out, in_, r, w):
            S.add("dve", lambda e: e.reciprocal(out=out, in_=in_), r=r, w=w, dur=80.0 + 8.4 * fsz(out))

        def scan(out, d0, d1, r, w):
            S.add("dve", lambda e: e.tensor_tensor_scan(out=out, data0=d0, data1=d1, initial=0.0,
                                                          op0=ALU.mult, op1=ALU.add), r=r, w=w, dur=80.0 + 2.1 * fsz(out))

        def mset(eng, ap, val, w):
            S.add(eng, lambda e: e.memset(ap, val), w=w, dur=100.0 + fsz(ap))

        swq = []
        SWLIM = 600

        def dma(q, out, in_, r=(), w=()):
            r = list(r)
            if q == "pool":
                def nd(ap):
                    n = ap.shape[0]
                    for d in ap.shape[1:-1]:
                        n *= d
                    return max(1, n // 16)
                n = max(nd(out), nd(in_))
                while swq and sum(x[1] for x in swq) + n > SWLIM:
                    old = swq.pop(0)
                    t = Tk()
                    t.w = old[0]
                    r.append(t)
                nbytes = fsz(out) * out.shape[0] * (4 if out.dtype == F32 else 2)
                me = S.add(q, lambda e: e.dma_start(out=out, in_=in_), r=r, w=w, dma=True, dur=1200.0,
                           lat=2000.0 + nbytes / 60.0)
                if me is not None:
                    swq.append((me, n))
                return
            nbytes = fsz(out) * out.shape[0] * (4 if out.dtype == F32 else 2)
            S.add(q, lambda e: e.dma_start(out=out, in_=in_), r=r, w=w, dma=True, dur=100.0,
                  lat=2000.0 + nbytes / 60.0)

        dma("sp", cstf[:], cst, w=[tkC])
        for c in range(8):
            for b in range(4):
                dma("sp", XFp[:, c, b * 512:(b + 1) * 512], xT[:, c, b * 512:(b + 1) * 512], w=XT(c, b * 512, 512))
        dma("sp", XFs, xsT, w=tkXs)
        cp("pool", cstb[:, 0:384], cstf[:, 0:384], [tkC], [tkC])
        cp("pool", cstb[:, 512:768], cstf[:, C_ID16:C_ID16 + 256], [tkC], [tkC])
        mset("pool", cstb[:, 384:512], 1.0 / 1024.0, [tkC])

        def xf(blk, c):
            if blk[0] == "S":
                return XFs[:, c, :]
            return XFp[:, c, blk[1]:blk[1] + blk[2]]

        def xtk(blk, c):
            if blk[0] == "S":
                return [tkXs[c]]
            return XT(c, blk[1], blk[2])

        def bw(blk):
            return NS if blk[0] == "S" else blk[2]

        def pkc(off, n=1):
            return pk[:, off:off + n]

        def load_w(dst3, src3, tks, step):
            M = dst3.shape[2]
            i = 0
            for m0 in range(0, M, step):
                m1 = min(M, m0 + step)
                dma("pool", dst3[:, :, m0:m1], src3[:, :, m0:m1], w=[tks[i]])
                i += 1

        class LNT:
            def __init__(self, W):
                self.W = W
                self.sq = scr.alloc(8 * W, BF16)
                self.sbb = scr.alloc(8 * W, BF16)
                self.small = [scr.alloc(W, F32) for _ in range(4)]
                self.t1 = [scr.alloc(W, F32) for _ in range(2)]
                self.t2 = [scr.alloc(W, F32) for _ in range(2)]
                self.i = 0

        def layernorm(lt, W, src, src_tk, src_is_bf, dst, dst_tk, gcol, bcol, func, rms=False):
            sq3 = lt.sq.ap.rearrange("p (c w) -> p c w", c=8)
            sb3 = lt.sbb.ap.rearrange("p (c w) -> p c w", c=8)
            for c in range(8):
                act(sq3[:, c, 0:W], src(c), AF.Square, src_tk(c), lt.sq.tk)
                if not src_is_bf and not rms:
                    cp("pool", sb3[:, c, 0:W], src(c), src_tk(c), lt.sbb.tk)
            psq, tq = psum()
            for c in range(8):
                mm(psq[:, 0:W], onesb, sq3[:, c, 0:W], c == 0, c == 7, [tkC] + lt.sq.tk, [tq])
            mean, m2, rstd, nmr = [b for b in lt.small]
            if not rms:
                psm, tm = psum()
                for c in range(8):
                    rhs = src(c) if src_is_bf else sb3[:, c, 0:W]
                    rtk = src_tk(c) if src_is_bf else lt.sbb.tk
                    mm(psm[:, 0:W], onesb, rhs, c == 0, c == 7, [tkC] + rtk, [tm])
                act(mean.ap[:, 0:W], psm[:, 0:W], AF.Copy, [tm], mean.tk)
                tt("pool", m2.ap[:, 0:W], mean.ap[:, 0:W], mean.ap[:, 0:W], ALU.mult, mean.tk, m2.tk)
                tt("dve", m2.ap[:, 0:W], psq[:, 0:W], m2.ap[:, 0:W], ALU.subtract, [tq] + m2.tk, m2.tk)
                act(rstd.ap[:, 0:W], m2.ap[:, 0:W], AF.Ln, m2.tk, rstd.tk, bias=EPS)
            else:
                act(rstd.ap[:, 0:W], psq[:, 0:W], AF.Ln, [tq], rstd.tk, bias=EPS)
            act(rstd.ap[:, 0:W], rstd.ap[:, 0:W], AF.Exp, rstd.tk, rstd.tk, scale=-0.5)
            if not rms:
                stt(nmr.ap[:, 0:W], mean.ap[:, 0:W], -1.0, rstd.ap[:, 0:W], ALU.mult, ALU.mult,
                    mean.tk + rstd.tk, nmr.tk)
            for c in range(8):
                i = lt.i
                lt.i = (i + 1) % 2
                t1, t2 = lt.t1[i], lt.t2[i]
                tt("dve", t1.ap[:, 0:W], src(c), rstd.ap[:, 0:W], ALU.mult, src_tk(c) + rstd.tk, t1.tk)
                if not rms:
                    tt("pool", t2.ap[:, 0:W], t1.ap[:, 0:W], nmr.ap[:, 0:W], ALU.add, t1.tk + nmr.tk, t2.tk)
                    tin, ttk = t2, t2.tk
                else:
                    tin, ttk = t1, t1.tk
                outs = dst(c)
                for oi, o in enumerate(outs):
                    if bcol is not None:
                        act(o, tin.ap[:, 0:W], func, ttk + [tkPk], dst_tk(c), bias=bcol(c), scale=gcol(c))
                    else:
                        act(o, tin.ap[:, 0:W], func, ttk + [tkPk], dst_tk(c), scale=gcol(c))

        PB512 = [("P", b * 512, 512) for b in range(4)]
        SB = ("S", 0, NS)

        for l in range(L):
          try:
            dma("sp", pk[:], pack[l], w=[tkPk])
            act(acol[:], pk[0:16, O_ALOG:O_ALOG + 1], AF.Exp, [tkPk], [tkAcol])
            ts("dve", acol[:], acol[:], -1.0, None, ALU.mult, ALU.bypass, [tkAcol], [tkAcol])

            chk(l, 0)
            scr.release(0)
            CO = scr.alloc(8 * T, BF16, ntk=32)
            COs = scr.alloc(8 * NS, BF16, ntk=8)
            CO3 = CO.v("p (c t) -> p c t", c=8)
            COs3 = COs.v("p (c t) -> p c t", c=8)
            mA = scr.mark()
            XB = scr.alloc(8 * T, BF16, ntk=32)
            XB3 = XB.v("p (c t) -> p c t", c=8)
            XBs3 = XBS3
            abuf = [scr.alloc(30 + T, BF16, ntk=5) for _ in range(2)]
            hs = [scr.alloc(31 * NS, BF16, ntk=2) for _ in range(2)]
            diag = [scr.alloc(31 * 128, BF16) for _ in range(2)]
            Wc = [scr.alloc(2 * 8 * 128, BF16, ntk=2) for _ in range(2)]
            sg = [scr.alloc(512, F32) for _ in range(2)]
            anew = [scr.alloc(NS, F32) for _ in range(2)]
            tailb = [scr.alloc(30, F32) for _ in range(2)]
            for c in range(8):
                for b in range(4):
                    cp("pool", XB3[:, c, b * 512:(b + 1) * 512], XFp[:, c, b * 512:(b + 1) * 512],
                       XT(c, b * 512, 512), [XB.tk[c * 4 + b]])
            cp("pool", XBs3, XFs, tkXs, tkXBS)
            for j in range(16):
                dma("sp", xbd[:, j], XB3[:, :, j * 128:(j + 1) * 128], r=[XB.tk[c * 4 + j // 4] for c in range(8)], w=[tkXbd[j]])
            for i in range(2):
                mset("pool", abuf[i].ap[:, 0:30], 0.0, [abuf[i].tk[0]])
            sgi = 0
            for c in range(8):
                i = c % 2
                Wc4 = Wc[i].v("p (h k m) -> p h k m", h=2, k=8)
                dma("pool", Wc4[:, 0], w_in[l][:, :, c * 128:(c + 1) * 128], w=[Wc[i].tk[0]])
                dma("pool", Wc4[:, 1], w_in[l][:, :, 1024 + c * 128:1024 + (c + 1) * 128], w=[Wc[i].tk[1]])
                dg3 = diag[i].v("p (k m) -> p k m", k=31)
                tt("pool", dg3, identb.unsqueeze(1).to_broadcast([128, 31, 128]),
                   pkc(O_CONFW + c * 31, 31).unsqueeze(2).to_broadcast([128, 31, 128]), ALU.mult,
                   [tkC, tkPk], diag[i].tk)
                hs3 = hs[i].v("p (k s) -> p k s", k=31)
                dma("pool", hs3[:, 0:30, :], confT[l][:, c], w=[hs[i].tk[0]])
                dma("sp", confS[l][:, c, 0:29, :], confT[l][:, c, 1:30, :])
                bcol = pkc(O_CONFB + c)
                for b in range(4):
                    psv, tv = psum()
                    pg, tg = psum()
                    for kc in range(8):
                        mm(psv[:, :], Wc4[:, 0, kc, :], XB3[:, kc, b * 512:(b + 1) * 512], kc == 0, kc == 7,
                           [Wc[i].tk[0], XB.tk[kc * 4 + b]], [tv])
                    for kc in range(8):
                        mm(pg[:, :], Wc4[:, 1, kc, :], XB3[:, kc, b * 512:(b + 1) * 512], kc == 0, kc == 7,
                           [Wc[i].tk[1], XB.tk[kc * 4 + b]], [tg])
                    s_ = sg[sgi]
                    sgi = (sgi + 1) % 2
                    act(s_.ap, pg[:, :], AF.Sigmoid, [tg], s_.tk)
                    tt("dve", abuf[i].ap[:, 30 + b * 512:30 + (b + 1) * 512], psv[:, :], s_.ap, ALU.mult,
                       [tv] + s_.tk, [abuf[i].tk[1 + b]])
                psv, tv = psum()
                pg, tg = psum()
                for kc in range(8):
                    mm(psv[:, 0:NS], Wc4[:, 0, kc, :], XBs3[:, kc, :], kc == 0, kc == 7, [Wc[i].tk[0]] + tkXBS, [tv])
                for kc in range(8):
                    mm(pg[:, 0:NS], Wc4[:, 1, kc, :], XBs3[:, kc, :], kc == 0, kc == 7, [Wc[i].tk[1]] + tkXBS, [tg])
                s_ = sg[sgi]
                sgi = (sgi + 1) % 2
                act(s_.ap[:, 0:NS], pg[:, 0:NS], AF.Sigmoid, [tg], s_.tk)
                tt("dve", anew[i].ap, psv[:, 0:NS], s_.ap[:, 0:NS], ALU.mult, [tv] + s_.tk, anew[i].tk)
                cp("pool", hs3[:, 30, :], anew[i].ap, anew[i].tk, [hs[i].tk[1]])
                dma("sp", confS[l][:, c, 29, :], anew[i].ap, r=anew[i].tk)
                for b in range(4):
                    pc, tc = psum()
                    rt = [abuf[i].tk[1 + b], abuf[i].tk[b]] + diag[i].tk
                    for k in range(31):
                        mm(pc[:, :], dg3[:, k, :], abuf[i].ap[:, b * 512 + k:b * 512 + k + 512], k == 0, k == 30, rt, [tc])
                    act(CO3[:, c, b * 512:(b + 1) * 512], pc[:, :], AF.Identity, [tc, tkPk], [CO.tk[c * 4 + b]], bias=bcol)
                pc, tc = psum()
                for k in range(31):
                    mm(pc[:, 0:NS], dg3[:, k, :], hs3[:, k, :], k == 0, k == 30, hs[i].tk + diag[i].tk, [tc])
                act(COs3[:, c, :], pc[:, 0:NS], AF.Identity, [tc, tkPk], [COs.tk[c]], bias=bcol)
                cp("act", tailb[i].ap, abuf[i].ap[:, T:T + 30], [abuf[i].tk[4]], tailb[i].tk)
                dma("sp", confP[l][:, c, :], tailb[i].ap, r=tailb[i].tk)

            chk(l, 1)
            scr.release(mA)
            WoA = scr.alloc(8 * 1024, BF16, ntk=4)
            WoA3 = WoA.v("p (k m) -> p k m", k=8)
            load_w(WoA3, w_oA[l], WoA.tk, 256)
            lnt = LNT(512)
            for blk in PB512 + [SB]:
                W = bw(blk)
                if blk[0] == "S":
                    srcf = lambda c: COs3[:, c, :]
                    stk = lambda c: [COs.tk[c]]
                else:
                    b = blk[1] // 512
                    srcf = lambda c, b=b: CO3[:, c, b * 512:(b + 1) * 512]
                    stk = lambda c, b=b: [CO.tk[c * 4 + b]]
                layernorm(lnt, W, srcf, stk, True, lambda c: [srcf(c)], stk,
                          lambda c: pkc(O_CLNG + c), lambda c: pkc(O_CLNB + c), AF.Silu)
                for m in range(8):
                    ps, tp = psum()
                    for kc in range(8):
                        mm(ps[:, 0:W], WoA3[:, kc, m * 128:(m + 1) * 128], srcf(kc), kc == 0, kc == 7,
                           [WoA.tk[m // 2]] + stk(kc), [tp])
                    stt(xf(blk, m), xf(blk, m), ALPHA, ps[:, 0:W], ALU.mult, ALU.add, xtk(blk, m) + [tp], xtk(blk, m))

            chk(l, 2)
            scr.release(0)
            Wx = scr.alloc(8 * 1536, BF16, ntk=6)
            Wz = scr.alloc(8 * 1024, BF16, ntk=4)
            Wdt = scr.alloc(8 * 16, BF16)
            WoY = scr.alloc(8 * 1024, BF16, ntk=4)
            Wx3 = Wx.v("p (k m) -> p k m", k=8)
            Wz3 = Wz.v("p (k m) -> p k m", k=8)
            Wdt3 = Wdt.v("p (k m) -> p k m", k=8)
            WoY3 = WoY.v("p (k m) -> p k m", k=8)
            load_w(Wx3, w_in[l][:, :, 3072:4608], Wx.tk, 256)
            load_w(Wdt3, w_in[l][:, :, 4608:4624], Wdt.tk, 16)
            load_w(Wz3, w_in[l][:, :, 2048:3072], Wz.tk, 256)
            load_w(WoY3, w_oY[l], WoY.tk, 256)
            dg4 = scr.alloc(12 * 4 * 128, BF16, ntk=12)
            dg44 = dg4.v("p (u k m) -> p u k m", u=12, k=4)
            for u in range(12):
                tt("pool", dg44[:, u], identb.unsqueeze(1).to_broadcast([128, 4, 128]),
                   pkc(O_SSDW + u * 4, 4).unsqueeze(2).to_broadcast([128, 4, 128]), ALU.mult, [tkC, tkPk], [dg4.tk[u]])
            lnt = LNT(256)
            mW = scr.mark()
            hT = scr.alloc(1024, F32, ntk=2)
            hTb = scr.alloc(1024, BF16, ntk=2)
            mset("pool", hT.ap, 0.0, hT.tk)
            mset("pool", hTb.ap, 0.0, hTb.tk)
            xb = scr.alloc(8 * 128, BF16)
            xb3 = xb.v("p (k t) -> p k t", k=8)
            raw = scr.alloc(12 * 131, BF16)
            raw3 = raw.v("p (u t) -> p u t", u=12)
            xact = scr.alloc(12 * 128, BF16)
            xact3 = xact.v("p (u t) -> p u t", u=12)
            rawt = scr.alloc(36, F32)
            fm = [scr.alloc(128, F32) for _ in range(4)]
            tok4 = scr.alloc(64, F32)
            tok43 = tok4.v("p (a h) -> p a h", a=4)
            e1 = scr.alloc(16, F32); dte = scr.alloc(16, F32); cdec = scr.alloc(16, F32)
            xsT_ = scr.alloc(1024, BF16); BtT = scr.alloc(256, BF16)
            Rb = scr.alloc(16 * 128, BF16); LT = scr.alloc(16 * 128, BF16); scT = Rb
            Gm = scr.alloc(256, BF16)
            dtx = scr.alloc(1024, BF16); dtxe = scr.alloc(1024, BF16); Dxs = scr.alloc(1024, BF16)
            yo = scr.alloc(1024, F32); zs = scr.alloc(1024, BF16)
            yn = scr.alloc(1024, BF16); Yfm = scr.alloc(8 * 256, BF16); junk = yn
            Yfm3p = Yfm.v("p (c t) -> p c t", c=8)
            ss = scr.alloc(1, F32); rs1 = scr.alloc(1, F32)

            def bc_h(ap16, lo, n):
                return ap16[:, lo:lo + n].unsqueeze(2).to_broadcast([128, n, 64])

            for j in range(16):
                t0 = j * 128
                blk = ("P", t0, 128)
                b5 = t0 // 512
                dma("sp", xb3, xbd[:, j], r=[tkXbd[j]], w=xb.tk)
                if j == 0:
                    mset("pool", raw3[:, :, 0:3], 0.0, raw.tk)
                else:
                    cp("pool", raw3[:, :, 0:3], raw3[:, :, 128:131], raw.tk, raw.tk)
                for g4 in range(3):
                    ps, tp = psum()
                    for uu in range(4):
                        u = g4 * 4 + uu
                        for kc in range(8):
                            mm(ps[:, uu * 128:(uu + 1) * 128], Wx3[:, kc, u * 128:(u + 1) * 128], xb3[:, kc, :],
                               kc == 0, kc == 7, [Wx.tk[u // 2]] + xb.tk, [tp])
                    cp("act", raw3[:, g4 * 4:g4 * 4 + 4, 3:131], ps[:, :].rearrange("p (u t) -> p u t", u=4), [tp], raw.tk)
                if j == 15:
                    cp("act", rawt.ap.rearrange("p (u k) -> p u k", u=12), raw3[:, :, 128:131], raw.tk, rawt.tk)
                    dma("sp", ssdcP[l], rawt.ap.rearrange("p (u k) -> p u k", u=12), r=rawt.tk)
                for g4 in range(3):
                    ps, tp = psum()
                    for uu in range(4):
                        u = g4 * 4 + uu
                        for k in range(4):
                            mm(ps[:, uu * 128:(uu + 1) * 128], dg44[:, u, k, :], raw3[:, u, k:k + 128], k == 0, k == 3,
                               raw.tk + [dg4.tk[u]], [tp])
                    for uu in range(4):
                        u = g4 * 4 + uu
                        act(xact3[:, u, :], ps[:, uu * 128:(uu + 1) * 128], AF.Silu, [tp, tkPk], xact.tk, bias=pkc(O_SSDB + u))
                ps, tp = psum()
                for kc in range(8):
                    mm(ps[0:16, 0:128], Wdt3[:, kc, :], xb3[:, kc, :], kc == 0, kc == 7, Wdt.tk + xb.tk, [tp])
                fe, fdt, fdtA, fac = fm
                act(fe.ap[0:16, :], ps[0:16, 0:128], AF.Exp, [tp, tkPk], fe.tk, bias=pk[0:16, O_DTB:O_DTB + 1])
                act(fdt.ap[0:16, :], fe.ap[0:16, :], AF.Ln, fe.tk, fdt.tk, bias=1.0)
                ts("dve", fdtA.ap[0:16, :], fdt.ap[0:16, :], acol[:], None, ALU.mult, ALU.bypass, fdt.tk + [tkAcol], fdtA.tk)
                scan(fac.ap[0:16, :], onesf[0:16, :], fdtA.ap[0:16, :], fdtA.tk + [tkC], fac.tk)
                ps, tp = psum()
                tr(ps[:, 0:16], fdt.ap[0:16, :], identf[0:16, 0:16], fdt.tk + [tkC], [tp])
                tr(ps[:, 16:32], fdtA.ap[0:16, :], identf[0:16, 0:16], fdtA.tk + [tkC], [tp])
                tr(ps[:, 32:48], fac.ap[0:16, :], identf[0:16, 0:16], fac.tk + [tkC], [tp])
                tr(ps[:, 48:64], fac.ap[0:16, 127:128].to_broadcast([16, 128]), identf[0:16, 0:16], fac.tk + [tkC], [tp])
                cp("dve", tok4.ap, ps[:, 0:64], [tp], tok4.tk)
                dtT, dtAT, acT, totT = tok43[:, 0, :], tok43[:, 1, :], tok43[:, 2, :], tok43[:, 3, :]
                act(e1.ap, acT, AF.Exp, tok4.tk, e1.tk)
                tt("dve", dte.ap, totT, acT, ALU.subtract, tok4.tk, dte.tk)
                act(dte.ap, dte.ap, AF.Exp, dte.tk, dte.tk)
                tt("dve", dte.ap, dte.ap, dtT, ALU.mult, dte.tk + tok4.tk, dte.tk)
                act(cdec.ap, totT, AF.Exp, tok4.tk, cdec.tk)
                ps, tp = psum()
                pbf = ps[:, :].bitcast(BF16)
                for c in range(8):
                    tr(pbf[:, c * 128:(c + 1) * 128], xact3[:, c, :], identb, xact.tk + [tkC], [tp])
                cp("act", xsT_.ap, pbf[:, 0:1024], [tp], xsT_.tk)
                ps, tp = psum()
                pbf = ps[:, :].bitcast(BF16)
                for g in range(2):
                    tr(pbf[:, g * 128:(g + 1) * 128], xact3[:, 8 + g, :], identb, xact.tk + [tkC], [tp])
                cp("dve", BtT.ap, pbf[:, 0:256], [tp], BtT.tk)
                Rb3 = Rb.v("p (h l) -> p h l", h=16)
                LT3 = LT.v("p (h l) -> p h l", h=16)
                sc3 = scT.v("p (h l) -> p h l", h=16)
                tt("pool", Rb3, ulef.unsqueeze(1).to_broadcast([128, 16, 128]),
                   dtAT.unsqueeze(2).to_broadcast([128, 16, 128]), ALU.mult, [tkC] + tok4.tk, Rb.tk)
                for q in range(4):
                    ps, tp = psum()
                    mm(ps[:, :], mstb, Rb.ap[:, q * 512:(q + 1) * 512], True, True, [tkC] + Rb.tk, [tp])
                    act(LT.ap[:, q * 512:(q + 1) * 512], ps[:, :], AF.Exp, [tp], LT.tk)
                ps, tp = psum()
                for g in range(2):
                    mm(ps[:, g * 128:(g + 1) * 128], xact3[:, 8 + g, :], xact3[:, 10 + g, :], True, True, xact.tk, [tp])
                tt("dve", Gm.v("p (g l) -> p g l", g=2), ps[:, 0:256].rearrange("p (g l) -> p g l", g=2),
                   ulef.unsqueeze(1).to_broadcast([128, 2, 128]), ALU.mult, [tp, tkC], Gm.tk)
                Gm3 = Gm.v("p (g l) -> p g l", g=2)
                for g in range(2):
                    tt("pool" if g else "dve", sc3[:, 8 * g:8 * g + 8, :], LT3[:, 8 * g:8 * g + 8, :],
                       Gm3[:, g, :].unsqueeze(1).to_broadcast([128, 8, 128]), ALU.mult, LT.tk + Gm.tk, scT.tk)
                xs3 = xsT_.v("p (h d) -> p h d", h=16)
                tt("dve", dtx.v("p (h d) -> p h d", h=16), xs3, bc_h(dtT, 0, 16), ALU.mult, xsT_.tk + tok4.tk, dtx.tk)
                tt("pool", dtxe.v("p (h d) -> p h d", h=16), xs3, bc_h(dte.ap, 0, 16), ALU.mult, xsT_.tk + dte.tk, dtxe.tk)
                tt("pool", Dxs.v("p (h d) -> p h d", h=16), xs3, bc_h(pk[:, O_DROW:O_DROW + 16], 0, 16), ALU.mult,
                   xsT_.tk + [tkPk], Dxs.tk)
                dtx3 = dtx.v("p (h d) -> p h d", h=16)
                psY = []
                for g in range(2):
                    ps, tp = psum()
                    mm(ps[:, :], identb, Dxs.ap[:, g * 512:(g + 1) * 512], True, False, [tkC] + Dxs.tk, [tp])
                    for hh in range(8):
                        h = g * 8 + hh
                        mm(ps[:, hh * 64:(hh + 1) * 64], sc3[:, h, :], dtx3[:, h, :], False, hh == 7, scT.tk + dtx.tk, [tp])
                    psY.append((ps, tp))
                yo3 = yo.v("p (h d) -> p h d", h=16)
                for g in range(2):
                    ps, tp = psum()
                    mm(ps[:, :], xact3[:, 10 + g, :], hTb.ap[:, g * 512:(g + 1) * 512], True, True, xact.tk + [hTb.tk[g]], [tp])
                    tt("dve", yo3[:, 8 * g:8 * g + 8, :], ps[:, :].rearrange("p (h d) -> p h d", h=8),
                       bc_h(e1.ap, 8 * g, 8), ALU.mult, [tp] + e1.tk, yo.tk)
                    tt("dve", yo.ap[:, g * 512:(g + 1) * 512], psY[g][0][:, :], yo.ap[:, g * 512:(g + 1) * 512], ALU.add,
                       [psY[g][1]] + yo.tk, yo.tk)
                for hf in range(2):
                    ps, tp = psum()
                    for kc in range(8):
                        mm(ps[:, :], xb3[:, kc, :], Wz3[:, kc, hf * 512:(hf + 1) * 512], kc == 0, kc == 7,
                           xb.tk + [Wz.tk[2 * hf], Wz.tk[2 * hf + 1]], [tp])
                    act(zs.ap[:, hf * 512:(hf + 1) * 512], ps[:, :], AF.Silu, [tp], zs.tk)
                tt("dve", yo.ap, yo.ap, zs.ap, ALU.mult, yo.tk + zs.tk, yo.tk)
                act(junk.ap, yo.ap, AF.Square, yo.tk, junk.tk + ss.tk, accum_out=ss.ap)
                act(rs1.ap, ss.ap, AF.Ln, ss.tk, rs1.tk, bias=EPS, scale=1.0 / 1024.0)
                act(rs1.ap, rs1.ap, AF.Exp, rs1.tk, rs1.tk, scale=-0.5)
                act(yn.ap, yo.ap, AF.Copy, yo.tk + rs1.tk, yn.tk, scale=rs1.ap)
                ps, tp = psum()
                pbf = ps[:, :].bitcast(BF16)
                for c in range(8):
                    tr(pbf[:, c * 128:(c + 1) * 128], yn.ap[:, c * 128:(c + 1) * 128], identb, yn.tk + [tkC], [tp])
                Yfm3 = Yfm3p[:, :, (j % 2) * 128:(j % 2) * 128 + 128]
                tt("dve", Yfm3, pbf[:, 0:1024].rearrange("p (c t) -> p c t", c=8),
                   pk[:, O_RMSG:O_RMSG + 8].unsqueeze(2).to_broadcast([128, 8, 128]), ALU.mult, [tp, tkPk], Yfm.tk)
                hT3 = hT.v("p (h d) -> p h d", h=16)
                for g in range(2):
                    ps, tp = psum()
                    mm(ps[:, :], BtT.ap[:, g * 128:(g + 1) * 128], dtxe.ap[:, g * 512:(g + 1) * 512], True, True,
                       BtT.tk + dtxe.tk, [tp])
                    tt("pool", hT3[:, 8 * g:8 * g + 8, :], hT3[:, 8 * g:8 * g + 8, :], bc_h(cdec.ap, 8 * g, 8), ALU.mult,
                       [hT.tk[g]] + cdec.tk, [hT.tk[g]])
                    tt("dve", hT.ap[:, g * 512:(g + 1) * 512], hT.ap[:, g * 512:(g + 1) * 512], ps[:, :], ALU.add,
                       [hT.tk[g], tp], [hT.tk[g]])
                    cp("act", hTb.ap[:, g * 512:(g + 1) * 512], hT.ap[:, g * 512:(g + 1) * 512], [hT.tk[g]], [hTb.tk[g]])
                if j % 2 == 1:
                    t0p = t0 - 128
                    blkp = ("P", t0p, 256)
                    for m2 in range(4):
                        ps, tp = psum()
                        for mm_ in range(2):
                            m = m2 * 2 + mm_
                            for kc in range(8):
                                mm(ps[:, mm_ * 256:(mm_ + 1) * 256], WoY3[:, kc, m * 128:(m + 1) * 128], Yfm3p[:, kc, :],
                                   kc == 0, kc == 7, [WoY.tk[m // 2]] + Yfm.tk, [tp])
                        xtks = [t_ for q in range(2) for t_ in XT(m2 * 2 + q, t0p, 256)]
                        tt("dve", XFp[:, m2 * 2:m2 * 2 + 2, t0p:t0p + 256], XFp[:, m2 * 2:m2 * 2 + 2, t0p:t0p + 256],
                           ps[:, :].rearrange("p (m t) -> p m t", m=2), ALU.add, xtks + [tp], xtks)
                    layernorm(lnt, 256, lambda c: xf(blkp, c), lambda c: xtk(blkp, c), False,
                              lambda c: [xf(blkp, c)], lambda c: xtk(blkp, c),
                              lambda c: pkc(O_MIXG + c), lambda c: pkc(O_MIXB + c), AF.Identity)
            dma("sp", ssdPT[l], hT.ap, r=hT.tk)

            chk(l, 3)
            scr.release(mW)
            xbs = Buf(XBS_t[:], tkXBS)
            xbs3 = XBS3
            raws = scr.alloc(12 * NS, F32)
            raws3 = raws.v("p (u s) -> p u s", u=12)
            hs4 = scr.alloc(12 * 4 * NS, BF16, ntk=2)
            hs44 = hs4.v("p (u k s) -> p u k s", u=12, k=4)
            xas = scr.alloc(12 * NS, F32)
            xas3 = xas.v("p (u s) -> p u s", u=12)
            ps, tp = psum()
            for u in range(12):
                for kc in range(8):
                    mm(ps[:, u * NS:(u + 1) * NS], Wx3[:, kc, u * 128:(u + 1) * 128], xbs3[:, kc, :], kc == 0, kc == 7,
                       [Wx.tk[u // 2]] + xbs.tk, [tp])
            cp("act", raws.ap, ps[:, 0:12 * NS], [tp], raws.tk)
            dma("sp", ssdcS[l][:, :, 2, :], raws3, r=raws.tk)
            dma("sp", ssdcS[l][:, :, 0:2, :], ssdcT[l][:, :, 1:3, :])
            dma("pool", hs44[:, :, 0:3, :], ssdcT[l], w=[hs4.tk[0]])
            cp("pool", hs44[:, :, 3, :], raws3, raws.tk, [hs4.tk[1]])
            ps, tp = psum()
            for u in range(12):
                for k in range(4):
                    mm(ps[:, u * NS:(u + 1) * NS], dg44[:, u, k, :], hs44[:, u, k, :], k == 0, k == 3, hs4.tk + [dg4.tk[u]], [tp])
            for u in range(12):
                act(xas3[:, u, :], ps[:, u * NS:(u + 1) * NS], AF.Silu, [tp, tkPk], xas.tk, bias=pkc(O_SSDB + u))
            xasb = scr.alloc(4 * NS, BF16)
            xasb3 = xasb.v("p (u s) -> p u s", u=4)
            cp("pool", xasb3, xas3[:, 8:12, :], xas.tk, xasb.tk)
            dts = scr.alloc(NS, F32); dAs = scr.alloc(NS, F32)
            ps, tp = psum()
            for kc in range(8):
                mm(ps[0:16, 0:NS], Wdt3[:, kc, :], xbs3[:, kc, :], kc == 0, kc == 7, Wdt.tk + xbs.tk, [tp])
            act(dts.ap[0:16, :], ps[0:16, 0:NS], AF.Exp, [tp, tkPk], dts.tk, bias=pk[0:16, O_DTB:O_DTB + 1])
            act(dts.ap[0:16, :], dts.ap[0:16, :], AF.Ln, dts.tk, dts.tk, bias=1.0)
            act(dAs.ap[0:16, :], dts.ap[0:16, :], AF.Exp, dts.tk + [tkAcol], dAs.tk, scale=acol[:])
            dE = scr.alloc(2 * 8 * NS, F32)
            dE4 = dE.v("p (a c s) -> p a c s", a=2, c=8)
            ps, tp = psum()
            for c in range(8):
                mm(ps[:, c * NS:(c + 1) * NS], ehpf[:, c, :], dts.ap[0:16, :], True, True, [tkC] + dts.tk, [tp])
            for c in range(8):
                mm(ps[:, 128 + c * NS:128 + (c + 1) * NS], ehpf[:, c, :], dAs.ap[0:16, :], True, True, [tkC] + dAs.tk, [tp])
            cp("dve", dE.ap, ps[:, 0:256], [tp], dE.tk)
            dtxs = scr.alloc(8 * NS, F32)
            dtxs3 = dtxs.v("p (c s) -> p c s", c=8)
            tt("dve", dtxs3, dE4[:, 0], xas3[:, 0:8, :], ALU.mult, dE.tk + xas.tk, dtxs.tk)
            ysm = scr.alloc(8 * NS, F32)
            ysm3 = ysm.v("p (c s) -> p c s", c=8)
            Hb = [scr.alloc(8 * 128, F32) for _ in range(2)]
            bcb = [scr.alloc(4 * 128, BF16) for _ in range(2)]
            t1s = [scr.alloc(128, F32) for _ in range(2)]
            jk2 = [scr.alloc(128, F32) for _ in range(2)]
            for s in range(NS):
                H = Hb[s % 2]
                H3 = H.v("p (c n) -> p c n", c=8)
                dma("sp", H3, ssdst[l, s].rearrange("(c q) n -> q c n", q=128), w=H.tk)
                bb = bcb[s % 2]
                bb3 = bb.v("p (u q) -> p u q", u=4)
                cp("pool", bb3, xasb3[:, :, s:s + 1].to_broadcast([128, 4, 128]), xasb.tk, bb.tk)
                psbc, tbc = psum()
                for u in range(4):
                    mm(psbc[:, u * 128:(u + 1) * 128], bb3[:, u, :], identb, True, True, bb.tk + [tkC], [tbc])
                for c in range(8):
                    g = c // 4
                    t1 = t1s[c % 2]
                    ts("dve", t1.ap, psbc[:, g * 128:(g + 1) * 128], dtxs3[:, c, s:s + 1], None, ALU.mult, ALU.bypass,
                       [tbc] + dtxs.tk, t1.tk)
                    stt(H3[:, c, :], H3[:, c, :], dE4[:, 1, c, s:s + 1], t1.ap, ALU.mult, ALU.add, H.tk + dE.tk + t1.tk, H.tk)
                    jk = jk2[c % 2]
                    stt(jk.ap, H3[:, c, :], 1.0, psbc[:, (2 + g) * 128:(3 + g) * 128], ALU.mult, ALU.mult,
                        H.tk + [tbc], jk.tk + ysm.tk, accum_out=ysm3[:, c, s:s + 1])
                dma("sp", ssdS[l, s].rearrange("(c q) n -> q c n", q=128), H3, r=H.tk)
            tmpd = scr.alloc(8 * NS, F32)
            tmpd3 = tmpd.v("p (c s) -> p c s", c=8)
            tt("dve", tmpd3, xas3[:, 0:8, :], pk[:, O_DCOL:O_DCOL + 8].unsqueeze(2).to_broadcast([128, 8, NS]), ALU.mult,
               xas.tk + [tkPk], tmpd.tk)
            tt("dve", ysm.ap, ysm.ap, tmpd.ap, ALU.add, ysm.tk + tmpd.tk, ysm.tk)
            zss = scr.alloc(8 * NS, F32)
            ps, tp = psum()
            for c in range(8):
                for kc in range(8):
                    mm(ps[:, c * NS:(c + 1) * NS], Wz3[:, kc, c * 128:(c + 1) * 128], xbs3[:, kc, :], kc == 0, kc == 7,
                       [Wz.tk[c // 2]] + xbs.tk, [tp])
            act(zss.ap, ps[:, 0:8 * NS], AF.Silu, [tp], zss.tk)
            tt("dve", ysm.ap, ysm.ap, zss.ap, ALU.mult, ysm.tk + zss.tk, ysm.tk)
            Ys = scr.alloc(8 * NS, BF16)
            Ys3 = Ys.v("p (c s) -> p c s", c=8)
            layernorm(lnt, NS, lambda c: ysm3[:, c, :], lambda c: ysm.tk, False, lambda c: [Ys3[:, c, :]], lambda c: Ys.tk,
                      lambda c: pkc(O_RMSG + c), None, AF.Copy, rms=True)
            ps, tp = psum()
            for m in range(8):
                for kc in range(8):
                    mm(ps[:, m * NS:(m + 1) * NS], WoY3[:, kc, m * 128:(m + 1) * 128], Ys3[:, kc, :], kc == 0, kc == 7,
                       [WoY.tk[m // 2]] + Ys.tk, [tp])
            tt("dve", XFs, XFs, ps[:, 0:8 * NS].rearrange("p (m s) -> p m s", m=8), ALU.add, tkXs + [tp], tkXs)
            layernorm(lnt, NS, lambda c: xf(SB, c), lambda c: xtk(SB, c), False, lambda c: [xf(SB, c)], lambda c: xtk(SB, c),
                      lambda c: pkc(O_MIXG + c), lambda c: pkc(O_MIXB + c), AF.Identity)

            chk(l, 4)
            scr.release(0)
            Wq = scr.alloc(8 * 1024, BF16, ntk=4); Wo = scr.alloc(8 * 1024, BF16, ntk=4)
            Wq3 = Wq.v("p (k m) -> p k m", k=8); Wo3 = Wo.v("p (k m) -> p k m", k=8)
            KT = scr.alloc(8 * 256, BF16); Vb = scr.alloc(2 * 1024, BF16)
            KT3 = KT.v("p (c m) -> p c m", c=8); Vb3 = Vb.v("p (a e) -> p a e", a=2)
            mC = scr.mark()
            Wk = scr.alloc(8 * 1024, BF16, ntk=4); Wv = scr.alloc(8 * 1024, BF16, ntk=4)
            Wk3 = Wk.v("p (k m) -> p k m", k=8); Wv3 = Wv.v("p (k m) -> p k m", k=8)
            memb = scr.alloc(8 * 256, BF16)
            memb3 = memb.v("p (c m) -> p c m", c=8)
            kf = [scr.alloc(256, F32) for _ in range(2)]
            vf = [scr.alloc(512, F32) for _ in range(2)]
            import os
            if os.environ.get("KVSKIP"):
                S.mute = True
            load_w(Wk3, wk[l], Wk.tk, 256)
            load_w(Wv3, wv[l], Wv.tk, 256)
            load_w(Wq3, wq[l], Wq.tk, 256)
            load_w(Wo3, wo[l], Wo.tk, 256)
            for hh in range(4):
                dma("pool", memb3[:, 2 * hh:2 * hh + 2, :], memT[:, 2 * hh:2 * hh + 2, :], w=memb.tk)
            chk(l, 4, -2)
            for c in range(8):
                ps, tp = psum()
                for kc in range(8):
                    mm(ps[:, 0:256], Wk3[:, kc, c * 128:(c + 1) * 128], memb3[:, kc, :], kc == 0, kc == 7,
                       [Wk.tk[c // 2]] + memb.tk, [tp])
                cp("act", KT3[:, c, :], ps[:, 0:256], [tp], KT.tk)
                cp("dve", kf[c % 2].ap, ps[:, 0:256], [tp], kf[c % 2].tk)
                dma("sp", kP[l][:, c, :], kf[c % 2].ap, r=kf[c % 2].tk)
            chk(l, 4, -1)
            vi = 0
            for mc in range(2):
                for hf in range(2):
                    ps, tp = psum()
                    for kc in range(8):
                        mm(ps[:, :], memb3[:, kc, mc * 128:(mc + 1) * 128], Wv3[:, kc, hf * 512:(hf + 1) * 512], kc == 0, kc == 7,
                           memb.tk + [Wv.tk[2 * hf], Wv.tk[2 * hf + 1]], [tp])
                    import os
                    V_ = os.environ.get("VDBG", "abc")
                    if "a" in V_:
                        cp("act", Vb3[:, mc, hf * 512:(hf + 1) * 512], ps[:, :], [tp], Vb.tk)
                    if "b" in V_:
                        cp("dve", vf[vi].ap, ps[:, :], [tp], vf[vi].tk)
                    if "c" in V_:
                        dma("sp", vP[l][mc * 128:(mc + 1) * 128, hf * 512:(hf + 1) * 512], vf[vi].ap, r=vf[vi].tk)
                    vi = (vi + 1) % 2
            chk(l, 4, 1)
            scr.release(mC)
            NCB = 2
            CSETS = []
            for _k in range(NCB):
                _d = {}
                _d["xb"] = scr.alloc(8 * 128, BF16)
                _d["qT"] = scr.alloc(8 * 128, BF16)
                for _n in ("mx", "nb", "rsum", "rinv"):
                    _d[_n] = scr.alloc(4, F32)
                _d["Pm"] = scr.alloc(4 * 256, BF16)
                _d["PT"] = scr.alloc(8 * 128, BF16)
                _d["Ob"] = scr.alloc(1024, BF16)
                _d["OT"] = scr.alloc(8 * 128, BF16)
                _d["lnt"] = LNT(128)
                CSETS.append(_d)
            NKB = 3
            kb = [scr.alloc(8 * 256, BF16) for _ in range(NKB)]
            vb = [scr.alloc(2 * 1024, BF16) for _ in range(NKB)]
            Qm = scr.alloc(8 * NS * NS, BF16)
            PTm = scr.alloc(8 * NS * NS, BF16)
            SCALE = 256.0 ** -0.5
            _d = CSETS[0]
            xb, qT, mx, nb, rsum, rinv, Pm, PT, Ob, OT, lnt = (_d[k] for k in
                                                                  ("xb", "qT", "mx", "nb", "rsum", "rinv", "Pm", "PT", "Ob", "OT", "lnt"))
            xb3 = xb.v("p (k t) -> p k t", k=8); qT3 = qT.v("p (c t) -> p c t", c=8)
            Pm3 = Pm.v("p (h m) -> p h m", h=4); PT4 = PT.v("p (h a t) -> p h a t", h=4, a=2)
            OT3 = OT.v("p (c t) -> p c t", c=8)

            def softmax_pv(Q, sbanks, blk):
                for h in range(4):
                    red(mx.ap[0:Q, h:h + 1], sbanks[h][0], ALU.max, [sbanks[h][1]], mx.tk)
                ts("dve", nb.ap[0:Q, :], mx.ap[0:Q, :], -SCALE, None, ALU.mult, ALU.bypass, mx.tk, nb.tk)
                for h in range(4):
                    act(Pm3[0:Q, h, :], sbanks[h][0], AF.Exp, [sbanks[h][1]] + nb.tk, Pm.tk + rsum.tk,
                        bias=nb.ap[0:Q, h:h + 1], scale=SCALE, accum_out=rsum.ap[0:Q, h:h + 1])
                recip(rinv.ap[0:Q, :], rsum.ap[0:Q, :], rsum.tk, rinv.tk)
                ps, tp = psum()
                pbf = ps[:, :].bitcast(BF16)
                for h in range(4):
                    for a in range(2):
                        tr(pbf[:, (h * 2 + a) * 128:(h * 2 + a) * 128 + Q], Pm3[0:Q, h, a * 128:(a + 1) * 128], identb[0:Q, 0:Q],
                           Pm.tk + [tkC], [tp])
                cp("act", PT4[:, :, :, 0:Q], pbf[:, 0:1024].rearrange("p (h a t) -> p h a t", h=4, a=2)[:, :, :, 0:Q], [tp], PT.tk)

            xbs = Buf(XBS_t[:], tkXBS)
            xbs3 = XBS3
            cp("pool", xbs3, XFs, tkXs, xbs.tk)
            qs = scr.alloc(8 * NS, BF16)
            qs3 = qs.v("p (c s) -> p c s", c=8)
            ps, tp = psum()
            for m in range(8):
                for kc in range(8):
                    mm(ps[:, m * NS:(m + 1) * NS], Wq3[:, kc, m * 128:(m + 1) * 128], xbs3[:, kc, :], kc == 0, kc == 7,
                       [Wq.tk[m // 2]] + xbs.tk, [tp])
            cp("act", qs.ap, ps[:, 0:8 * NS], [tp], qs.tk)
            Qm4 = Qm.v("p (c s q) -> p c s q", c=8, s=NS)
            tt("pool", Qm4, qs3.unsqueeze(3).to_broadcast([128, 8, NS, NS]),
               id16b.unsqueeze(1).to_broadcast([128, 8, NS, NS]), ALU.mult, qs.tk + [tkC], Qm.tk)
            sbk = [psum() for _ in range(4)]
            for s in range(NS):
                kbs = kb[s % NKB]
                kbs3 = kbs.v("p (c m) -> p c m", c=8)
                for hh in range(2):
                    dma("pool", kbs3[:, 4 * hh:4 * hh + 4, :], kcT[l, s][:, 4 * hh:4 * hh + 4, :], w=kbs.tk)
                for h in range(4):
                    for dc in range(2):
                        mm(sbk[h][0][0:NS, 0:256], Qm4[:, 2 * h + dc, s, :], kbs3[:, 2 * h + dc, :], s == 0 and dc == 0,
                           s == NS - 1 and dc == 1, Qm.tk + kbs.tk, [sbk[h][1]])
            softmax_pv(NS, [(sbk[h][0][0:NS, 0:256], sbk[h][1]) for h in range(4)], SB)
            PTm5 = PTm.v("p (h a s q) -> p h a s q", h=4, a=2, s=NS)
            for h in range(4):
                tt("pool", PTm5[:, h], PT4[:, h, :, 0:NS].unsqueeze(3).to_broadcast([128, 2, NS, NS]),
                   id16b.unsqueeze(1).to_broadcast([128, 2, NS, NS]), ALU.mult, PT.tk + [tkC], PTm.tk)
            obk = [psum() for _ in range(4)]
            for s in range(NS):
                vbs = vb[s % NKB]
                vbs3 = vbs.v("p (a e) -> p a e", a=2)
                dma("pool", vbs3, vc[l, s].rearrange("(a p) e -> p a e", p=128), w=vbs.tk)
                for h in range(4):
                    for a in range(2):
                        mm(obk[h][0][0:NS, 0:256], PTm5[:, h, a, s, :], vbs3[:, a, h * 256:(h + 1) * 256], s == 0 and a == 0,
                           s == NS - 1 and a == 1, PTm.tk + vbs.tk, [obk[h][1]])
            for h in range(4):
                ts("dve", Ob.ap[0:NS, h * 256:(h + 1) * 256], obk[h][0][0:NS, 0:256], rinv.ap[0:NS, h:h + 1], None,
                   ALU.mult, ALU.bypass, [obk[h][1]] + rinv.tk, Ob.tk)
            ps, tp = psum()
            pbf = ps[:, :].bitcast(BF16)
            for c in range(8):
                tr(pbf[:, c * 128:c * 128 + NS], Ob.ap[0:NS, c * 128:(c + 1) * 128], identb[0:NS, 0:NS], Ob.tk + [tkC], [tp])
            cp("act", OT3[:, :, 0:NS], pbf[:, 0:1024].rearrange("p (c t) -> p c t", c=8)[:, :, 0:NS], [tp], OT.tk)
            ps, tp = psum()
            for m in range(8):
                for kc in range(8):
                    mm(ps[:, m * NS:(m + 1) * NS], Wo3[:, kc, m * 128:(m + 1) * 128], OT3[:, kc, 0:NS], kc == 0, kc == 7,
                       [Wo.tk[m // 2]] + OT.tk, [tp])
            stt(XFs, XFs, ALPHA, ps[:, 0:8 * NS].rearrange("p (m s) -> p m s", m=8), ALU.mult, ALU.add, tkXs + [tp], tkXs)
            layernorm(lnt, NS, lambda c: xf(SB, c), lambda c: xtk(SB, c), False, lambda c: [xf(SB, c)], lambda c: xtk(SB, c),
                      lambda c: pkc(O_XAG + c), lambda c: pkc(O_XAB + c), AF.Identity)

            chk(l, 5)
            for j in range(16):
                t0 = j * 128
                blk = ("P", t0, 128)
                b5 = t0 // 512
                _d = CSETS[j % NCB]
                xb, qT, mx, nb, rsum, rinv, Pm, PT, Ob, OT, lnt = (_d[k] for k in
                                                                      ("xb", "qT", "mx", "nb", "rsum", "rinv", "Pm", "PT", "Ob", "OT", "lnt"))
                xb3 = xb.v("p (k t) -> p k t", k=8); qT3 = qT.v("p (c t) -> p c t", c=8)
                Pm3 = Pm.v("p (h m) -> p h m", h=4); PT4 = PT.v("p (h a t) -> p h a t", h=4, a=2)
                OT3 = OT.v("p (c t) -> p c t", c=8)
                import os
                Q_ = os.environ.get("QDBG", "abc")
                if "a" in Q_:
                    if os.environ.get("XBSRC") == "cst":
                        cp("pool", xb.ap, cstf[:, 0:1024], [tkC], xb.tk)
                    elif os.environ.get("XBSRC") == "2d":
                        for c in range(8):
                            cp("pool", xb3[:, c, :], XFp[:, c, t0:t0 + 128], [tkXp[c][j]], xb.tk)
                    else:
                        cp(os.environ.get("XBENG", "pool"), xb3, XFp[:, :, t0:t0 + 128], [tkXp[c][j] for c in range(8)], xb.tk)
                for m4 in range(2):
                    ps, tp = psum()
                    for mm_ in range(4):
                        m = m4 * 4 + mm_
                        for kc in range(8):
                            if "b" in Q_:
                                mm(ps[:, mm_ * 128:(mm_ + 1) * 128], Wq3[:, kc, m * 128:(m + 1) * 128], xb3[:, kc, :], kc == 0, kc == 7,
                                   [Wq.tk[m // 2]] + xb.tk, [tp])
                    if "c" in Q_:
                        cp("act", qT3[:, m4 * 4:m4 * 4 + 4, :], ps[:, :].rearrange("p (m t) -> p m t", m=4), [tp], qT.tk)
                chk(l, 4, 2)
                sb_ = []
                for hp in range(2):
                    ps, tp = psum()
                    for hh in range(2):
                        h = hp * 2 + hh
                        for dc in range(2):
                            mm(ps[:, hh * 256:(hh + 1) * 256], qT3[:, 2 * h + dc, :], KT3[:, 2 * h + dc, :], dc == 0, dc == 1,
                               qT.tk + KT.tk, [tp])
                    sb_.append((ps[:, 0:256], tp))
                    sb_.append((ps[:, 256:512], tp))
                chk(l, 4, 3)
                softmax_pv(128, sb_, blk)
                chk(l, 4, 4)
                for hp in range(2):
                    ps, tp = psum()
                    for hh in range(2):
                        h = hp * 2 + hh
                        for a in range(2):
                            mm(ps[:, hh * 256:(hh + 1) * 256], PT4[:, h, a, :], Vb3[:, a, h * 256:(h + 1) * 256], a == 0, a == 1,
                               PT.tk + Vb.tk, [tp])
                    tt("dve", Ob.ap[:, hp * 512:(hp + 1) * 512].rearrange("p (h d) -> p h d", h=2),
                       ps[:, :].rearrange("p (h d) -> p h d", h=2),
                       rinv.ap[:, hp * 2:hp * 2 + 2].unsqueeze(2).to_broadcast([128, 2, 256]), ALU.mult, [tp] + rinv.tk, Ob.tk)
                chk(l, 4, 5)
                ps, tp = psum()
                pbf = ps[:, :].bitcast(BF16)
                for c in range(8):
                    tr(pbf[:, c * 128:(c + 1) * 128], Ob.ap[:, c * 128:(c + 1) * 128], identb, Ob.tk + [tkC], [tp])
                cp("act", OT.ap, pbf[:, 0:1024], [tp], OT.tk)
                chk(l, 4, 6)
                for m4 in range(2):
                    ps, tp = psum()
                    for mm_ in range(4):
                        m = m4 * 4 + mm_
                        for kc in range(8):
                            mm(ps[:, mm_ * 128:(mm_ + 1) * 128], Wo3[:, kc, m * 128:(m + 1) * 128], OT3[:, kc, :], kc == 0, kc == 7,
                               [Wo.tk[m // 2]] + OT.tk, [tp])
                    xtks = [tkXp[m4 * 4 + q][j] for q in range(4)]
                    stt(XFp[:, m4 * 4:m4 * 4 + 4, t0:t0 + 128], XFp[:, m4 * 4:m4 * 4 + 4, t0:t0 + 128], ALPHA,
                        ps[:, :].rearrange("p (m t) -> p m t", m=4), ALU.mult, ALU.add, xtks + [tp], xtks)
                layernorm(lnt, 128, lambda c: xf(blk, c), lambda c: xtk(blk, c), False,
                          lambda c: [xf(blk, c)], lambda c: xtk(blk, c),
                          lambda c: pkc(O_XAG + c), lambda c: pkc(O_XAB + c), AF.Identity)

            chk(l, 6)
            WD = 256
            for gi in range(2):
                scr.release(0)
                Wup = scr.alloc(8 * 2 * 1408, BF16, ntk=22)
                Wup4 = Wup.v("p (k h m) -> p k h m", k=8, h=2)
                Wdn = scr.alloc(11 * 1024, BF16, ntk=8)
                Wdn3 = Wdn.v("p (k m) -> p k m", k=11)
                for hh in range(2):
                    for jj in range(11):
                        c0 = hh * 2816 + gi * 1408 + jj * 128
                        dma("pool", Wup4[:, :, hh, jj * 128:(jj + 1) * 128], w_up[l][:, :, c0:c0 + 128], w=[Wup.tk[hh * 11 + jj]])
                for m in range(8):
                    dma("pool", Wdn3[:, :, m * 128:(m + 1) * 128], w_dn[l][:, gi * 11:(gi + 1) * 11, m * 128:(m + 1) * 128],
                        w=[Wdn.tk[m]])
                dg3 = scr.alloc(22 * 3 * 128, BF16, ntk=22)
                dg34 = dg3.v("p (u k m) -> p u k m", u=22, k=3)

                def chid(u):
                    return (u // 11) * 22 + gi * 11 + (u % 11)

                for u in range(22):
                    tt("pool", dg34[:, u], identb.unsqueeze(1).to_broadcast([128, 3, 128]),
                       pkc(O_FFW + chid(u) * 3, 3).unsqueeze(2).to_broadcast([128, 3, 128]), ALU.mult, [tkC, tkPk], [dg3.tk[u]])
                xb = scr.alloc(8 * WD, BF16)
                xb3 = xb.v("p (k t) -> p k t", k=8)
                ur = scr.alloc(22 * (WD + 2), BF16, ntk=22)
                ur3 = ur.v("p (u t) -> p u t", u=22)
                gt = scr.alloc(11 * WD, BF16, ntk=11)
                gt3 = gt.v("p (u t) -> p u t", u=11)
                sgf = [scr.alloc(WD, F32) for _ in range(2)]
                urt = scr.alloc(44, F32)
                lnt = LNT(WD)
                for bi in range(T // WD):
                    t0 = bi * WD
                    blk = ("P", t0, WD)
                    b5 = t0 // 512
                    if gi == 0:
                        cp("pool", xb3, XFp[:, :, t0:t0 + WD], [t_ for c in range(8) for t_ in XT(c, t0, WD)], xb.tk)
                        for a in range(2):
                            dma("sp", xbd[:, 2 * bi + a], xb3[:, :, a * 128:(a + 1) * 128], r=xb.tk, w=[tkXbd[2 * bi + a]])
                    else:
                        for a in range(2):
                            dma("sp", xb3[:, :, a * 128:(a + 1) * 128], xbd[:, 2 * bi + a], r=[tkXbd[2 * bi + a]], w=xb.tk)
                    if bi == 0:
                        mset("pool", ur3[:, :, 0:2], 0.0, ur.tk)
                    else:
                        cp("pool", ur3[:, :, 0:2], ur3[:, :, WD:WD + 2], ur.tk, ur.tk)
                    for u in range(22):
                        ps, tp = psum()
                        for kc in range(8):
                            mm(ps[:, 0:WD], Wup4[:, kc, u // 11, (u % 11) * 128:(u % 11 + 1) * 128], xb3[:, kc, :], kc == 0, kc == 7,
                               [Wup.tk[u]] + xb.tk, [tp])
                        cp("act" if u % 2 else "dve", ur3[:, u, 2:WD + 2], ps[:, 0:WD], [tp], [ur.tk[u]])
                    if bi == T // WD - 1:
                        cp("act", urt.ap.rearrange("p (u k) -> p u k", u=22), ur3[:, :, WD:WD + 2], ur.tk, urt.tk)
                        urt3 = urt.ap.rearrange("p (u k) -> p u k", u=22)
                        dma("sp", ffncP[l][:, gi * 11:(gi + 1) * 11, :], urt3[:, 0:11, :], r=urt.tk)
                        dma("sp", ffncP[l][:, 22 + gi * 11:22 + (gi + 1) * 11, :], urt3[:, 11:22, :], r=urt.tk)
                    for jj in range(11):
                        pv, tv = psum()
                        pg, tg = psum()
                        for k in range(3):
                            mm(pv[:, 0:WD], dg34[:, jj, k, :], ur3[:, jj, k:k + WD], k == 0, k == 2, [ur.tk[jj], dg3.tk[jj]], [tv])
                        for k in range(3):
                            mm(pg[:, 0:WD], dg34[:, 11 + jj, k, :], ur3[:, 11 + jj, k:k + WD], k == 0, k == 2,
                               [ur.tk[11 + jj], dg3.tk[11 + jj]], [tg])
                        s_ = sgf[jj % 2]
                        act(s_.ap, pg[:, 0:WD], AF.Silu, [tg, tkPk], s_.tk, bias=pkc(O_FFBIAS + chid(11 + jj)))
                        stt(gt3[:, jj, :], pv[:, 0:WD], pkc(O_FFBIAS + chid(jj)), s_.ap, ALU.add, ALU.mult,
                            [tv, tkPk] + s_.tk, [gt.tk[jj]])
                    for m in range(8):
                        ps, tp = psum()
                        for jj in range(11):
                            mm(ps[:, 0:WD], Wdn3[:, jj, m * 128:(m + 1) * 128], gt3[:, jj, :], jj == 0, jj == 10,
                               [Wdn.tk[m], gt.tk[jj]], [tp])
                        if gi == 0:
                            stt(xf(blk, m), xf(blk, m), ALPHA, ps[:, 0:WD], ALU.mult, ALU.add, xtk(blk, m) + [tp], xtk(blk, m))
                        else:
                            tt("dve", xf(blk, m), xf(blk, m), ps[:, 0:WD], ALU.add, xtk(blk, m) + [tp], xtk(blk, m))
                    if gi == 1:
                        layernorm(lnt, WD, lambda c: xf(blk, c), lambda c: xtk(blk, c), False,
                                  lambda c: [xf(blk, c)], lambda c: xtk(blk, c),
                                  lambda c: pkc(O_FFG + c), lambda c: pkc(O_FFB + c), AF.Identity)
                xbs = Buf(XBS_t[:], tkXBS)
                xbs3 = XBS3
                if gi == 0:
                    cp("pool", xbs3, XFs, tkXs, xbs.tk)
                urs = scr.alloc(22 * NS, F32)
                urs3 = urs.v("p (u s) -> p u s", u=22)
                h3 = scr.alloc(22 * 3 * NS, BF16, ntk=2)
                h34 = h3.v("p (u k s) -> p u k s", u=22, k=3)
                ps, tp = psum()
                for u in range(22):
                    for kc in range(8):
                        mm(ps[:, u * NS:(u + 1) * NS], Wup4[:, kc, u // 11, (u % 11) * 128:(u % 11 + 1) * 128], xbs3[:, kc, :],
                           kc == 0, kc == 7, [Wup.tk[u]] + xbs.tk, [tp])
                cp("act", urs.ap, ps[:, 0:22 * NS], [tp], urs.tk)
                for hh in range(2):
                    c0 = hh * 22 + gi * 11
                    dma("sp", ffncS[l][:, c0:c0 + 11, 1, :], urs3[:, hh * 11:(hh + 1) * 11, :], r=urs.tk)
                    dma("sp", ffncS[l][:, c0:c0 + 11, 0, :], ffncT[l][:, c0:c0 + 11, 1, :])
                    dma("pool", h34[:, hh * 11:(hh + 1) * 11, 0:2, :], ffncT[l][:, c0:c0 + 11], w=[h3.tk[0]])
                cp("pool", h34[:, :, 2, :], urs3, urs.tk, [h3.tk[1]])
                pv, tv = psum()
                for u in range(22):
                    for k in range(3):
                        mm(pv[:, u * NS:(u + 1) * NS], dg34[:, u, k, :], h34[:, u, k, :], k == 0, k == 2, h3.tk + [dg3.tk[u]], [tv])
                sgs = scr.alloc(11 * NS, F32)
                gts = scr.alloc(11 * NS, BF16)
                gts3 = gts.v("p (u s) -> p u s", u=11)
                for jj in range(11):
                    act(sgs.ap[:, jj * NS:(jj + 1) * NS], pv[:, (11 + jj) * NS:(12 + jj) * NS], AF.Silu, [tv, tkPk], sgs.tk,
                        bias=pkc(O_FFBIAS + chid(11 + jj)))
                for jj in range(11):
                    stt(gts3[:, jj, :], pv[:, jj * NS:(jj + 1) * NS], pkc(O_FFBIAS + chid(jj)), sgs.ap[:, jj * NS:(jj + 1) * NS],
                        ALU.add, ALU.mult, [tv, tkPk] + sgs.tk, gts.tk)
                ps, tp = psum()
                for m in range(8):
                    for jj in range(11):
                        mm(ps[:, m * NS:(m + 1) * NS], Wdn3[:, jj, m * 128:(m + 1) * 128], gts3[:, jj, :], jj == 0, jj == 10,
                           [Wdn.tk[m]] + gts.tk, [tp])
                psv3 = ps[:, 0:8 * NS].rearrange("p (m s) -> p m s", m=8)
                if gi == 0:
                    stt(XFs, XFs, ALPHA, psv3, ALU.mult, ALU.add, tkXs + [tp], tkXs)
                else:
                    tt("dve", XFs, XFs, psv3, ALU.add, tkXs + [tp], tkXs)
                    layernorm(lnt, NS, lambda c: xf(SB, c), lambda c: xtk(SB, c), False, lambda c: [xf(SB, c)],
                              lambda c: xtk(SB, c), lambda c: pkc(O_FFG + c), lambda c: pkc(O_FFB + c), AF.Identity)

          except _Stop:
            break
        for c in range(8):
            for b in range(4):
                dma("sp", yT[:, c, b * 512:(b + 1) * 512], XFp[:, c, b * 512:(b + 1) * 512], r=XT(c, b * 512, 512))
        dma("sp", ysT, XFs, r=tkXs)
        S.emit(nc, st)
    return nc


def _wl(w):
    Lw, K, M = w.shape
    return np.ascontiguousarray(w.reshape(Lw, K // 128, 128, M).transpose(0, 2, 1, 3))


def _colT(v, nch):
    return v.reshape(v.shape[0], nch, 128).transpose(0, 2, 1)


def _build_pack(inp):
    pk = np.zeros((L, 128, NPK), np.float32)
    cw = inp["conf_conv_w"]
    pk[:, :, O_CONFW:O_CONFW + 248] = cw.reshape(L, 31, 8, 128).transpose(0, 3, 2, 1).reshape(L, 128, 248)
    pk[:, :, O_CONFB:O_CONFB + 8] = _colT(inp["conf_conv_b"], 8)
    pk[:, :, O_CLNG:O_CLNG + 8] = _colT(inp["conf_ln_g"], 8)
    pk[:, :, O_CLNB:O_CLNB + 8] = _colT(inp["conf_ln_b"], 8)
    sw = inp["ssd_conv_w"]
    pk[:, :, O_SSDW:O_SSDW + 48] = sw.reshape(L, 4, 12, 128).transpose(0, 3, 2, 1).reshape(L, 128, 48)
    pk[:, :, O_SSDB:O_SSDB + 12] = _colT(inp["ssd_conv_b"], 12)
    pk[:, :, O_RMSG:O_RMSG + 8] = _colT(inp["ssd_norm_g"], 8)
    for off, nm in ((O_MIXG, "ln_mix_g"), (O_MIXB, "ln_mix_b"), (O_XAG, "ln_xa_g"), (O_XAB, "ln_xa_b"),
                    (O_FFG, "ln_ffn_g"), (O_FFB, "ln_ffn_b")):
        pk[:, :, off:off + 8] = _colT(inp[nm], 8)
    fw = inp["ffn_conv_w"]
    pk[:, :, O_FFW:O_FFW + 132] = fw.reshape(L, 3, 44, 128).transpose(0, 3, 2, 1).reshape(L, 128, 132)
    pk[:, :, O_FFBIAS:O_FFBIAS + 44] = _colT(inp["ffn_conv_b"], 44)
    pk[:, 0:16, O_DTB] = inp["ssd_dt_bias"]
    pk[:, 0:16, O_ALOG] = inp["ssd_a_log"]
    pk[:, :, O_DROW:O_DROW + 16] = inp["ssd_d"][:, None, :]
    q = np.arange(128)
    for c in range(8):
        pk[:, :, O_DCOL + c] = inp["ssd_d"][:, 2 * c + q // 64]
    return pk


def _build_cst():
    c = np.zeros((128, NCST), np.float32)
    p = np.arange(128)
    c[:, C_ID:C_ID + 128] = np.eye(128)
    c[:, C_ULE:C_ULE + 128] = (p[:, None] <= p[None, :])
    c[:, C_MST:C_MST + 128] = (p[:, None] > p[None, :])
    c[:, C_ONE:C_ONE + 128] = 1.0
    c[:, C_ID16:C_ID16 + 256] = np.eye(16).reshape(1, 256)
    e = np.zeros((16, 8, 128), np.float32)
    for cc in range(8):
        for q in range(128):
            e[2 * cc + q // 64, cc, q] = 1.0
    c[0:16, C_EHP:C_EHP + 1024] = e.reshape(16, 1024)
    return c


_NC_CACHE = {}
STOP = None


def kernel(**inp):
    inp = {k: np.asarray(v) for k, v in inp.items()}
    f = np.float32
    shared = {
        "w_in": _wl(inp["w_in"]), "w_oA": _wl(inp["w_out"][:, 0:1024, :]), "w_oY": _wl(inp["w_out"][:, 1024:2048, :]),
        "wq": _wl(inp["xa_wq"]), "wk": _wl(inp["xa_wk"]), "wv": _wl(inp["xa_wv"]), "wo": _wl(inp["xa_wo"]),
        "w_up": _wl(inp["ffn_w_up"]), "w_dn": _wl(inp["ffn_w_down"]),
        "pack": _build_pack(inp), "cst": _build_cst(),
    }
    in_maps = []
    for i in range(NCORES):
        sl = slice(NS * i, NS * (i + 1))
        m = dict(shared)
        m["xT"] = np.ascontiguousarray(inp["x_prompt"][i].reshape(T, 8, 128).transpose(2, 1, 0))
        m["xsT"] = np.ascontiguousarray(inp["x_sample"][sl, 0].reshape(NS, 8, 128).transpose(2, 1, 0))
        m["memT"] = np.ascontiguousarray(inp["mem_prompt"][i].reshape(256, 8, 128).transpose(2, 1, 0))
        m["kcT"] = np.ascontiguousarray(inp["cache_mem_k"][:, sl].reshape(L, NS, 256, 8, 128).transpose(0, 1, 4, 3, 2))
        m["vc"] = np.ascontiguousarray(inp["cache_mem_v"][:, sl].reshape(L, NS, 256, 1024))
        m["confT"] = np.ascontiguousarray(inp["state_conf_conv"][:, sl].reshape(L, NS, 30, 8, 128).transpose(0, 4, 3, 2, 1))
        m["ssdcT"] = np.ascontiguousarray(inp["state_ssd_conv"][:, sl].reshape(L, NS, 3, 12, 128).transpose(0, 4, 3, 2, 1))
        m["ffncT"] = np.ascontiguousarray(inp["state_ffn_conv"][:, sl].reshape(L, NS, 2, 44, 128).transpose(0, 4, 3, 2, 1))
        m["ssdst"] = np.ascontiguousarray(inp["state_ssd"][:, sl].reshape(L, NS, 1024, 128))
        in_maps.append(m)
    if "nc" not in _NC_CACHE:
        _NC_CACHE["nc"] = build_program()
    nc = _NC_CACHE["nc"]
    res = run_bass_kernel_spmd(nc, in_maps, core_ids=list(range(NCORES)))
    R = res.results
    B = NCORES
    y_prompt = np.stack([R[i]["yT"].transpose(2, 1, 0).reshape(T, 1024) for i in range(B)]).astype(f)
    y_sample = np.concatenate([R[i]["ysT"].transpose(2, 1, 0).reshape(NS, 1, 1024) for i in range(B)]).astype(f)
    confP = np.stack([R[i]["confP"].transpose(0, 3, 2, 1).reshape(L, 30, 1024) for i in range(B)], axis=1)
    ssdcP = np.stack([R[i]["ssdcP"].transpose(0, 3, 2, 1).reshape(L, 3, 1536) for i in range(B)], axis=1)
    ssdP = np.stack([R[i]["ssdPT"].transpose(0, 2, 1).reshape(L, 16, 64, 128) for i in range(B)], axis=1)
    ffncP = np.stack([R[i]["ffncP"].transpose(0, 3, 2, 1).reshape(L, 2, NFF) for i in range(B)], axis=1)
    kPo = np.stack([R[i]["kP"].transpose(0, 3, 2, 1).reshape(L, 256, 4, 256) for i in range(B)], axis=1)
    vPo = np.stack([R[i]["vP"].reshape(L, 256, 4, 256) for i in range(B)], axis=1)
    confS = np.concatenate([R[i]["confS"].transpose(0, 4, 3, 2, 1).reshape(L, NS, 30, 1024) for i in range(B)], axis=1)
    ssdcS = np.concatenate([R[i]["ssdcS"].transpose(0, 4, 3, 2, 1).reshape(L, NS, 3, 1536) for i in range(B)], axis=1)
    ssdS = np.concatenate([R[i]["ssdS"].reshape(L, NS, 16, 64, 128) for i in range(B)], axis=1)
    ffncS = np.concatenate([R[i]["ffncS"].transpose(0, 4, 3, 2, 1).reshape(L, NS, 2, NFF) for i in range(B)], axis=1)
    outs = (y_prompt, y_sample, confP, ssdcP, ssdP, ffncP, kPo, vPo, confS, ssdcS, ssdS, ffncS)
    return tuple(np.ascontiguousarray(o, dtype=f) for o in outs)
```

```python
from contextlib import ExitStack
import numpy as np
import concourse.bass as bass
import concourse.mybir as mybir
from concourse.bass_utils import run_bass_kernel_spmd

F32 = mybir.dt.float32
BF16 = mybir.dt.bfloat16
AF = mybir.ActivationFunctionType
ALU = mybir.AluOpType
AX = mybir.AxisListType

NCORES = 8
L = 4
T = 2048
NS = 16
ALPHA = (2.0 * L) ** 0.25
EPS = 1e-5
D_IN = 4624
NFF = 5632

O_CONFW, O_CONFB, O_CLNG, O_CLNB = 0, 248, 256, 264
O_SSDW, O_SSDB, O_RMSG = 272, 320, 332
O_MIXG, O_MIXB, O_XAG, O_XAB, O_FFG, O_FFB = 340, 348, 356, 364, 372, 380
O_FFW, O_FFBIAS = 388, 520
O_DTB, O_ALOG, O_DROW, O_DCOL = 564, 565, 566, 582
NPK = 590
C_ID, C_ULE, C_MST, C_ONE, C_ID16, C_EHP = 0, 128, 256, 384, 512, 768
NCST = 768 + 1024

ENGS = ["pe", "act", "dve", "pool", "sp"]
ENGMAP = {"pe": "tensor", "act": "scalar", "dve": "vector", "pool": "gpsimd", "sp": "sync"}
NLANES = 32


class Tk:
    __slots__ = ("w", "r", "excl")

    def __init__(self, excl=False):
        self.w = None
        self.r = []
        self.excl = excl


class Sched:
    def __init__(self):
        self.ops = {e: [] for e in ENGS}
        self.lane_rr = 0
        self.lane_rr2 = [0, 0]
        self.lane_last = [None] * NLANES
        self.lane_seq = [0] * NLANES

    def _compress(self, deps):
        best = {}
        out = set()
        for d in deps:
            op = self.ops[d[0]][d[1]]
            if op["dma"]:
                key = ("L", op["lane"])
                if key not in best or op["ticket"] > best[key][0]:
                    best[key] = (op["ticket"], d)
            else:
                key = d[0]
                if key not in best or d[1] > best[key][0]:
                    best[key] = (d[1], d)
        for v in best.values():
            out.add(v[1])
        return out

    mute = False

    def add(self, eng, fn, r=(), w=(), dma=False, dur=100.0, lat=0.0):
        if self.mute:
            return None
        idx = len(self.ops[eng])
        me = (eng, idx)
        deps = set()
        if any(t.excl for t in r):
            w = list(w) + [t for t in r if t.excl]
            r = [t for t in r if not t.excl]
        for t in r:
            if t.w is not None:
                deps.add(t.w)
        for t in w:
            if t.w is not None:
                deps.add(t.w)
            deps.update(t.r)
        deps.discard(me)
        import sys as _sys
        op = dict(fn=fn, deps=None, dma=dma, signal=bool(dma), lane=None, ticket=None, dur=dur, lat=lat,
                  line=_sys._getframe(2).f_lineno)
        if dma:
            half = NLANES // 2
            k = 1 if eng == "pool" else 0
            lane = k * half + self.lane_rr2[k]
            self.lane_rr2[k] = (self.lane_rr2[k] + 1) % half
            op["lane"] = lane
            if self.lane_last[lane] is not None:
                deps.add(self.lane_last[lane])
            self.lane_last[lane] = me
            self.lane_seq[lane] += 1
            op["ticket"] = 16 * self.lane_seq[lane]
        op["raw"] = deps
        self.ops[eng].append(op)
        for t in r:
            t.r.append(me)
        for t in w:
            t.w = me
            t.r = []
        return me

    def schedule(self):
        import heapq
        SYNC = 120.0
        ndeps = {}
        users = {}
        for e in ENGS:
            for i, op in enumerate(self.ops[e]):
                ndeps[(e, i)] = len(op["raw"])
                for d in op["raw"]:
                    users.setdefault(d, []).append((e, i))
        ready = {e: [] for e in ENGS}
        readyt = {}
        for e in ENGS:
            for i, op in enumerate(self.ops[e]):
                if not op["raw"]:
                    heapq.heappush(ready[e], i)
                    readyt[(e, i)] = 0.0
        free = {e: 0.0 for e in ENGS}
        busy = {e: False for e in ENGS}
        order = {e: [] for e in ENGS}
        events = []
        now = 0.0
        remaining = sum(len(self.ops[e]) for e in ENGS)

        def try_issue(e, now):
            if busy[e] or not ready[e]:
                return
            i = heapq.heappop(ready[e])
            op = self.ops[e][i]
            order[e].append(i)
            busy[e] = True
            t_free = now + op["dur"]
            heapq.heappush(events, (t_free, 1, e, -1))
            heapq.heappush(events, (t_free + op["lat"] + SYNC, 0, e, i))

        for e in ENGS:
            try_issue(e, 0.0)
        while events:
            t, kind, e, i = heapq.heappop(events)
            now = t
            if kind == 1:
                busy[e] = False
                try_issue(e, now)
            else:
                remaining -= 1
                for u in users.get((e, i), ()):
                    ndeps[u] -= 1
                    if ndeps[u] == 0:
                        heapq.heappush(ready[u[0]], u[1])
                        try_issue(u[0], now)
        assert remaining == 0, ("scheduler: unscheduled ops (cycle?)", remaining)
        self.est_ns = now
        return order

    def emit(self, nc, stack, reorder=True):
        if reorder:
            order = self.schedule()
        else:
            order = {e: list(range(len(self.ops[e]))) for e in ENGS}
        pos = {}
        for e in ENGS:
            for k, i in enumerate(order[e]):
                pos[(e, i)] = k
        for e in ENGS:
            for i in order[e]:
                op = self.ops[e][i]
                best = {}
                for d in op["raw"]:
                    dop = self.ops[d[0]][d[1]]
                    if dop["dma"]:
                        key = ("L", dop["lane"])
                        val = dop["ticket"]
                    else:
                        if e == "pe" and d[0] == "pe" and not op["dma"]:
                            assert pos[d] < pos[(e, i)]
                            continue
                        key = d[0]
                        val = pos[d]
                        if d[0] == e:
                            assert pos[d] < pos[(e, i)], "same-engine dependency order violated"
                    if key not in best or val > best[key][0]:
                        best[key] = (val, d)
                op["deps"] = [v[1] for v in best.values()]
                for d in op["deps"]:
                    self.ops[d[0]][d[1]]["signal"] = True
        esem = {e: stack.enter_context(nc.semaphore("s_" + e)) for e in ENGS}
        lsem = [stack.enter_context(nc.semaphore("l_%d" % i)) for i in range(NLANES)]
        for e in ENGS:
            cnt = 0
            for i in order[e]:
                op = self.ops[e][i]
                if op["dma"]:
                    continue
                if op["signal"]:
                    cnt += 1
                    op["ticket"] = cnt

        def sem_of(dep):
            dop = self.ops[dep[0]][dep[1]]
            if dop["dma"]:
                return lsem[dop["lane"]], dop["ticket"]
            return esem[dep[0]], dop["ticket"]

        with nc.Block() as block:
            for e in ENGS:
                ops = [self.ops[e][i] for i in order[e]]

                def body(eng, e=e, ops=ops):
                    waited = {}
                    for op in ops:
                        for dep in sorted(op["deps"]):
                            sem, val = sem_of(dep)
                            if waited.get(sem.num, 0) >= val:
                                continue
                            eng.wait_ge(sem, val)
                            waited[sem.num] = val
                        inst = op["fn"](eng)
                        if op["signal"]:
                            if op["dma"]:
                                inst.then_inc(lsem[op["lane"]], 16)
                            else:
                                inst.then_inc(esem[e], 1)
                    if e == "sp":
                        for ln in range(NLANES):
                            last = self.lane_last[ln]
                            if last is not None:
                                sem, val = sem_of(last)
                                if waited.get(sem.num, 0) < val:
                                    eng.wait_ge(sem, val)
                                    waited[sem.num] = val

                getattr(block, ENGMAP[e])(body)


class Buf:
    def __init__(self, ap, tk):
        self.ap = ap
        self.tk = tk

    def v(self, pat, **kw):
        return self.ap.rearrange(pat, **kw)


class Scratch:
    def __init__(self, tensor, nbytes):
        self.t = tensor
        self.nbytes = nbytes
        self.top = 0
        self.hist = []

    def alloc(self, nelem, dtype, ntk=1):
        esz = 4 if dtype == F32 else 2
        size = (nelem * esz + 31) // 32 * 32
        off = self.top
        assert off + size <= self.nbytes, ("scratch overflow", off, size, self.nbytes)
        self.top = off + size
        tks = [Tk() for _ in range(ntk)]
        inherit = []
        keep = []
        for (o, s, ts) in self.hist:
            if o < off + size and off < o + s:
                for t in ts:
                    if t.w is not None:
                        inherit.append(t.w)
                    inherit.extend(t.r)
                if not (off <= o and o + s <= off + size):
                    keep.append((o, s, ts))
            else:
                keep.append((o, s, ts))
        self.hist = keep
        inherit = list(set(inherit))
        for t in tks:
            t.r = list(inherit)
        self.hist.append((off, size, tks))
        ap = self.t[:, off // 2:(off + size) // 2]
        if dtype == F32:
            ap = ap.bitcast(F32)[:, 0:nelem]
        else:
            ap = ap[:, 0:nelem]
        return Buf(ap, tks)

    def mark(self):
        return self.top

    def release(self, m):
        self.top = m


class _Stop(Exception):
    pass


def build_program(stop=None, only=None):
    nc = bass.Bass("TRN2", target_bir_lowering=False)

    def chk(l, ph, sub=0):
        if stop is not None and (l, ph, sub) > tuple(stop) + (0,) * (3 - len(stop)):
            S.mute = False
            raise _Stop()
        S.mute = only is not None and ph not in only
        import os
        if os.environ.get("KVSKIP") == "2" and ph == 4 and sub < 1:
            S.mute = True

    S = Sched()

    def din(name, shape):
        return nc.dram_tensor(name, list(shape), F32, kind="ExternalInput").ap()

    def dout(name, shape):
        return nc.dram_tensor(name, list(shape), F32, kind="ExternalOutput").ap()

    xT = din("xT", [128, 8, T]); xsT = din("xsT", [128, 8, NS]); memT = din("memT", [128, 8, 256])
    kcT = din("kcT", [L, NS, 128, 8, 256]); vc = din("vc", [L, NS, 256, 1024])
    confT = din("confT", [L, 128, 8, 30, NS]); ssdcT = din("ssdcT", [L, 128, 12, 3, NS])
    ffncT = din("ffncT", [L, 128, 44, 2, NS]); ssdst = din("ssdst", [L, NS, 1024, 128])
    w_in = din("w_in", [L, 128, 8, D_IN]); w_oA = din("w_oA", [L, 128, 8, 1024]); w_oY = din("w_oY", [L, 128, 8, 1024])
    wq = din("wq", [L, 128, 8, 1024]); wk = din("wk", [L, 128, 8, 1024]); wv = din("wv", [L, 128, 8, 1024])
    wo = din("wo", [L, 128, 8, 1024]); w_up = din("w_up", [L, 128, 8, NFF]); w_dn = din("w_dn", [L, 128, 22, 1024])
    pack = din("pack", [L, 128, NPK]); cst = din("cst", [128, NCST])

    yT = dout("yT", [128, 8, T]); ysT = dout("ysT", [128, 8, NS])
    confP = dout("confP", [L, 128, 8, 30]); ssdcP = dout("ssdcP", [L, 128, 12, 3])
    ssdPT = dout("ssdPT", [L, 128, 1024]); ffncP = dout("ffncP", [L, 128, 44, 2])
    kP = dout("kP", [L, 128, 8, 256]); vP = dout("vP", [L, 256, 1024])
    confS = dout("confS", [L, 128, 8, 30, NS]); ssdcS = dout("ssdcS", [L, 128, 12, 3, NS])
    ssdS = dout("ssdS", [L, NS, 1024, 128]); ffncS = dout("ffncS", [L, 128, 44, 2, NS])

    with ExitStack() as st:
        def sb(name, shape, dt):
            return st.enter_context(nc.sbuf_tensor(name, shape, dt))

        XFp_t = sb("XFp", [128, 8 * T], F32)
        XFs_t = sb("XFs", [128, 8 * NS], F32)
        XFp = XFp_t[:].rearrange("p (c t) -> p c t", c=8)
        XFs = XFs_t[:].rearrange("p (c t) -> p c t", c=8)
        tkXp = [[Tk() for _ in range(16)] for _ in range(8)]

        def XT(c, t0, W):
            return [tkXp[c][j] for j in range(t0 // 128, (t0 + W - 1) // 128 + 1)]
        tkXs = [Tk() for _ in range(8)]
        cstf = sb("cstf", [128, NCST], F32)
        cstb = sb("cstb", [128, 768], BF16)
        pk = sb("pk", [128, NPK], F32)
        acol = sb("acol", [16, 1], F32)
        tkC, tkPk, tkAcol = Tk(), Tk(), Tk()
        SCRB = 130 * 1024
        scr_t = sb("scr", [128, SCRB // 2], BF16)
        scr = Scratch(scr_t, SCRB)
        xbd = nc.dram_tensor("xbd", [128, 16, 8, 128], BF16).ap()
        tkXbd = [Tk() for _ in range(16)]
        XBS_t = sb("XBS", [128, 8 * NS], BF16)
        XBS3 = XBS_t[:].rearrange("p (k s) -> p k s", k=8)
        tkXBS = [Tk()]
        psb = [st.enter_context(nc.psum_tensor("psb%d" % i, [128, 512], F32)) for i in range(8)]
        tkPS = [Tk(excl=True) for _ in range(8)]
        psrr = [0]

        def psum():
            i = psrr[0]
            psrr[0] = (i + 1) % 8
            return psb[i], tkPS[i]

        identf = cstf[:, C_ID:C_ID + 128]
        ulef = cstf[:, C_ULE:C_ULE + 128]
        onesf = cstf[:, C_ONE:C_ONE + 128]
        ehpf = cstf[0:16, C_EHP:C_EHP + 1024].rearrange("p (c q) -> p c q", c=8)
        identb = cstb[:, 0:128]
        mstb = cstb[:, 256:384]
        onesb = cstb[:, 384:512]
        id16b = cstb[:, 512:768].rearrange("p (a b) -> p a b", a=16)

        def fsz(ap):
            n = 1
            for d in ap.shape[1:]:
                n *= d
            return n

        def mm(out, lhsT, rhs, start, stop, r, w):
            n = fsz(rhs)
            d = 25.0 + max(n, 64) / 2.0
            if rhs.dtype == F32:
                d *= 4
            S.add("pe", lambda e: e.matmul(out, lhsT=lhsT, rhs=rhs, start=start, stop=stop), r=r, w=w, dur=d)

        def tr(out, in_, ident, r, w):
            S.add("pe", lambda e: e.transpose(out=out, in_=in_, identity=ident), r=r, w=w, dur=90.0)

        def act(out, in_, func, r, w, bias=None, scale=None, accum_out=None):
            kw = {}
            if bias is not None:
                kw["bias"] = bias
            if scale is not None:
                kw["scale"] = scale
            if accum_out is not None:
                kw["accum_out"] = accum_out
            S.add("act", lambda e: e.activation(out=out, in_=in_, func=func, **kw), r=r, w=w,
                  dur=230.0 + fsz(out) / 1.2)

        def tt(eng, out, in0, in1, op, r, w):
            S.add(eng, lambda e: e.tensor_tensor(out=out, in0=in0, in1=in1, op=op), r=r, w=w,
                  dur=(80.0 + 1.6 * fsz(out)) if eng == "dve" else (200.0 + 2.2 * fsz(out)))

        def ts(eng, out, in0, s1, s2, op0, op1, r, w):
            S.add(eng, lambda e: e.tensor_scalar(out=out, in0=in0, scalar1=s1, scalar2=s2, op0=op0, op1=op1), r=r, w=w,
                  dur=(80.0 + 1.05 * fsz(out)) if eng == "dve" else (200.0 + 2.0 * fsz(out)))

        def stt(out, in0, scalar, in1, op0, op1, r, w, accum_out=None):
            kw = {}
            if accum_out is not None:
                kw["accum_out"] = accum_out
            S.add("dve", lambda e: e.scalar_tensor_tensor(out=out, in0=in0, scalar=scalar, in1=in1,
                                                          op0=op0, op1=op1, **kw), r=r, w=w, dur=80.0 + 1.1 * fsz(out))

        def cp(eng, out, in_, r, w):
            if eng == "act":
                act(out, in_, AF.Copy, r, w)
            else:
                S.add(eng, lambda e: e.tensor_copy(out=out, in_=in_), r=r, w=w,
                      dur=(80.0 + 1.05 * fsz(out)) if eng == "dve" else (200.0 + 1.7 * fsz(out)))

        def red(out, in_, op, r, w):
            S.add("dve", lambda e: e.tensor_reduce(out=out, in_=in_, axis=AX.X, op=op), r=r, w=w, dur=80.0 + 1.05 * fsz(in_))

        def recip(out, in_, r, w):
            S.add("dve", lambda e: e.reciprocal(out=out, in_=in_), r=r, w=w, dur=80.0 + 8.4 * fsz(out))

        def scan(out, d0, d1, r, w):
            S.add("dve", lambda e: e.tensor_tensor_scan(out=out, data0=d0, data1=d1, initial=0.0,
                                                          op0=ALU.mult, op1=ALU.add), r=r, w=w, dur=80.0 + 2.1 * fsz(out))

        def mset(eng, ap, val, w):
            S.add(eng, lambda e: e.memset(ap, val), w=w, dur=100.0 + fsz(ap))

        swq = []
        SWLIM = 600

        def dma(q, out, in_, r=(), w=()):
            r = list(r)
            if q == "pool":
                def nd(ap):
                    n = ap.shape[0]
                    for d in ap.shape[1:-1]:
                        n *= d
                    return max(1, n // 16)
                n = max(nd(out), nd(in_))
                while swq and sum(x[1] for x in swq) + n > SWLIM:
                    old = swq.pop(0)
                    t = Tk()
                    t.w = old[0]
                    r.append(t)
                nbytes = fsz(out) * out.shape[0] * (4 if out.dtype == F32 else 2)
                me = S.add(q, lambda e: e.dma_start(out=out, in_=in_), r=r, w=w, dma=True, dur=1200.0,
                           lat=2000.0 + nbytes / 60.0)
                if me is not None:
                    swq.append((me, n))
                return
            nbytes = fsz(out) * out.shape[0] * (4 if out.dtype == F32 else 2)
            S.add(q, lambda e: e.dma_start(out=out, in_=in_), r=r, w=w, dma=True, dur=100.0,
                  lat=2000.0 + nbytes / 60.0)

        dma("sp", cstf[:], cst, w=[tkC])
        for c in range(8):
            for b in range(4):
                dma("sp", XFp[:, c, b * 512:(b + 1) * 512], xT[:, c, b * 512:(b + 1) * 512], w=XT(c, b * 512, 512))
        dma("sp", XFs, xsT, w=tkXs)
        cp("pool", cstb[:, 0:384], cstf[:, 0:384], [tkC], [tkC])
        cp("pool", cstb[:, 512:768], cstf[:, C_ID16:C_ID16 + 256], [tkC], [tkC])
        mset("pool", cstb[:, 384:512], 1.0 / 1024.0, [tkC])

        def xf(blk, c):
            if blk[0] == "S":
                return XFs[:, c, :]
            return XFp[:, c, blk[1]:blk[1] + blk[2]]

        def xtk(blk, c):
            if blk[0] == "S":
                return [tkXs[c]]
            return XT(c, blk[1], blk[2])

        def bw(blk):
            return NS if blk[0] == "S" else blk[2]

        def pkc(off, n=1):
            return pk[:, off:off + n]

        def load_w(dst3, src3, tks, step):
            M = dst3.shape[2]
            i = 0
            for m0 in range(0, M, step):
                m1 = min(M, m0 + step)
                dma("pool", dst3[:, :, m0:m1], src3[:, :, m0:m1], w=[tks[i]])
                i += 1

        class LNT:
            def __init__(self, W):
                self.W = W
                self.sq = scr.alloc(8 * W, BF16)
                self.sbb = scr.alloc(8 * W, BF16)
                self.small = [scr.alloc(W, F32) for _ in range(4)]
                self.t1 = [scr.alloc(W, F32) for _ in range(2)]
                self.t2 = [scr.alloc(W, F32) for _ in range(2)]
                self.i = 0

        def layernorm(lt, W, src, src_tk, src_is_bf, dst, dst_tk, gcol, bcol, func, rms=False):
            sq3 = lt.sq.ap.rearrange("p (c w) -> p c w", c=8)
            sb3 = lt.sbb.ap.rearrange("p (c w) -> p c w", c=8)
            for c in range(8):
                act(sq3[:, c, 0:W], src(c), AF.Square, src_tk(c), lt.sq.tk)
                if not src_is_bf and not rms:
                    cp("pool", sb3[:, c, 0:W], src(c), src_tk(c), lt.sbb.tk)
            psq, tq = psum()
            for c in range(8):
                mm(psq[:, 0:W], onesb, sq3[:, c, 0:W], c == 0, c == 7, [tkC] + lt.sq.tk, [tq])
            mean, m2, rstd, nmr = [b for b in lt.small]
            if not rms:
                psm, tm = psum()
                for c in range(8):
                    rhs = src(c) if src_is_bf else sb3[:, c, 0:W]
                    rtk = src_tk(c) if src_is_bf else lt.sbb.tk
                    mm(psm[:, 0:W], onesb, rhs, c == 0, c == 7, [tkC] + rtk, [tm])
                act(mean.ap[:, 0:W], psm[:, 0:W], AF.Copy, [tm], mean.tk)
                tt("pool", m2.ap[:, 0:W], mean.ap[:, 0:W], mean.ap[:, 0:W], ALU.mult, mean.tk, m2.tk)
                tt("dve", m2.ap[:, 0:W], psq[:, 0:W], m2.ap[:, 0:W], ALU.subtract, [tq] + m2.tk, m2.tk)
                act(rstd.ap[:, 0:W], m2.ap[:, 0:W], AF.Ln, m2.tk, rstd.tk, bias=EPS)
            else:
                act(rstd.ap[:, 0:W], psq[:, 0:W], AF.Ln, [tq], rstd.tk, bias=EPS)
            act(rstd.ap[:, 0:W], rstd.ap[:, 0:W], AF.Exp, rstd.tk, rstd.tk, scale=-0.5)
            if not rms:
                stt(nmr.ap[:, 0:W], mean.ap[:, 0:W], -1.0, rstd.ap[:, 0:W], ALU.mult, ALU.mult,
                    mean.tk + rstd.tk, nmr.tk)
            for c in range(8):
                i = lt.i
                lt.i = (i + 1) % 2
                t1, t2 = lt.t1[i], lt.t2[i]
                tt("dve", t1.ap[:, 0:W], src(c), rstd.ap[:, 0:W], ALU.mult, src_tk(c) + rstd.tk, t1.tk)
                if not rms:
                    tt("pool", t2.ap[:, 0:W], t1.ap[:, 0:W], nmr.ap[:, 0:W], ALU.add, t1.tk + nmr.tk, t2.tk)
                    tin, ttk = t2, t2.tk
                else:
                    tin, ttk = t1, t1.tk
                outs = dst(c)
                for oi, o in enumerate(outs):
                    if bcol is not None:
                        act(o, tin.ap[:, 0:W], func, ttk + [tkPk], dst_tk(c), bias=bcol(c), scale=gcol(c))
                    else:
                        act(o, tin.ap[:, 0:W], func, ttk + [tkPk], dst_tk(c), scale=gcol(c))

        PB512 = [("P", b * 512, 512) for b in range(4)]
        SB = ("S", 0, NS)

        for l in range(L):
          try:
            dma("sp", pk[:], pack[l], w=[tkPk])
            act(acol[:], pk[0:16, O_ALOG:O_ALOG + 1], AF.Exp, [tkPk], [tkAcol])
            ts("dve", acol[:], acol[:], -1.0, None, ALU.mult, ALU.bypass, [tkAcol], [tkAcol])

            chk(l, 0)
            scr.release(0)
            CO = scr.alloc(8 * T, BF16, ntk=32)
            COs = scr.alloc(8 * NS, BF16, ntk=8)
            CO3 = CO.v("p (c t) -> p c t", c=8)
            COs3 = COs.v("p (c t) -> p c t", c=8)
            mA = scr.mark()
            XB = scr.alloc(8 * T, BF16, ntk=32)
            XB3 = XB.v("p (c t) -> p c t", c=8)
            XBs3 = XBS3
            abuf = [scr.alloc(30 + T, BF16, ntk=5) for _ in range(2)]
            hs = [scr.alloc(31 * NS, BF16, ntk=2) for _ in range(2)]
            diag = [scr.alloc(31 * 128, BF16) for _ in range(2)]
            Wc = [scr.alloc(2 * 8 * 128, BF16, ntk=2) for _ in range(2)]
            sg = [scr.alloc(512, F32) for _ in range(2)]
            anew = [scr.alloc(NS, F32) for _ in range(2)]
            tailb = [scr.alloc(30, F32) for _ in range(2)]
            for c in range(8):
                for b in range(4):
                    cp("pool", XB3[:, c, b * 512:(b + 1) * 512], XFp[:, c, b * 512:(b + 1) * 512],
                       XT(c, b * 512, 512), [XB.tk[c * 4 + b]])
            cp("pool", XBs3, XFs, tkXs, tkXBS)
            for j in range(16):
                dma("sp", xbd[:, j], XB3[:, :, j * 128:(j + 1) * 128], r=[XB.tk[c * 4 + j // 4] for c in range(8)], w=[tkXbd[j]])
            for i in range(2):
                mset("pool", abuf[i].ap[:, 0:30], 0.0, [abuf[i].tk[0]])
            sgi = 0
            for c in range(8):
                i = c % 2
                Wc4 = Wc[i].v("p (h k m) -> p h k m", h=2, k=8)
                dma("pool", Wc4[:, 0], w_in[l][:, :, c * 128:(c + 1) * 128], w=[Wc[i].tk[0]])
                dma("pool", Wc4[:, 1], w_in[l][:, :, 1024 + c * 128:1024 + (c + 1) * 128], w=[Wc[i].tk[1]])
                dg3 = diag[i].v("p (k m) -> p k m", k=31)
                tt("pool", dg3, identb.unsqueeze(1).to_broadcast([128, 31, 128]),
                   pkc(O_CONFW + c * 31, 31).unsqueeze(2).to_broadcast([128, 31, 128]), ALU.mult,
                   [tkC, tkPk], diag[i].tk)
                hs3 = hs[i].v("p (k s) -> p k s", k=31)
                dma("pool", hs3[:, 0:30, :], confT[l][:, c], w=[hs[i].tk[0]])
                dma("sp", confS[l][:, c, 0:29, :], confT[l][:, c, 1:30, :])
                bcol = pkc(O_CONFB + c)
                for b in range(4):
                    psv, tv = psum()
                    pg, tg = psum()
                    for kc in range(8):
                        mm(psv[:, :], Wc4[:, 0, kc, :], XB3[:, kc, b * 512:(b + 1) * 512], kc == 0, kc == 7,
                           [Wc[i].tk[0], XB.tk[kc * 4 + b]], [tv])
                    for kc in range(8):
                        mm(pg[:, :], Wc4[:, 1, kc, :], XB3[:, kc, b * 512:(b + 1) * 512], kc == 0, kc == 7,
                           [Wc[i].tk[1], XB.tk[kc * 4 + b]], [tg])
                    s_ = sg[sgi]
                    sgi = (sgi + 1) % 2
                    act(s_.ap, pg[:, :], AF.Sigmoid, [tg], s_.tk)
                    tt("dve", abuf[i].ap[:, 30 + b * 512:30 + (b + 1) * 512], psv[:, :], s_.ap, ALU.mult,
                       [tv] + s_.tk, [abuf[i].tk[1 + b]])
                psv, tv = psum()
                pg, tg = psum()
                for kc in range(8):
                    mm(psv[:, 0:NS], Wc4[:, 0, kc, :], XBs3[:, kc, :], kc == 0, kc == 7, [Wc[i].tk[0]] + tkXBS, [tv])
                for kc in range(8):
                    mm(pg[:, 0:NS], Wc4[:, 1, kc, :], XBs3[:, kc, :], kc == 0, kc == 7, [Wc[i].tk[1]] + tkXBS, [tg])
                s_ = sg[sgi]
                sgi = (sgi + 1) % 2
                act(s_.ap[:, 0:NS], pg[:, 0:NS], AF.Sigmoid, [tg], s_.tk)
                tt("dve", anew[i].ap, psv[:, 0:NS], s_.ap[:, 0:NS], ALU.mult, [tv] + s_.tk, anew[i].tk)
                cp("pool", hs3[:, 30, :], anew[i].ap, anew[i].tk, [hs[i].tk[1]])
                dma("sp", confS[l][:, c, 29, :], anew[i].ap, r=anew[i].tk)
                for b in range(4):
                    pc, tc = psum()
                    rt = [abuf[i].tk[1 + b], abuf[i].tk[b]] + diag[i].tk
                    for k in range(31):
                        mm(pc[:, :], dg3[:, k, :], abuf[i].ap[:, b * 512 + k:b * 512 + k + 512], k == 0, k == 30, rt, [tc])
                    act(CO3[:, c, b * 512:(b + 1) * 512], pc[:, :], AF.Identity, [tc, tkPk], [CO.tk[c * 4 + b]], bias=bcol)
                pc, tc = psum()
                for k in range(31):
                    mm(pc[:, 0:NS], dg3[:, k, :], hs3[:, k, :], k == 0, k == 30, hs[i].tk + diag[i].tk, [tc])
                act(COs3[:, c, :], pc[:, 0:NS], AF.Identity, [tc, tkPk], [COs.tk[c]], bias=bcol)
                cp("act", tailb[i].ap, abuf[i].ap[:, T:T + 30], [abuf[i].tk[4]], tailb[i].tk)
                dma("sp", confP[l][:, c, :], tailb[i].ap, r=tailb[i].tk)

            chk(l, 1)
            scr.release(mA)
            WoA = scr.alloc(8 * 1024, BF16, ntk=4)
            WoA3 = WoA.v("p (k m) -> p k m", k=8)
            load_w(WoA3, w_oA[l], WoA.tk, 256)
            lnt = LNT(512)
            for blk in PB512 + [SB]:
                W = bw(blk)
                if blk[0] == "S":
                    srcf = lambda c: COs3[:, c, :]
                    stk = lambda c: [COs.tk[c]]
                else:
                    b = blk[1] // 512
                    srcf = lambda c, b=b: CO3[:, c, b * 512:(b + 1) * 512]
                    stk = lambda c, b=b: [CO.tk[c * 4 + b]]
                layernorm(lnt, W, srcf, stk, True, lambda c: [srcf(c)], stk,
                          lambda c: pkc(O_CLNG + c), lambda c: pkc(O_CLNB + c), AF.Silu)
                for m in range(8):
                    ps, tp = psum()
                    for kc in range(8):
                        mm(ps[:, 0:W], WoA3[:, kc, m * 128:(m + 1) * 128], srcf(kc), kc == 0, kc == 7,
                           [WoA.tk[m // 2]] + stk(kc), [tp])
                    stt(xf(blk, m), xf(blk, m), ALPHA, ps[:, 0:W], ALU.mult, ALU.add, xtk(blk, m) + [tp], xtk(blk, m))

            chk(l, 2)
            scr.release(0)
            Wx = scr.alloc(8 * 1536, BF16, ntk=6)
            Wz = scr.alloc(8 * 1024, BF16, ntk=4)
            Wdt = scr.alloc(8 * 16, BF16)
            WoY = scr.alloc(8 * 1024, BF16, ntk=4)
            Wx3 = Wx.v("p (k m) -> p k m", k=8)
            Wz3 = Wz.v("p (k m) -> p k m", k=8)
            Wdt3 = Wdt.v("p (k m) -> p k m", k=8)
            WoY3 = WoY.v("p (k m) -> p k m", k=8)
            load_w(Wx3, w_in[l][:, :, 3072:4608], Wx.tk, 256)
            load_w(Wdt3, w_in[l][:, :, 4608:4624], Wdt.tk, 16)
            load_w(Wz3, w_in[l][:, :, 2048:3072], Wz.tk, 256)
            load_w(WoY3, w_oY[l], WoY.tk, 256)
            dg4 = scr.alloc(12 * 4 * 128, BF16, ntk=12)
            dg44 = dg4.v("p (u k m) -> p u k m", u=12, k=4)
            for u in range(12):
                tt("pool", dg44[:, u], identb.unsqueeze(1).to_broadcast([128, 4, 128]),
                   pkc(O_SSDW + u * 4, 4).unsqueeze(2).to_broadcast([128, 4, 128]), ALU.mult, [tkC, tkPk], [dg4.tk[u]])
            lnt = LNT(256)
            mW = scr.mark()
            hT = scr.alloc(1024, F32, ntk=2)
            hTb = scr.alloc(1024, BF16, ntk=2)
            mset("pool", hT.ap, 0.0, hT.tk)
            mset("pool", hTb.ap, 0.0, hTb.tk)
            xb = scr.alloc(8 * 128, BF16)
            xb3 = xb.v("p (k t) -> p k t", k=8)
            raw = scr.alloc(12 * 131, BF16)
            raw3 = raw.v("p (u t) -> p u t", u=12)
            xact = scr.alloc(12 * 128, BF16)
            xact3 = xact.v("p (u t) -> p u t", u=12)
            rawt = scr.alloc(36, F32)
            fm = [scr.alloc(128, F32) for _ in range(4)]
            tok4 = scr.alloc(64, F32)
            tok43 = tok4.v("p (a h) -> p a h", a=4)
            e1 = scr.alloc(16, F32); dte = scr.alloc(16, F32); cdec = scr.alloc(16, F32)
            xsT_ = scr.alloc(1024, BF16); BtT = scr.alloc(256, BF16)
            Rb = scr.alloc(16 * 128, BF16); LT = scr.alloc(16 * 128, BF16); scT = Rb
            Gm = scr.alloc(256, BF16)
            dtx = scr.alloc(1024, BF16); dtxe = scr.alloc(1024, BF16); Dxs = scr.alloc(1024, BF16)
            yo = scr.alloc(1024, F32); zs = scr.alloc(1024, BF16)
            yn = scr.alloc(1024, BF16); Yfm = scr.alloc(8 * 256, BF16); junk = yn
            Yfm3p = Yfm.v("p (c t) -> p c t", c=8)
            ss = scr.alloc(1, F32); rs1 = scr.alloc(1, F32)

            def bc_h(ap16, lo, n):
                return ap16[:, lo:lo + n].unsqueeze(2).to_broadcast([128, n, 64])

            for j in range(16):
                t0 = j * 128
                blk = ("P", t0, 128)
                b5 = t0 // 512
                dma("sp", xb3, xbd[:, j], r=[tkXbd[j]], w=xb.tk)
                if j == 0:
                    mset("pool", raw3[:, :, 0:3], 0.0, raw.tk)
                else:
                    cp("pool", raw3[:, :, 0:3], raw3[:, :, 128:131], raw.tk, raw.tk)
                for g4 in range(3):
                    ps, tp = psum()
                    for uu in range(4):
                        u = g4 * 4 + uu
                        for kc in range(8):
                            mm(ps[:, uu * 128:(uu + 1) * 128], Wx3[:, kc, u * 128:(u + 1) * 128], xb3[:, kc, :],
                               kc == 0, kc == 7, [Wx.tk[u // 2]] + xb.tk, [tp])
                    cp("act", raw3[:, g4 * 4:g4 * 4 + 4, 3:131], ps[:, :].rearrange("p (u t) -> p u t", u=4), [tp], raw.tk)
                if j == 15:
                    cp("act", rawt.ap.rearrange("p (u k) -> p u k", u=12), raw3[:, :, 128:131], raw.tk, rawt.tk)
                    dma("sp", ssdcP[l], rawt.ap.rearrange("p (u k) -> p u k", u=12), r=rawt.tk)
                for g4 in range(3):
                    ps, tp = psum()
                    for uu in range(4):
                        u = g4 * 4 + uu
                        for k in range(4):
                            mm(ps[:, uu * 128:(uu + 1) * 128], dg44[:, u, k, :], raw3[:, u, k:k + 128], k == 0, k == 3,
                               raw.tk + [dg4.tk[u]], [tp])
                    for uu in range(4):
                        u = g4 * 4 + uu
                        act(xact3[:, u, :], ps[:, uu * 128:(uu + 1) * 128], AF.Silu, [tp, tkPk], xact.tk, bias=pkc(O_SSDB + u))
                ps, tp = psum()
                for kc in range(8):
                    mm(ps[0:16, 0:128], Wdt3[:, kc, :], xb3[:, kc, :], kc == 0, kc == 7, Wdt.tk + xb.tk, [tp])
                fe, fdt, fdtA, fac = fm
                act(fe.ap[0:16, :], ps[0:16, 0:128], AF.Exp, [tp, tkPk], fe.tk, bias=pk[0:16, O_DTB:O_DTB + 1])
                act(fdt.ap[0:16, :], fe.ap[0:16, :], AF.Ln, fe.tk, fdt.tk, bias=1.0)
                ts("dve", fdtA.ap[0:16, :], fdt.ap[0:16, :], acol[:], None, ALU.mult, ALU.bypass, fdt.tk + [tkAcol], fdtA.tk)
                scan(fac.ap[0:16, :], onesf[0:16, :], fdtA.ap[0:16, :], fdtA.tk + [tkC], fac.tk)
                ps, tp = psum()
                tr(ps[:, 0:16], fdt.ap[0:16, :], identf[0:16, 0:16], fdt.tk + [tkC], [tp])
                tr(ps[:, 16:32], fdtA.ap[0:16, :], identf[0:16, 0:16], fdtA.tk + [tkC], [tp])
                tr(ps[:, 32:48], fac.ap[0:16, :], identf[0:16, 0:16], fac.tk + [tkC], [tp])
                tr(ps[:, 48:64], fac.ap[0:16, 127:128].to_broadcast([16, 128]), identf[0:16, 0:16], fac.tk + [tkC], [tp])
                cp("dve", tok4.ap, ps[:, 0:64], [tp], tok4.tk)
                dtT, dtAT, acT, totT = tok43[:, 0, :], tok43[:, 1, :], tok43[:, 2, :], tok43[:, 3, :]
                act(e1.ap, acT, AF.Exp, tok4.tk, e1.tk)
                tt("dve", dte.ap, totT, acT, ALU.subtract, tok4.tk, dte.tk)
                act(dte.ap, dte.ap, AF.Exp, dte.tk, dte.tk)
                tt("dve", dte.ap, dte.ap, dtT, ALU.mult, dte.tk + tok4.tk, dte.tk)
                act(cdec.ap, totT, AF.Exp, tok4.tk, cdec.tk)
                ps, tp = psum()
                pbf = ps[:, :].bitcast(BF16)
                for c in range(8):
                    tr(pbf[:, c * 128:(c + 1) * 128], xact3[:, c, :], identb, xact.tk + [tkC], [tp])
                cp("act", xsT_.ap, pbf[:, 0:1024], [tp], xsT_.tk)
                ps, tp = psum()
                pbf = ps[:, :].bitcast(BF16)
                for g in range(2):
                    tr(pbf[:, g * 128:(g + 1) * 128], xact3[:, 8 + g, :], identb, xact.tk + [tkC], [tp])
                cp("dve", BtT.ap, pbf[:, 0:256], [tp], BtT.tk)
                Rb3 = Rb.v("p (h l) -> p h l", h=16)
                LT3 = LT.v("p (h l) -> p h l", h=16)
                sc3 = scT.v("p (h l) -> p h l", h=16)
                tt("pool", Rb3, ulef.unsqueeze(1).to_broadcast([128, 16, 128]),
                   dtAT.unsqueeze(2).to_broadcast([128, 16, 128]), ALU.mult, [tkC] + tok4.tk, Rb.tk)
                for q in range(4):
                    ps, tp = psum()
                    mm(ps[:, :], mstb, Rb.ap[:, q * 512:(q + 1) * 512], True, True, [tkC] + Rb.tk, [tp])
                    act(LT.ap[:, q * 512:(q + 1) * 512], ps[:, :], AF.Exp, [tp], LT.tk)
                ps, tp = psum()
                for g in range(2):
                    mm(ps[:, g * 128:(g + 1) * 128], xact3[:, 8 + g, :], xact3[:, 10 + g, :], True, True, xact.tk, [tp])
                tt("dve", Gm.v("p (g l) -> p g l", g=2), ps[:, 0:256].rearrange("p (g l) -> p g l", g=2),
                   ulef.unsqueeze(1).to_broadcast([128, 2, 128]), ALU.mult, [tp, tkC], Gm.tk)
                Gm3 = Gm.v("p (g l) -> p g l", g=2)
                for g in range(2):
                    tt("pool" if g else "dve", sc3[:, 8 * g:8 * g + 8, :], LT3[:, 8 * g:8 * g + 8, :],
                       Gm3[:, g, :].unsqueeze(1).to_broadcast([128, 8, 128]), ALU.mult, LT.tk + Gm.tk, scT.tk)
                xs3 = xsT_.v("p (h d) -> p h d", h=16)
                tt("dve", dtx.v("p (h d) -> p h d", h=16), xs3, bc_h(dtT, 0, 16), ALU.mult, xsT_.tk + tok4.tk, dtx.tk)
                tt("pool", dtxe.v("p (h d) -> p h d", h=16), xs3, bc_h(dte.ap, 0, 16), ALU.mult, xsT_.tk + dte.tk, dtxe.tk)
                tt("pool", Dxs.v("p (h d) -> p h d", h=16), xs3, bc_h(pk[:, O_DROW:O_DROW + 16], 0, 16), ALU.mult,
                   xsT_.tk + [tkPk], Dxs.tk)
                dtx3 = dtx.v("p (h d) -> p h d", h=16)
                psY = []
                for g in range(2):
                    ps, tp = psum()
                    mm(ps[:, :], identb, Dxs.ap[:, g * 512:(g + 1) * 512], True, False, [tkC] + Dxs.tk, [tp])
                    for hh in range(8):
                        h = g * 8 + hh
                        mm(ps[:, hh * 64:(hh + 1) * 64], sc3[:, h, :], dtx3[:, h, :], False, hh == 7, scT.tk + dtx.tk, [tp])
                    psY.append((ps, tp))
                yo3 = yo.v("p (h d) -> p h d", h=16)
                for g in range(2):
                    ps, tp = psum()
                    mm(ps[:, :], xact3[:, 10 + g, :], hTb.ap[:, g * 512:(g + 1) * 512], True, True, xact.tk + [hTb.tk[g]], [tp])
                    tt("dve", yo3[:, 8 * g:8 * g + 8, :], ps[:, :].rearrange("p (h d) -> p h d", h=8),
                       bc_h(e1.ap, 8 * g, 8), ALU.mult, [tp] + e1.tk, yo.tk)
                    tt("dve", yo.ap[:, g * 512:(g + 1) * 512], psY[g][0][:, :], yo.ap[:, g * 512:(g + 1) * 512], ALU.add,
                       [psY[g][1]] + yo.tk, yo.tk)
                for hf in range(2):
                    ps, tp = psum()
                    for kc in range(8):
                        mm(ps[:, :], xb3[:, kc, :], Wz3[:, kc, hf * 512:(hf + 1) * 512], kc == 0, kc == 7,
                           xb.tk + [Wz.tk[2 * hf], Wz.tk[2 * hf + 1]], [tp])
                    act(zs.ap[:, hf * 512:(hf + 1) * 512], ps[:, :], AF.Silu, [tp], zs.tk)
                tt("dve", yo.ap, yo.ap, zs.ap, ALU.mult, yo.tk + zs.tk, yo.tk)
                act(junk.ap, yo.ap, AF.Square, yo.tk, junk.tk + ss.tk, accum_out=ss.ap)
                act(rs1.ap, ss.ap, AF.Ln, ss.tk, rs1.tk, bias=EPS, scale=1.0 / 1024.0)
                act(rs1.ap, rs1.ap, AF.Exp, rs1.tk, rs1.tk, scale=-0.5)
                act(yn.ap, yo.ap, AF.Copy, yo.tk + rs1.tk, yn.tk, scale=rs1.ap)
                ps, tp = psum()
                pbf = ps[:, :].bitcast(BF16)
                for c in range(8):
                    tr(pbf[:, c * 128:(c + 1) * 128], yn.ap[:, c * 128:(c + 1) * 128], identb, yn.tk + [tkC], [tp])
                Yfm3 = Yfm3p[:, :, (j % 2) * 128:(j % 2) * 128 + 128]
                tt("dve", Yfm3, pbf[:, 0:1024].rearrange("p (c t) -> p c t", c=8),
                   pk[:, O_RMSG:O_RMSG + 8].unsqueeze(2).to_broadcast([128, 8, 128]), ALU.mult, [tp, tkPk], Yfm.tk)
                hT3 = hT.v("p (h d) -> p h d", h=16)
                for g in range(2):
                    ps, tp = psum()
                    mm(ps[:, :], BtT.ap[:, g * 128:(g + 1) * 128], dtxe.ap[:, g * 512:(g + 1) * 512], True, True,
                       BtT.tk + dtxe.tk, [tp])
                    tt("pool", hT3[:, 8 * g:8 * g + 8, :], hT3[:, 8 * g:8 * g + 8, :], bc_h(cdec.ap, 8 * g, 8), ALU.mult,
                       [hT.tk[g]] + cdec.tk, [hT.tk[g]])
                    tt("dve", hT.ap[:, g * 512:(g + 1) * 512], hT.ap[:, g * 512:(g + 1) * 512], ps[:, :], ALU.add,
                       [hT.tk[g], tp], [hT.tk[g]])
                    cp("act", hTb.ap[:, g * 512:(g + 1) * 512], hT.ap[:, g * 512:(g + 1) * 512], [hT.tk[g]], [hTb.tk[g]])
                if j % 2 == 1:
                    t0p = t0 - 128
                    blkp = ("P", t0p, 256)
                    for m2 in range(4):
                        ps, tp = psum()
                        for mm_ in range(2):
                            m = m2 * 2 + mm_
                            for kc in range(8):
                                mm(ps[:, mm_ * 256:(mm_ + 1) * 256], WoY3[:, kc, m * 128:(m + 1) * 128], Yfm3p[:, kc, :],
                                   kc == 0, kc == 7, [WoY.tk[m // 2]] + Yfm.tk, [tp])
                        xtks = [t_ for q in range(2) for t_ in XT(m2 * 2 + q, t0p, 256)]
                        tt("dve", XFp[:, m2 * 2:m2 * 2 + 2, t0p:t0p + 256], XFp[:, m2 * 2:m2 * 2 + 2, t0p:t0p + 256],
                           ps[:, :].rearrange("p (m t) -> p m t", m=2), ALU.add, xtks + [tp], xtks)
                    layernorm(lnt, 256, lambda c: xf(blkp, c), lambda c: xtk(blkp, c), False,
                              lambda c: [xf(blkp, c)], lambda c: xtk(blkp, c),
                              lambda c: pkc(O_MIXG + c), lambda c: pkc(O_MIXB + c), AF.Identity)
            dma("sp", ssdPT[l], hT.ap, r=hT.tk)

            chk(l, 3)
            scr.release(mW)
            xbs = Buf(XBS_t[:], tkXBS)
            xbs3 = XBS3
            raws = scr.alloc(12 * NS, F32)
            raws3 = raws.v("p (u s) -> p u s", u=12)
            hs4 = scr.alloc(12 * 4 * NS, BF16, ntk=2)
            hs44 = hs4.v("p (u k s) -> p u k s", u=12, k=4)
            xas = scr.alloc(12 * NS, F32)
            xas3 = xas.v("p (u s) -> p u s", u=12)
            ps, tp = psum()
            for u in range(12):
                for kc in range(8):
                    mm(ps[:, u * NS:(u + 1) * NS], Wx3[:, kc, u * 128:(u + 1) * 128], xbs3[:, kc, :], kc == 0, kc == 7,
                       [Wx.tk[u // 2]] + xbs.tk, [tp])
            cp("act", raws.ap, ps[:, 0:12 * NS], [tp], raws.tk)
            dma("sp", ssdcS[l][:, :, 2, :], raws3, r=raws.tk)
            dma("sp", ssdcS[l][:, :, 0:2, :], ssdcT[l][:, :, 1:3, :])
            dma("pool", hs44[:, :, 0:3, :], ssdcT[l], w=[hs4.tk[0]])
            cp("pool", hs44[:, :, 3, :], raws3, raws.tk, [hs4.tk[1]])
            ps, tp = psum()
            for u in range(12):
                for k in range(4):
                    mm(ps[:, u * NS:(u + 1) * NS], dg44[:, u, k, :], hs44[:, u, k, :], k == 0, k == 3, hs4.tk + [dg4.tk[u]], [tp])
            for u in range(12):
                act(xas3[:, u, :], ps[:, u * NS:(u + 1) * NS], AF.Silu, [tp, tkPk], xas.tk, bias=pkc(O_SSDB + u))
            xasb = scr.alloc(4 * NS, BF16)
            xasb3 = xasb.v("p (u s) -> p u s", u=4)
            cp("pool", xasb3, xas3[:, 8:12, :], xas.tk, xasb.tk)
            dts = scr.alloc(NS, F32); dAs = scr.alloc(NS, F32)
            ps, tp = psum()
            for kc in range(8):
                mm(ps[0:16, 0:NS], Wdt3[:, kc, :], xbs3[:, kc, :], kc == 0, kc == 7, Wdt.tk + xbs.tk, [tp])
            act(dts.ap[0:16, :], ps[0:16, 0:NS], AF.Exp, [tp, tkPk], dts.tk, bias=pk[0:16, O_DTB:O_DTB + 1])
            act(dts.ap[0:16, :], dts.ap[0:16, :], AF.Ln, dts.tk, dts.tk, bias=1.0)
            act(dAs.ap[0:16, :], dts.ap[0:16, :], AF.Exp, dts.tk + [tkAcol], dAs.tk, scale=acol[:])
            dE = scr.alloc(2 * 8 * NS, F32)
            dE4 = dE.v("p (a c s) -> p a c s", a=2, c=8)
            ps, tp = psum()
            for c in range(8):
                mm(ps[:, c * NS:(c + 1) * NS], ehpf[:, c, :], dts.ap[0:16, :], True, True, [tkC] + dts.tk, [tp])
            for c in range(8):
                mm(ps[:, 128 + c * NS:128 + (c + 1) * NS], ehpf[:, c, :], dAs.ap[0:16, :], True, True, [tkC] + dAs.tk, [tp])
            cp("dve", dE.ap, ps[:, 0:256], [tp], dE.tk)
            dtxs = scr.alloc(8 * NS, F32)
            dtxs3 = dtxs.v("p (c s) -> p c s", c=8)
            tt("dve", dtxs3, dE4[:, 0], xas3[:, 0:8, :], ALU.mult, dE.tk + xas.tk, dtxs.tk)
            ysm = scr.alloc(8 * NS, F32)
            ysm3 = ysm.v("p (c s) -> p c s", c=8)
            Hb = [scr.alloc(8 * 128, F32) for _ in range(2)]
            bcb = [scr.alloc(4 * 128, BF16) for _ in range(2)]
            t1s = [scr.alloc(128, F32) for _ in range(2)]
            jk2 = [scr.alloc(128, F32) for _ in range(2)]
            for s in range(NS):
                H = Hb[s % 2]
                H3 = H.v("p (c n) -> p c n", c=8)
                dma("sp", H3, ssdst[l, s].rearrange("(c q) n -> q c n", q=128), w=H.tk)
                bb = bcb[s % 2]
                bb3 = bb.v("p (u q) -> p u q", u=4)
                cp("pool", bb3, xasb3[:, :, s:s + 1].to_broadcast([128, 4, 128]), xasb.tk, bb.tk)
                psbc, tbc = psum()
                for u in range(4):
                    mm(psbc[:, u * 128:(u + 1) * 128], bb3[:, u, :], identb, True, True, bb.tk + [tkC], [tbc])
                for c in range(8):
                    g = c // 4
                    t1 = t1s[c % 2]
                    ts("dve", t1.ap, psbc[:, g * 128:(g + 1) * 128], dtxs3[:, c, s:s + 1], None, ALU.mult, ALU.bypass,
                       [tbc] + dtxs.tk, t1.tk)
                    stt(H3[:, c, :], H3[:, c, :], dE4[:, 1, c, s:s + 1], t1.ap, ALU.mult, ALU.add, H.tk + dE.tk + t1.tk, H.tk)
                    jk = jk2[c % 2]
                    stt(jk.ap, H3[:, c, :], 1.0, psbc[:, (2 + g) * 128:(3 + g) * 128], ALU.mult, ALU.mult,
                        H.tk + [tbc], jk.tk + ysm.tk, accum_out=ysm3[:, c, s:s + 1])
                dma("sp", ssdS[l, s].rearrange("(c q) n -> q c n", q=128), H3, r=H.tk)
            tmpd = scr.alloc(8 * NS, F32)
            tmpd3 = tmpd.v("p (c s) -> p c s", c=8)
            tt("dve", tmpd3, xas3[:, 0:8, :], pk[:, O_DCOL:O_DCOL + 8].unsqueeze(2).to_broadcast([128, 8, NS]), ALU.mult,
               xas.tk + [tkPk], tmpd.tk)
            tt("dve", ysm.ap, ysm.ap, tmpd.ap, ALU.add, ysm.tk + tmpd.tk, ysm.tk)
            zss = scr.alloc(8 * NS, F32)
            ps, tp = psum()
            for c in range(8):
                for kc in range(8):
                    mm(ps[:, c * NS:(c + 1) * NS], Wz3[:, kc, c * 128:(c + 1) * 128], xbs3[:, kc, :], kc == 0, kc == 7,
                       [Wz.tk[c // 2]] + xbs.tk, [tp])
            act(zss.ap, ps[:, 0:8 * NS], AF.Silu, [tp], zss.tk)
            tt("dve", ysm.ap, ysm.ap, zss.ap, ALU.mult, ysm.tk + zss.tk, ysm.tk)
            Ys = scr.alloc(8 * NS, BF16)
            Ys3 = Ys.v("p (c s) -> p c s", c=8)
            layernorm(lnt, NS, lambda c: ysm3[:, c, :], lambda c: ysm.tk, False, lambda c: [Ys3[:, c, :]], lambda c: Ys.tk,
                      lambda c: pkc(O_RMSG + c), None, AF.Copy, rms=True)
            ps, tp = psum()
            for m in range(8):
                for kc in range(8):
                    mm(ps[:, m * NS:(m + 1) * NS], WoY3[:, kc, m * 128:(m + 1) * 128], Ys3[:, kc, :], kc == 0, kc == 7,
                       [WoY.tk[m // 2]] + Ys.tk, [tp])
            tt("dve", XFs, XFs, ps[:, 0:8 * NS].rearrange("p (m s) -> p m s", m=8), ALU.add, tkXs + [tp], tkXs)
            layernorm(lnt, NS, lambda c: xf(SB, c), lambda c: xtk(SB, c), False, lambda c: [xf(SB, c)], lambda c: xtk(SB, c),
                      lambda c: pkc(O_MIXG + c), lambda c: pkc(O_MIXB + c), AF.Identity)

            chk(l, 4)
            scr.release(0)
            Wq = scr.alloc(8 * 1024, BF16, ntk=4); Wo = scr.alloc(8 * 1024, BF16, ntk=4)
            Wq3 = Wq.v("p (k m) -> p k m", k=8); Wo3 = Wo.v("p (k m) -> p k m", k=8)
            KT = scr.alloc(8 * 256, BF16); Vb = scr.alloc(2 * 1024, BF16)
            KT3 = KT.v("p (c m) -> p c m", c=8); Vb3 = Vb.v("p (a e) -> p a e", a=2)
            mC = scr.mark()
            Wk = scr.alloc(8 * 1024, BF16, ntk=4); Wv = scr.alloc(8 * 1024, BF16, ntk=4)
            Wk3 = Wk.v("p (k m) -> p k m", k=8); Wv3 = Wv.v("p (k m) -> p k m", k=8)
            memb = scr.alloc(8 * 256, BF16)
            memb3 = memb.v("p (c m) -> p c m", c=8)
            kf = [scr.alloc(256, F32) for _ in range(2)]
            vf = [scr.alloc(512, F32) for _ in range(2)]
            import os
            if os.environ.get("KVSKIP"):
                S.mute = True
            load_w(Wk3, wk[l], Wk.tk, 256)
            load_w(Wv3, wv[l], Wv.tk, 256)
            load_w(Wq3, wq[l], Wq.tk, 256)
            load_w(Wo3, wo[l], Wo.tk, 256)
            for hh in range(4):
                dma("pool", memb3[:, 2 * hh:2 * hh + 2, :], memT[:, 2 * hh:2 * hh + 2, :], w=memb.tk)
            chk(l, 4, -2)
            for c in range(8):
                ps, tp = psum()
                for kc in range(8):
                    mm(ps[:, 0:256], Wk3[:, kc, c * 128:(c + 1) * 128], memb3[:, kc, :], kc == 0, kc == 7,
                       [Wk.tk[c // 2]] + memb.tk, [tp])
                cp("act", KT3[:, c, :], ps[:, 0:256], [tp], KT.tk)
                cp("dve", kf[c % 2].ap, ps[:, 0:256], [tp], kf[c % 2].tk)
                dma("sp", kP[l][:, c, :], kf[c % 2].ap, r=kf[c % 2].tk)
            chk(l, 4, -1)
            vi = 0
            for mc in range(2):
                for hf in range(2):
                    ps, tp = psum()
                    for kc in range(8):
                        mm(ps[:, :], memb3[:, kc, mc * 128:(mc + 1) * 128], Wv3[:, kc, hf * 512:(hf + 1) * 512], kc == 0, kc == 7,
                           memb.tk + [Wv.tk[2 * hf], Wv.tk[2 * hf + 1]], [tp])
                    import os
                    V_ = os.environ.get("VDBG", "abc")
                    if "a" in V_:
                        cp("act", Vb3[:, mc, hf * 512:(hf + 1) * 512], ps[:, :], [tp], Vb.tk)
                    if "b" in V_:
                        cp("dve", vf[vi].ap, ps[:, :], [tp], vf[vi].tk)
                    if "c" in V_:
                        dma("sp", vP[l][mc * 128:(mc + 1) * 128, hf * 512:(hf + 1) * 512], vf[vi].ap, r=vf[vi].tk)
                    vi = (vi + 1) % 2
            chk(l, 4, 1)
            scr.release(mC)
            NCB = 2
            CSETS = []
            for _k in range(NCB):
                _d = {}
                _d["xb"] = scr.alloc(8 * 128, BF16)
                _d["qT"] = scr.alloc(8 * 128, BF16)
                for _n in ("mx", "nb", "rsum", "rinv"):
                    _d[_n] = scr.alloc(4, F32)
                _d["Pm"] = scr.alloc(4 * 256, BF16)
                _d["PT"] = scr.alloc(8 * 128, BF16)
                _d["Ob"] = scr.alloc(1024, BF16)
                _d["OT"] = scr.alloc(8 * 128, BF16)
                _d["lnt"] = LNT(256)
                CSETS.append(_d)
            NKB = 3
            kb = [scr.alloc(8 * 256, BF16) for _ in range(NKB)]
            vb = [scr.alloc(2 * 1024, BF16) for _ in range(NKB)]
            Qm = scr.alloc(8 * NS * NS, BF16)
            PTm = scr.alloc(8 * NS * NS, BF16)
            SCALE = 256.0 ** -0.5
            _d = CSETS[0]
            xb, qT, mx, nb, rsum, rinv, Pm, PT, Ob, OT, lnt = (_d[k] for k in
                                                                  ("xb", "qT", "mx", "nb", "rsum", "rinv", "Pm", "PT", "Ob", "OT", "lnt"))
            xb3 = xb.v("p (k t) -> p k t", k=8); qT3 = qT.v("p (c t) -> p c t", c=8)
            Pm3 = Pm.v("p (h m) -> p h m", h=4); PT4 = PT.v("p (h a t) -> p h a t", h=4, a=2)
            OT3 = OT.v("p (c t) -> p c t", c=8)

            def softmax_pv(Q, sbanks, blk):
                for h in range(4):
                    red(mx.ap[0:Q, h:h + 1], sbanks[h][0], ALU.max, [sbanks[h][1]], mx.tk)
                ts("dve", nb.ap[0:Q, :], mx.ap[0:Q, :], -SCALE, None, ALU.mult, ALU.bypass, mx.tk, nb.tk)
                for h in range(4):
                    act(Pm3[0:Q, h, :], sbanks[h][0], AF.Exp, [sbanks[h][1]] + nb.tk, Pm.tk + rsum.tk,
                        bias=nb.ap[0:Q, h:h + 1], scale=SCALE, accum_out=rsum.ap[0:Q, h:h + 1])
                recip(rinv.ap[0:Q, :], rsum.ap[0:Q, :], rsum.tk, rinv.tk)
                ps, tp = psum()
                pbf = ps[:, :].bitcast(BF16)
                for h in range(4):
                    for a in range(2):
                        tr(pbf[:, (h * 2 + a) * 128:(h * 2 + a) * 128 + Q], Pm3[0:Q, h, a * 128:(a + 1) * 128], identb[0:Q, 0:Q],
                           Pm.tk + [tkC], [tp])
                cp("act", PT4[:, :, :, 0:Q], pbf[:, 0:1024].rearrange("p (h a t) -> p h a t", h=4, a=2)[:, :, :, 0:Q], [tp], PT.tk)

            xbs = Buf(XBS_t[:], tkXBS)
            xbs3 = XBS3
            cp("pool", xbs3, XFs, tkXs, xbs.tk)
            qs = scr.alloc(8 * NS, BF16)
            qs3 = qs.v("p (c s) -> p c s", c=8)
            ps, tp = psum()
            for m in range(8):
                for kc in range(8):
                    mm(ps[:, m * NS:(m + 1) * NS], Wq3[:, kc, m * 128:(m + 1) * 128], xbs3[:, kc, :], kc == 0, kc == 7,
                       [Wq.tk[m // 2]] + xbs.tk, [tp])
            cp("act", qs.ap, ps[:, 0:8 * NS], [tp], qs.tk)
            Qm4 = Qm.v("p (c s q) -> p c s q", c=8, s=NS)
            tt("pool", Qm4, qs3.unsqueeze(3).to_broadcast([128, 8, NS, NS]),
               id16b.unsqueeze(1).to_broadcast([128, 8, NS, NS]), ALU.mult, qs.tk + [tkC], Qm.tk)
            sbk = [psum() for _ in range(4)]
            for s in range(NS):
                kbs = kb[s % NKB]
                kbs3 = kbs.v("p (c m) -> p c m", c=8)
                for hh in range(2):
                    dma("pool", kbs3[:, 4 * hh:4 * hh + 4, :], kcT[l, s][:, 4 * hh:4 * hh + 4, :], w=kbs.tk)
                for h in range(4):
                    for dc in range(2):
                        mm(sbk[h][0][0:NS, 0:256], Qm4[:, 2 * h + dc, s, :], kbs3[:, 2 * h + dc, :], s == 0 and dc == 0,
                           s == NS - 1 and dc == 1, Qm.tk + kbs.tk, [sbk[h][1]])
            softmax_pv(NS, [(sbk[h][0][0:NS, 0:256], sbk[h][1]) for h in range(4)], SB)
            PTm5 = PTm.v("p (h a s q) -> p h a s q", h=4, a=2, s=NS)
            for h in range(4):
                tt("pool", PTm5[:, h], PT4[:, h, :, 0:NS].unsqueeze(3).to_broadcast([128, 2, NS, NS]),
                   id16b.unsqueeze(1).to_broadcast([128, 2, NS, NS]), ALU.mult, PT.tk + [tkC], PTm.tk)
            obk = [psum() for _ in range(4)]
            for s in range(NS):
                vbs = vb[s % NKB]
                vbs3 = vbs.v("p (a e) -> p a e", a=2)
                dma("pool", vbs3, vc[l, s].rearrange("(a p) e -> p a e", p=128), w=vbs.tk)
                for h in range(4):
                    for a in range(2):
                        mm(obk[h][0][0:NS, 0:256], PTm5[:, h, a, s, :], vbs3[:, a, h * 256:(h + 1) * 256], s == 0 and a == 0,
                           s == NS - 1 and a == 1, PTm.tk + vbs.tk, [obk[h][1]])
            for h in range(4):
                ts("dve", Ob.ap[0:NS, h * 256:(h + 1) * 256], obk[h][0][0:NS, 0:256], rinv.ap[0:NS, h:h + 1], None,
                   ALU.mult, ALU.bypass, [obk[h][1]] + rinv.tk, Ob.tk)
            ps, tp = psum()
            pbf = ps[:, :].bitcast(BF16)
            for c in range(8):
                tr(pbf[:, c * 128:c * 128 + NS], Ob.ap[0:NS, c * 128:(c + 1) * 128], identb[0:NS, 0:NS], Ob.tk + [tkC], [tp])
            cp("act", OT3[:, :, 0:NS], pbf[:, 0:1024].rearrange("p (c t) -> p c t", c=8)[:, :, 0:NS], [tp], OT.tk)
            ps, tp = psum()
            for m in range(8):
                for kc in range(8):
                    mm(ps[:, m * NS:(m + 1) * NS], Wo3[:, kc, m * 128:(m + 1) * 128], OT3[:, kc, 0:NS], kc == 0, kc == 7,
                       [Wo.tk[m // 2]] + OT.tk, [tp])
            stt(XFs, XFs, ALPHA, ps[:, 0:8 * NS].rearrange("p (m s) -> p m s", m=8), ALU.mult, ALU.add, tkXs + [tp], tkXs)
            layernorm(lnt, NS, lambda c: xf(SB, c), lambda c: xtk(SB, c), False, lambda c: [xf(SB, c)], lambda c: xtk(SB, c),
                      lambda c: pkc(O_XAG + c), lambda c: pkc(O_XAB + c), AF.Identity)

            chk(l, 5)
            for j in range(16):
                t0 = j * 128
                blk = ("P", t0, 128)
                b5 = t0 // 512
                _d = CSETS[j % NCB]
                xb, qT, mx, nb, rsum, rinv, Pm, PT, Ob, OT, lnt = (_d[k] for k in
                                                                      ("xb", "qT", "mx", "nb", "rsum", "rinv", "Pm", "PT", "Ob", "OT", "lnt"))
                xb3 = xb.v("p (k t) -> p k t", k=8); qT3 = qT.v("p (c t) -> p c t", c=8)
                Pm3 = Pm.v("p (h m) -> p h m", h=4); PT4 = PT.v("p (h a t) -> p h a t", h=4, a=2)
                OT3 = OT.v("p (c t) -> p c t", c=8)
                import os
                Q_ = os.environ.get("QDBG", "abc")
                if "a" in Q_:
                    if os.environ.get("XBSRC") == "cst":
                        cp("pool", xb.ap, cstf[:, 0:1024], [tkC], xb.tk)
                    elif os.environ.get("XBSRC") == "2d":
                        for c in range(8):
                            cp("pool", xb3[:, c, :], XFp[:, c, t0:t0 + 128], [tkXp[c][j]], xb.tk)
                    else:
                        cp(os.environ.get("XBENG", "pool"), xb3, XFp[:, :, t0:t0 + 128], [tkXp[c][j] for c in range(8)], xb.tk)
                for m4 in range(2):
                    ps, tp = psum()
                    for mm_ in range(4):
                        m = m4 * 4 + mm_
                        for kc in range(8):
                            if "b" in Q_:
                                mm(ps[:, mm_ * 128:(mm_ + 1) * 128], Wq3[:, kc, m * 128:(m + 1) * 128], xb3[:, kc, :], kc == 0, kc == 7,
                                   [Wq.tk[m // 2]] + xb.tk, [tp])
                    if "c" in Q_:
                        cp("act", qT3[:, m4 * 4:m4 * 4 + 4, :], ps[:, :].rearrange("p (m t) -> p m t", m=4), [tp], qT.tk)
                chk(l, 4, 2)
                sb_ = []
                for hp in range(2):
                    ps, tp = psum()
                    for hh in range(2):
                        h = hp * 2 + hh
                        for dc in range(2):
                            mm(ps[:, hh * 256:(hh + 1) * 256], qT3[:, 2 * h + dc, :], KT3[:, 2 * h + dc, :], dc == 0, dc == 1,
                               qT.tk + KT.tk, [tp])
                    sb_.append((ps[:, 0:256], tp))
                    sb_.append((ps[:, 256:512], tp))
                chk(l, 4, 3)
                softmax_pv(128, sb_, blk)
                chk(l, 4, 4)
                for hp in range(2):
                    ps, tp = psum()
                    for hh in range(2):
                        h = hp * 2 + hh
                        for a in range(2):
                            mm(ps[:, hh * 256:(hh + 1) * 256], PT4[:, h, a, :], Vb3[:, a, h * 256:(h + 1) * 256], a == 0, a == 1,
                               PT.tk + Vb.tk, [tp])
                    tt("dve", Ob.ap[:, hp * 512:(hp + 1) * 512].rearrange("p (h d) -> p h d", h=2),
                       ps[:, :].rearrange("p (h d) -> p h d", h=2),
                       rinv.ap[:, hp * 2:hp * 2 + 2].unsqueeze(2).to_broadcast([128, 2, 256]), ALU.mult, [tp] + rinv.tk, Ob.tk)
                chk(l, 4, 5)
                ps, tp = psum()
                pbf = ps[:, :].bitcast(BF16)
                for c in range(8):
                    tr(pbf[:, c * 128:(c + 1) * 128], Ob.ap[:, c * 128:(c + 1) * 128], identb, Ob.tk + [tkC], [tp])
                cp("act", OT.ap, pbf[:, 0:1024], [tp], OT.tk)
                chk(l, 4, 6)
                for m4 in range(2):
                    ps, tp = psum()
                    for mm_ in range(4):
                        m = m4 * 4 + mm_
                        for kc in range(8):
                            mm(ps[:, mm_ * 128:(mm_ + 1) * 128], Wo3[:, kc, m * 128:(m + 1) * 128], OT3[:, kc, :], kc == 0, kc == 7,
                               [Wo.tk[m // 2]] + OT.tk, [tp])
                    xtks = [tkXp[m4 * 4 + q][j] for q in range(4)]
                    stt(XFp[:, m4 * 4:m4 * 4 + 4, t0:t0 + 128], XFp[:, m4 * 4:m4 * 4 + 4, t0:t0 + 128], ALPHA,
                        ps[:, :].rearrange("p (m t) -> p m t", m=4), ALU.mult, ALU.add, xtks + [tp], xtks)
                if j % 2 == 1:
                    blkp = ("P", t0 - 128, 256)
                    layernorm(lnt, 256, lambda c: xf(blkp, c), lambda c: xtk(blkp, c), False,
                              lambda c: [xf(blkp, c)], lambda c: xtk(blkp, c),
                              lambda c: pkc(O_XAG + c), lambda c: pkc(O_XAB + c), AF.Identity)

            chk(l, 6)
            WD = 256
            for gi in range(2):
                scr.release(0)
                Wup = scr.alloc(8 * 2 * 1408, BF16, ntk=22)
                Wup4 = Wup.v("p (k h m) -> p k h m", k=8, h=2)
                Wdn = scr.alloc(11 * 1024, BF16, ntk=8)
                Wdn3 = Wdn.v("p (k m) -> p k m", k=11)
                for hh in range(2):
                    for jj in range(11):
                        c0 = hh * 2816 + gi * 1408 + jj * 128
                        dma("pool", Wup4[:, :, hh, jj * 128:(jj + 1) * 128], w_up[l][:, :, c0:c0 + 128], w=[Wup.tk[hh * 11 + jj]])
                for m in range(8):
                    dma("pool", Wdn3[:, :, m * 128:(m + 1) * 128], w_dn[l][:, gi * 11:(gi + 1) * 11, m * 128:(m + 1) * 128],
                        w=[Wdn.tk[m]])
                dg3 = scr.alloc(22 * 3 * 128, BF16, ntk=22)
                dg34 = dg3.v("p (u k m) -> p u k m", u=22, k=3)

                def chid(u):
                    return (u // 11) * 22 + gi * 11 + (u % 11)

                for u in range(22):
                    tt("pool", dg34[:, u], identb.unsqueeze(1).to_broadcast([128, 3, 128]),
                       pkc(O_FFW + chid(u) * 3, 3).unsqueeze(2).to_broadcast([128, 3, 128]), ALU.mult, [tkC, tkPk], [dg3.tk[u]])
                xb = scr.alloc(8 * WD, BF16)
                xb3 = xb.v("p (k t) -> p k t", k=8)
                ur = scr.alloc(22 * (WD + 2), BF16, ntk=22)
                ur3 = ur.v("p (u t) -> p u t", u=22)
                gt = scr.alloc(11 * WD, BF16, ntk=11)
                gt3 = gt.v("p (u t) -> p u t", u=11)
                sgf = [scr.alloc(WD, F32) for _ in range(2)]
                urt = scr.alloc(44, F32)
                lnt = LNT(WD)
                for bi in range(T // WD):
                    t0 = bi * WD
                    blk = ("P", t0, WD)
                    b5 = t0 // 512
                    if gi == 0:
                        cp("pool", xb3, XFp[:, :, t0:t0 + WD], [t_ for c in range(8) for t_ in XT(c, t0, WD)], xb.tk)
                        for a in range(2):
                            dma("sp", xbd[:, 2 * bi + a], xb3[:, :, a * 128:(a + 1) * 128], r=xb.tk, w=[tkXbd[2 * bi + a]])
                    else:
                        for a in range(2):
                            dma("sp", xb3[:, :, a * 128:(a + 1) * 128], xbd[:, 2 * bi + a], r=[tkXbd[2 * bi + a]], w=xb.tk)
                    if bi == 0:
                        mset("pool", ur3[:, :, 0:2], 0.0, ur.tk)
                    else:
                        cp("pool", ur3[:, :, 0:2], ur3[:, :, WD:WD + 2], ur.tk, ur.tk)
                    for u in range(22):
                        ps, tp = psum()
                        for kc in range(8):
                            mm(ps[:, 0:WD], Wup4[:, kc, u // 11, (u % 11) * 128:(u % 11 + 1) * 128], xb3[:, kc, :], kc == 0, kc == 7,
                               [Wup.tk[u]] + xb.tk, [tp])
                        cp("act" if u % 2 else "dve", ur3[:, u, 2:WD + 2], ps[:, 0:WD], [tp], [ur.tk[u]])
                    if bi == T // WD - 1:
                        cp("act", urt.ap.rearrange("p (u k) -> p u k", u=22), ur3[:, :, WD:WD + 2], ur.tk, urt.tk)
                        urt3 = urt.ap.rearrange("p (u k) -> p u k", u=22)
                        dma("sp", ffncP[l][:, gi * 11:(gi + 1) * 11, :], urt3[:, 0:11, :], r=urt.tk)
                        dma("sp", ffncP[l][:, 22 + gi * 11:22 + (gi + 1) * 11, :], urt3[:, 11:22, :], r=urt.tk)
                    for jj in range(11):
                        pv, tv = psum()
                        pg, tg = psum()
                        for k in range(3):
                            mm(pv[:, 0:WD], dg34[:, jj, k, :], ur3[:, jj, k:k + WD], k == 0, k == 2, [ur.tk[jj], dg3.tk[jj]], [tv])
                        for k in range(3):
                            mm(pg[:, 0:WD], dg34[:, 11 + jj, k, :], ur3[:, 11 + jj, k:k + WD], k == 0, k == 2,
                               [ur.tk[11 + jj], dg3.tk[11 + jj]], [tg])
                        s_ = sgf[jj % 2]
                        act(s_.ap, pg[:, 0:WD], AF.Silu, [tg, tkPk], s_.tk, bias=pkc(O_FFBIAS + chid(11 + jj)))
                        stt(gt3[:, jj, :], pv[:, 0:WD], pkc(O_FFBIAS + chid(jj)), s_.ap, ALU.add, ALU.mult,
                            [tv, tkPk] + s_.tk, [gt.tk[jj]])
                    for m in range(8):
                        ps, tp = psum()
                        for jj in range(11):
                            mm(ps[:, 0:WD], Wdn3[:, jj, m * 128:(m + 1) * 128], gt3[:, jj, :], jj == 0, jj == 10,
                               [Wdn.tk[m], gt.tk[jj]], [tp])
                        if gi == 0:
                            stt(xf(blk, m), xf(blk, m), ALPHA, ps[:, 0:WD], ALU.mult, ALU.add, xtk(blk, m) + [tp], xtk(blk, m))
                        else:
                            tt("dve", xf(blk, m), xf(blk, m), ps[:, 0:WD], ALU.add, xtk(blk, m) + [tp], xtk(blk, m))
                    if gi == 1:
                        layernorm(lnt, WD, lambda c: xf(blk, c), lambda c: xtk(blk, c), False,
                                  lambda c: [xf(blk, c)], lambda c: xtk(blk, c),
                                  lambda c: pkc(O_FFG + c), lambda c: pkc(O_FFB + c), AF.Identity)
                xbs = Buf(XBS_t[:], tkXBS)
                xbs3 = XBS3
                if gi == 0:
                    cp("pool", xbs3, XFs, tkXs, xbs.tk)
                urs = scr.alloc(22 * NS, F32)
                urs3 = urs.v("p (u s) -> p u s", u=22)
                h3 = scr.alloc(22 * 3 * NS, BF16, ntk=2)
                h34 = h3.v("p (u k s) -> p u k s", u=22, k=3)
                ps, tp = psum()
                for u in range(22):
                    for kc in range(8):
                        mm(ps[:, u * NS:(u + 1) * NS], Wup4[:, kc, u // 11, (u % 11) * 128:(u % 11 + 1) * 128], xbs3[:, kc, :],
                           kc == 0, kc == 7, [Wup.tk[u]] + xbs.tk, [tp])
                cp("act", urs.ap, ps[:, 0:22 * NS], [tp], urs.tk)
                for hh in range(2):
                    c0 = hh * 22 + gi * 11
                    dma("sp", ffncS[l][:, c0:c0 + 11, 1, :], urs3[:, hh * 11:(hh + 1) * 11, :], r=urs.tk)
                    dma("sp", ffncS[l][:, c0:c0 + 11, 0, :], ffncT[l][:, c0:c0 + 11, 1, :])
                    dma("pool", h34[:, hh * 11:(hh + 1) * 11, 0:2, :], ffncT[l][:, c0:c0 + 11], w=[h3.tk[0]])
                cp("pool", h34[:, :, 2, :], urs3, urs.tk, [h3.tk[1]])
                pv, tv = psum()
                for u in range(22):
                    for k in range(3):
                        mm(pv[:, u * NS:(u + 1) * NS], dg34[:, u, k, :], h34[:, u, k, :], k == 0, k == 2, h3.tk + [dg3.tk[u]], [tv])
                sgs = scr.alloc(11 * NS, F32)
                gts = scr.alloc(11 * NS, BF16)
                gts3 = gts.v("p (u s) -> p u s", u=11)
                for jj in range(11):
                    act(sgs.ap[:, jj * NS:(jj + 1) * NS], pv[:, (11 + jj) * NS:(12 + jj) * NS], AF.Silu, [tv, tkPk], sgs.tk,
                        bias=pkc(O_FFBIAS + chid(11 + jj)))
                for jj in range(11):
                    stt(gts3[:, jj, :], pv[:, jj * NS:(jj + 1) * NS], pkc(O_FFBIAS + chid(jj)), sgs.ap[:, jj * NS:(jj + 1) * NS],
                        ALU.add, ALU.mult, [tv, tkPk] + sgs.tk, gts.tk)
                ps, tp = psum()
                for m in range(8):
                    for jj in range(11):
                        mm(ps[:, m * NS:(m + 1) * NS], Wdn3[:, jj, m * 128:(m + 1) * 128], gts3[:, jj, :], jj == 0, jj == 10,
                           [Wdn.tk[m]] + gts.tk, [tp])
                psv3 = ps[:, 0:8 * NS].rearrange("p (m s) -> p m s", m=8)
                if gi == 0:
                    stt(XFs, XFs, ALPHA, psv3, ALU.mult, ALU.add, tkXs + [tp], tkXs)
                else:
                    tt("dve", XFs, XFs, psv3, ALU.add, tkXs + [tp], tkXs)
                    layernorm(lnt, NS, lambda c: xf(SB, c), lambda c: xtk(SB, c), False, lambda c: [xf(SB, c)],
                              lambda c: xtk(SB, c), lambda c: pkc(O_FFG + c), lambda c: pkc(O_FFB + c), AF.Identity)

          except _Stop:
            break
        for c in range(8):
            for b in range(4):
                dma("sp", yT[:, c, b * 512:(b + 1) * 512], XFp[:, c, b * 512:(b + 1) * 512], r=XT(c, b * 512, 512))
        dma("sp", ysT, XFs, r=tkXs)
        S.emit(nc, st)
    return nc


def _wl(w):
    Lw, K, M = w.shape
    return np.ascontiguousarray(w.reshape(Lw, K // 128, 128, M).transpose(0, 2, 1, 3))


def _colT(v, nch):
    return v.reshape(v.shape[0], nch, 128).transpose(0, 2, 1)


def _build_pack(inp):
    pk = np.zeros((L, 128, NPK), np.float32)
    cw = inp["conf_conv_w"]
    pk[:, :, O_CONFW:O_CONFW + 248] = cw.reshape(L, 31, 8, 128).transpose(0, 3, 2, 1).reshape(L, 128, 248)
    pk[:, :, O_CONFB:O_CONFB + 8] = _colT(inp["conf_conv_b"], 8)
    pk[:, :, O_CLNG:O_CLNG + 8] = _colT(inp["conf_ln_g"], 8)
    pk[:, :, O_CLNB:O_CLNB + 8] = _colT(inp["conf_ln_b"], 8)
    sw = inp["ssd_conv_w"]
    pk[:, :, O_SSDW:O_SSDW + 48] = sw.reshape(L, 4, 12, 128).transpose(0, 3, 2, 1).reshape(L, 128, 48)
    pk[:, :, O_SSDB:O_SSDB + 12] = _colT(inp["ssd_conv_b"], 12)
    pk[:, :, O_RMSG:O_RMSG + 8] = _colT(inp["ssd_norm_g"], 8)
    for off, nm in ((O_MIXG, "ln_mix_g"), (O_MIXB, "ln_mix_b"), (O_XAG, "ln_xa_g"), (O_XAB, "ln_xa_b"),
                    (O_FFG, "ln_ffn_g"), (O_FFB, "ln_ffn_b")):
        pk[:, :, off:off + 8] = _colT(inp[nm], 8)
    fw = inp["ffn_conv_w"]
    pk[:, :, O_FFW:O_FFW + 132] = fw.reshape(L, 3, 44, 128).transpose(0, 3, 2, 1).reshape(L, 128, 132)
    pk[:, :, O_FFBIAS:O_FFBIAS + 44] = _colT(inp["ffn_conv_b"], 44)
    pk[:, 0:16, O_DTB] = inp["ssd_dt_bias"]
    pk[:, 0:16, O_ALOG] = inp["ssd_a_log"]
    pk[:, :, O_DROW:O_DROW + 16] = inp["ssd_d"][:, None, :]
    q = np.arange(128)
    for c in range(8):
        pk[:, :, O_DCOL + c] = inp["ssd_d"][:, 2 * c + q // 64]
    return pk


def _build_cst():
    c = np.zeros((128, NCST), np.float32)
    p = np.arange(128)
    c[:, C_ID:C_ID + 128] = np.eye(128)
    c[:, C_ULE:C_ULE + 128] = (p[:, None] <= p[None, :])
    c[:, C_MST:C_MST + 128] = (p[:, None] > p[None, :])
    c[:, C_ONE:C_ONE + 128] = 1.0
    c[:, C_ID16:C_ID16 + 256] = np.eye(16).reshape(1, 256)
    e = np.zeros((16, 8, 128), np.float32)
    for cc in range(8):
        for q in range(128):
            e[2 * cc + q // 64, cc, q] = 1.0
    c[0:16, C_EHP:C_EHP + 1024] = e.reshape(16, 1024)
    return c


_NC_CACHE = {}
STOP = None


def kernel(**inp):
    inp = {k: np.asarray(v) for k, v in inp.items()}
    f = np.float32
    shared = {
        "w_in": _wl(inp["w_in"]), "w_oA": _wl(inp["w_out"][:, 0:1024, :]), "w_oY": _wl(inp["w_out"][:, 1024:2048, :]),
        "wq": _wl(inp["xa_wq"]), "wk": _wl(inp["xa_wk"]), "wv": _wl(inp["xa_wv"]), "wo": _wl(inp["xa_wo"]),
        "w_up": _wl(inp["ffn_w_up"]), "w_dn": _wl(inp["ffn_w_down"]),
        "pack": _build_pack(inp), "cst": _build_cst(),
    }
    in_maps = []
    for i in range(NCORES):
        sl = slice(NS * i, NS * (i + 1))
        m = dict(shared)
        m["xT"] = np.ascontiguousarray(inp["x_prompt"][i].reshape(T, 8, 128).transpose(2, 1, 0))
        m["xsT"] = np.ascontiguousarray(inp["x_sample"][sl, 0].reshape(NS, 8, 128).transpose(2, 1, 0))
        m["memT"] = np.ascontiguousarray(inp["mem_prompt"][i].reshape(256, 8, 128).transpose(2, 1, 0))
        m["kcT"] = np.ascontiguousarray(inp["cache_mem_k"][:, sl].reshape(L, NS, 256, 8, 128).transpose(0, 1, 4, 3, 2))
        m["vc"] = np.ascontiguousarray(inp["cache_mem_v"][:, sl].reshape(L, NS, 256, 1024))
        m["confT"] = np.ascontiguousarray(inp["state_conf_conv"][:, sl].reshape(L, NS, 30, 8, 128).transpose(0, 4, 3, 2, 1))
        m["ssdcT"] = np.ascontiguousarray(inp["state_ssd_conv"][:, sl].reshape(L, NS, 3, 12, 128).transpose(0, 4, 3, 2, 1))
        m["ffncT"] = np.ascontiguousarray(inp["state_ffn_conv"][:, sl].reshape(L, NS, 2, 44, 128).transpose(0, 4, 3, 2, 1))
        m["ssdst"] = np.ascontiguousarray(inp["state_ssd"][:, sl].reshape(L, NS, 1024, 128))
        in_maps.append(m)
    if "nc" not in _NC_CACHE:
        _NC_CACHE["nc"] = build_program()
    nc = _NC_CACHE["nc"]
    res = run_bass_kernel_spmd(nc, in_maps, core_ids=list(range(NCORES)))
    R = res.results
    B = NCORES
    y_prompt = np.stack([R[i]["yT"].transpose(2, 1, 0).reshape(T, 1024) for i in range(B)]).astype(f)
    y_sample = np.concatenate([R[i]["ysT"].transpose(2, 1, 0).reshape(NS, 1, 1024) for i in range(B)]).astype(f)
    confP = np.stack([R[i]["confP"].transpose(0, 3, 2, 1).reshape(L, 30, 1024) for i in range(B)], axis=1)
    ssdcP = np.stack([R[i]["ssdcP"].transpose(0, 3, 2, 1).reshape(L, 3, 1536) for i in range(B)], axis=1)
    ssdP = np.stack([R[i]["ssdPT"].transpose(0, 2, 1).reshape(L, 16, 64, 128) for i in range(B)], axis=1)
    ffncP = np.stack([R[i]["ffncP"].transpose(0, 3, 2, 1).reshape(L, 2, NFF) for i in range(B)], axis=1)
    kPo = np.stack([R[i]["kP"].transpose(0, 3, 2, 1).reshape(L, 256, 4, 256) for i in range(B)], axis=1)
    vPo = np.stack([R[i]["vP"].reshape(L, 256, 4, 256) for i in range(B)], axis=1)
    confS = np.concatenate([R[i]["confS"].transpose(0, 4, 3, 2, 1).reshape(L, NS, 30, 1024) for i in range(B)], axis=1)
    ssdcS = np.concatenate([R[i]["ssdcS"].transpose(0, 4, 3, 2, 1).reshape(L, NS, 3, 1536) for i in range(B)], axis=1)
    ssdS = np.concatenate([R[i]["ssdS"].reshape(L, NS, 16, 64, 128) for i in range(B)], axis=1)
    ffncS = np.concatenate([R[i]["ffncS"].transpose(0, 4, 3, 2, 1).reshape(L, NS, 2, NFF) for i in range(B)], axis=1)
    outs = (y_prompt, y_sample, confP, ssdcP, ssdP, ffncP, kPo, vPo, confS, ssdcS, ssdS, ffncS)
    return tuple(np.ascontiguousarray(o, dtype=f) for o in outs)
```
